# Optimizing a Trainium2 kernel written in Bass

```python
import numpy as np
import jax
import jax.numpy as jnp
from jax import lax

D_MODEL = 2048
BATCH = 16
SEQ = 256
DEPTH = 4
DEC_BATCH = 4
DEC_SEQ = 2048
PAST_LEN = 256

GRID_W = 64
HEAD_DIM = 128
ROPE_THETA = 10000.0
EPS = 1e-6
N_MOD = 6
N_BRANCH = 4

POOL_GROUPS = 4
POOL_GROUP_DIM = 128
POOL_WIDTH = POOL_GROUPS * POOL_GROUP_DIM
POOL_WINDOWS = (2, 4, 8, 16)

ATTN_Q_HEADS = 8
ATTN_KV_HEADS = 2
ATTN_GROUP = ATTN_Q_HEADS // ATTN_KV_HEADS
ATTN_WINDOW = 128
ATTN_BLOCK = 128
ATTN_Q_W = ATTN_Q_HEADS * HEAD_DIM
ATTN_KV_W = ATTN_KV_HEADS * HEAD_DIM

DN_HEADS = 4
DN_DK = 128
DN_DV = 128
DN_W = DN_HEADS * DN_DV
DN_CONV = 4
DN_CHUNK = 64

GLA_HEADS = 4
GLA_DK = 64
GLA_DV = 128
GLA_QK_W = GLA_HEADS * GLA_DK
GLA_V_W = GLA_HEADS * GLA_DV
GLA_RANK = 16
GLA_NORMALIZER = 16.0
GLA_CHUNK = 64

FFN_HIDDEN = -(-(8 * D_MODEL) // (3 * 256)) * 256

IN_SIZES = (POOL_WIDTH, ATTN_Q_W, ATTN_KV_W, ATTN_KV_W, 3 * DN_W, DN_W, 2 * DN_HEADS, 2 * DN_HEADS,
            GLA_QK_W, GLA_QK_W, GLA_V_W, GLA_V_W, 2 * GLA_RANK, N_BRANCH * D_MODEL)
IN_WIDTH = sum(IN_SIZES)

kernel_name = 'hybrid_flow_backbone_ctx_prefix_step'


def _rms(x, w):
    xf = x.astype(jnp.float32)
    y = xf * lax.rsqrt(jnp.mean(xf * xf, axis=-1, keepdims=True) + EPS)
    return (y * w.astype(jnp.float32)).astype(x.dtype)


def _l2norm(x):
    return x * lax.rsqrt(jnp.sum(x * x, axis=-1, keepdims=True) + EPS)


def _modulation(cond, w_mod, b_mod):
    m = jax.nn.silu(cond) @ w_mod + b_mod
    return jnp.split(m[..., None, :], N_MOD, axis=-1)


def _grid_positions(n_tok):
    rows = n_tok // GRID_W
    row = jnp.repeat(jnp.arange(rows, dtype=jnp.float32), GRID_W)
    col = jnp.tile(jnp.arange(GRID_W, dtype=jnp.float32), rows)
    return row, col


def _axial_rope(x, row, col):
    quarter = HEAD_DIM // 4
    half = HEAD_DIM // 2
    inv = ROPE_THETA ** (-jnp.arange(quarter, dtype=jnp.float32) / quarter)
    shape = (1, x.shape[1]) + (1,) * (x.ndim - 3) + (quarter,)

    def rot(xp, p):
        ang = (p[:, None] * inv[None, :]).reshape(shape)
        cs, sn = jnp.cos(ang), jnp.sin(ang)
        x1, x2 = xp[..., :quarter], xp[..., quarter:]
        return jnp.concatenate([x1 * cs - x2 * sn, x1 * sn + x2 * cs], axis=-1)

    return jnp.concatenate([rot(x[..., :half], row), rot(x[..., half:], col)], axis=-1)


def _pool_mixer(u, w_group, scale):
    B, T, _ = u.shape
    uf = u.astype(jnp.float32)
    cs = jnp.concatenate([jnp.zeros((B, 1, POOL_WIDTH), jnp.float32), jnp.cumsum(uf, axis=1)], axis=1)
    pos = jnp.arange(T)
    outs = []
    for gi, win in enumerate(POOL_WINDOWS):
        lo = jnp.clip(pos - win // 2, 0, T)
        hi = jnp.clip(pos + win // 2, 0, T)
        sl = slice(gi * POOL_GROUP_DIM, (gi + 1) * POOL_GROUP_DIM)
        csg = cs[..., sl]
        ssum = jnp.take(csg, hi, axis=1) - jnp.take(csg, lo, axis=1)
        cnt = (hi - lo).astype(jnp.float32)[None, :, None]
        outs.append(ssum / cnt - uf[..., sl])
    p = jnp.stack(outs, axis=2)
    y = jnp.einsum('btgc,gcd->btgd', p, w_group.astype(jnp.float32)).reshape(B, T, POOL_WIDTH)
    return y * scale.astype(jnp.float32)


def _attend_block(q, k_c, v_c, sink, k_l=None, v_l=None, mask_l=None):
    B, Q = q.shape[:2]
    s_c = jnp.einsum('bqngd,bpnd->bngqp', q, k_c)
    snk = jnp.broadcast_to(sink.astype(jnp.float32).reshape(1, ATTN_KV_HEADS, ATTN_GROUP, 1, 1),
                           (B, ATTN_KV_HEADS, ATTN_GROUP, Q, 1))
    logits = [snk, s_c]
    if k_l is not None:
        s_l = jnp.einsum('bqngd,blnd->bngql', q, k_l)
        logits.append(jnp.where(mask_l, s_l, -1e30))
    p = jax.nn.softmax(jnp.concatenate(logits, axis=-1), axis=-1)
    P = k_c.shape[1]
    o = jnp.einsum('bngqp,bpnd->bqngd', p[..., 1:1 + P], v_c)
    if k_l is not None:
        o = o + jnp.einsum('bngql,blnd->bqngd', p[..., 1 + P:], v_l)
    return o


def _context_attention(q, k, v, sink):
    B, S = q.shape[:2]
    nb = S // ATTN_BLOCK
    qb = q.reshape(B, nb, ATTN_BLOCK, ATTN_KV_HEADS, ATTN_GROUP, HEAD_DIM).transpose(1, 0, 2, 3, 4, 5)
    out = lax.map(lambda qi: _attend_block(qi, k, v, sink), qb)
    return out.transpose(1, 0, 2, 3, 4, 5).reshape(B, S, ATTN_KV_HEADS, ATTN_GROUP, HEAD_DIM)


def _latent_attention(q, k, v, k_c, v_c, sink):
    B, T = q.shape[:2]
    nb = T // ATTN_BLOCK
    pad = ((0, 0), (ATTN_BLOCK, ATTN_BLOCK), (0, 0), (0, 0))
    kp = jnp.pad(k, pad)
    vp = jnp.pad(v, pad)
    qb = q.reshape(B, nb, ATTN_BLOCK, ATTN_KV_HEADS, ATTN_GROUP, HEAD_DIM).transpose(1, 0, 2, 3, 4, 5)

    def blk(args):
        i, qi = args
        kl = lax.dynamic_slice_in_dim(kp, i * ATTN_BLOCK, 3 * ATTN_BLOCK, axis=1)
        vl = lax.dynamic_slice_in_dim(vp, i * ATTN_BLOCK, 3 * ATTN_BLOCK, axis=1)
        qpos = i * ATTN_BLOCK + jnp.arange(ATTN_BLOCK)
        kpos = (i - 1) * ATTN_BLOCK + jnp.arange(3 * ATTN_BLOCK)
        mask = (jnp.abs(qpos[:, None] - kpos[None, :]) <= ATTN_WINDOW) & (kpos[None, :] >= 0) & (kpos[None, :] < T)
        return _attend_block(qi, k_c, v_c, sink, kl, vl, mask)

    out = lax.map(blk, (jnp.arange(nb), qb))
    return out.transpose(1, 0, 2, 3, 4, 5).reshape(B, T, ATTN_KV_HEADS, ATTN_GROUP, HEAD_DIM)


def _short_conv(x, w):
    C = x.shape[-1]
    return lax.conv_general_dilated(x, w[:, None, :], window_strides=(1,),
                                    padding=[(DN_CONV // 2, DN_CONV - 1 - DN_CONV // 2)],
                                    dimension_numbers=('NWC', 'WIO', 'NWC'), feature_group_count=C)


def _gated_delta_chunked(q, k, v, g, beta, s0):
    B, H, T, DK = q.shape
    DV = v.shape[-1]
    C = DN_CHUNK
    n = T // C
    q = q.reshape(B, H, n, C, DK)
    k = k.reshape(B, H, n, C, DK)
    v = v.reshape(B, H, n, C, DV)
    beta = beta.reshape(B, H, n, C)
    gc = jnp.cumsum(g.reshape(B, H, n, C), axis=-1)
    incl = jnp.tril(jnp.ones((C, C), dtype=bool))
    strict = jnp.tril(jnp.ones((C, C), dtype=bool), -1)
    decay = jnp.exp(jnp.where(incl, gc[..., :, None] - gc[..., None, :], -jnp.inf))
    kb = k * beta[..., None]
    L = jnp.where(strict, jnp.einsum('bhnid,bhnjd->bhnij', kb, k) * decay, 0.0)
    eye = jnp.eye(C, dtype=jnp.float32)
    tinv = lax.linalg.triangular_solve(eye + L, jnp.broadcast_to(eye, L.shape), left_side=True, lower=True,
                                       unit_diagonal=True)
    u = jnp.einsum('bhnij,bhnje->bhnie', tinv, v * beta[..., None])
    w = jnp.einsum('bhnij,bhnjd->bhnid', tinv, kb * jnp.exp(gc)[..., None])
    a_intra = jnp.einsum('bhnid,bhnjd->bhnij', q, k) * decay
    qe = q * jnp.exp(gc)[..., None]
    kd = k * jnp.exp(gc[..., -1:] - gc)[..., None]
    dl = jnp.exp(gc[..., -1])
    mv = lambda t: jnp.moveaxis(t, 2, 0)

    def step(S, xs):
        qe_i, kd_i, u_i, w_i, a_i, dl_i = xs
        v_new = u_i - jnp.einsum('bhcd,bhde->bhce', w_i, S)
        o = jnp.einsum('bhcd,bhde->bhce', qe_i, S) + jnp.einsum('bhcj,bhje->bhce', a_i, v_new)
        S = dl_i[..., None, None] * S + jnp.einsum('bhcd,bhce->bhde', kd_i, v_new)
        return S, o

    S, o = lax.scan(step, s0, (mv(qe), mv(kd), mv(u), mv(w), mv(a_intra), mv(dl)))
    return jnp.moveaxis(o, 0, 2).reshape(B, H, T, DV), S


def _deltanet(qkv, z, a_raw, b_raw, conv_w, a_log, dt_bias, norm_w, s0):
    B, T, _ = qkv.shape
    qkv = jax.nn.silu(_short_conv(qkv.astype(jnp.float32), conv_w.astype(jnp.float32)))
    heads = lambda t: t.reshape(B, T, DN_HEADS, -1).transpose(0, 2, 1, 3)
    q, k, v = [heads(t) for t in jnp.split(qkv, 3, axis=-1)]
    q = _l2norm(q) * DN_DK ** -0.5
    k = _l2norm(k)
    a = a_raw.astype(jnp.float32).reshape(B, T, 2, DN_HEADS)
    g = -jnp.exp(a_log.astype(jnp.float32)) * jax.nn.softplus(a + dt_bias.astype(jnp.float32))
    beta = jax.nn.sigmoid(b_raw.astype(jnp.float32).reshape(B, T, 2, DN_HEADS))
    g = g.transpose(2, 0, 3, 1)
    beta = beta.transpose(2, 0, 3, 1)
    s0 = s0.astype(jnp.float32)
    flip = lambda t: jnp.flip(t, axis=2)
    o_f, s_f = _gated_delta_chunked(q, k, v, g[0], beta[0], s0[:, 0])
    o_b, s_b = _gated_delta_chunked(flip(q), flip(k), flip(v), flip(g[1]), flip(beta[1]), s0[:, 1])
    o = (o_f + flip(o_b)).transpose(0, 2, 1, 3)
    o = _rms(o, norm_w) * jax.nn.silu(z.astype(jnp.float32).reshape(B, T, DN_HEADS, DN_DV))
    return o.reshape(B, T, DN_W), jnp.stack([s_f, s_b], axis=1)


def _gla_chunked(q, k, v, g, s0):
    B, H, T, DK = q.shape
    DV = v.shape[-1]
    C = GLA_CHUNK
    n = T // C
    q = q.reshape(B, H, n, C, DK)
    k = k.reshape(B, H, n, C, DK)
    v = v.reshape(B, H, n, C, DV)
    gc = jnp.cumsum(g.reshape(B, H, n, C, DK), axis=3)
    incl = jnp.tril(jnp.ones((C, C), dtype=bool))
    qg = q * jnp.exp(gc)
    kg = k * jnp.exp(-gc)
    a_intra = jnp.where(incl, jnp.einsum('bhnid,bhnjd->bhnij', qg, kg), 0.0)
    kd = k * jnp.exp(gc[:, :, :, -1:, :] - gc)
    dl = jnp.exp(gc[:, :, :, -1, :])
    mv = lambda t: jnp.moveaxis(t, 2, 0)

    def step(S, xs):
        qg_i, a_i, v_i, kd_i, dl_i = xs
        o = jnp.einsum('bhcd,bhde->bhce', qg_i, S) + jnp.einsum('bhcj,bhje->bhce', a_i, v_i)
        S = dl_i[..., None] * S + jnp.einsum('bhcd,bhce->bhde', kd_i, v_i)
        return S, o

    S, o = lax.scan(step, s0, (mv(qg), mv(a_intra), mv(v), mv(kd), mv(dl)))
    return jnp.moveaxis(o, 0, 2).reshape(B, H, T, DV), S


def _gla(q, k, v, r, lr, w2, b2, norm_w, s0):
    B, T, _ = q.shape
    heads = lambda t, d: t.astype(jnp.float32).reshape(B, T, GLA_HEADS, d).transpose(0, 2, 1, 3)
    q = heads(q, GLA_DK) * GLA_DK ** -0.5
    k = heads(k, GLA_DK)
    v = heads(v, GLA_DV)
    lr = lr.astype(jnp.float32).reshape(B, T, 2, GLA_RANK)
    gk = jax.nn.log_sigmoid(jnp.einsum('btor,ork->obtk', lr, w2.astype(jnp.float32))
                            + b2.astype(jnp.float32)[:, None, None, :]) / GLA_NORMALIZER
    gk = gk.reshape(2, B, T, GLA_HEADS, GLA_DK).transpose(0, 1, 3, 2, 4)
    s0 = s0.astype(jnp.float32)
    flip = lambda t: jnp.flip(t, axis=2)
    o_f, s_f = _gla_chunked(q, k, v, gk[0], s0[:, 0])
    o_b, s_b = _gla_chunked(flip(q), flip(k), flip(v), flip(gk[1]), s0[:, 1])
    o = (o_f + flip(o_b)).transpose(0, 2, 1, 3)
    o = _rms(o, norm_w) * jax.nn.silu(r.astype(jnp.float32).reshape(B, T, GLA_HEADS, GLA_DV))
    return o.reshape(B, T, GLA_V_W), jnp.stack([s_f, s_b], axis=1)


def _layer(x, cond, lw, ctx):
    B, T, _ = x.shape
    dt = x.dtype
    sh1, sc1, gt1, sh2, sc2, gt2 = _modulation(cond, lw['w_mod'], lw['b_mod'])
    h = _rms(x, lw['norm1']) * (1 + sc1) + sh1
    parts = jnp.split(h @ lw['w_in'], np.cumsum(IN_SIZES)[:-1].tolist(), axis=-1)
    (u_pool, q_at, k_at, v_at, qkv_dn, z_dn, a_dn, b_dn,
     q_gl, k_gl, v_gl, r_gl, lr_gl, gate_logits) = parts

    y_a = _pool_mixer(u_pool, lw['pool_w'], lw['pool_scale'])

    q = q_at.astype(jnp.float32).reshape(B, T, ATTN_KV_HEADS, ATTN_GROUP, HEAD_DIM) * HEAD_DIM ** -0.5
    k = k_at.reshape(B, T, ATTN_KV_HEADS, HEAD_DIM)
    v = v_at.reshape(B, T, ATTN_KV_HEADS, HEAD_DIM)
    if ctx is None:
        y_b = _context_attention(q, k.astype(jnp.float32), v.astype(jnp.float32), lw['sink'])
        s_dn0 = jnp.zeros((B, 2, DN_HEADS, DN_DK, DN_DV), jnp.float32)
        s_gl0 = jnp.zeros((B, 2, GLA_HEADS, GLA_DK, GLA_DV), jnp.float32)
    else:
        k_c, v_c, s_dn0, s_gl0 = ctx
        row, col = _grid_positions(T)
        y_b = _latent_attention(_axial_rope(q, row, col), _axial_rope(k.astype(jnp.float32), row, col),
                                v.astype(jnp.float32), k_c.astype(jnp.float32), v_c.astype(jnp.float32),
                                lw['sink'])
    y_b = y_b.reshape(B, T, ATTN_Q_W)

    y_c, s_dn = _deltanet(qkv_dn, z_dn, a_dn, b_dn, lw['dn_conv'], lw['dn_a_log'], lw['dn_dt_bias'],
                          lw['dn_norm'], s_dn0)
    y_d, s_gl = _gla(q_gl, k_gl, v_gl, r_gl, lr_gl, lw['gla_w2'], lw['gla_b2'], lw['gla_norm'], s_gl0)

    g = jax.nn.sigmoid(gate_logits.astype(jnp.float32)).astype(dt).reshape(B, T, N_BRANCH, D_MODEL)
    merged = (g[:, :, 0] * (y_a.astype(dt) @ lw['w_br_pool'])
              + g[:, :, 1] * (y_b.astype(dt) @ lw['w_br_attn'])
              + g[:, :, 2] * (y_c.astype(dt) @ lw['w_br_delta'])
              + g[:, :, 3] * (y_d.astype(dt) @ lw['w_br_gla']))
    x = x + gt1 * (merged @ lw['w_out'])

    h2 = _rms(x, lw['norm2']) * (1 + sc2) + sh2
    x = x + gt2 * ((jax.nn.silu(h2 @ lw['w_gate']) * (h2 @ lw['w_up'])) @ lw['w_down'])
    return x, (k, v, s_dn, s_gl)


def setup_inputs(seed: int = 0) -> dict:
    key = jax.random.key(seed)
    ks = iter(jax.random.split(key, 48))
    nrm = lambda shape, scale: jax.random.normal(next(ks), shape, jnp.float32) * scale
    uni = lambda shape, lo, hi: jax.random.uniform(next(ks), shape, jnp.float32, lo, hi)
    L, D = DEPTH, D_MODEL
    dt_init = jnp.exp(uni((L, 2, DN_HEADS), float(np.log(1e-3)), float(np.log(1e-1))))
    return {
        'x_prompt': nrm((BATCH, SEQ, D), 1.0),
        'x_sample': nrm((DEC_BATCH, DEC_SEQ, D), 1.0),
        'cache_k': nrm((DEC_BATCH, L, PAST_LEN, ATTN_KV_HEADS, HEAD_DIM), 1.0),
        'cache_v': nrm((DEC_BATCH, L, PAST_LEN, ATTN_KV_HEADS, HEAD_DIM), 1.0),
        'state_delta': nrm((DEC_BATCH, L, 2, DN_HEADS, DN_DK, DN_DV), 0.5),
        'state_gla': nrm((DEC_BATCH, L, 2, GLA_HEADS, GLA_DK, GLA_DV), 0.5),
        'c': nrm((DEC_BATCH, D), 1.0),
        'c_ctx': nrm((D,), 1.0),
        'w_mod': nrm((L, D, N_MOD * D), 0.5 * D ** -0.5),
        'b_mod': nrm((L, N_MOD * D), 0.01),
        'norm1': 1.0 + nrm((L, D), 0.02),
        'w_in': nrm((L, D, IN_WIDTH), D ** -0.5),
        'pool_w': nrm((L, POOL_GROUPS, POOL_GROUP_DIM, POOL_GROUP_DIM), POOL_GROUP_DIM ** -0.5),
        'pool_scale': 1.0 + nrm((L, POOL_WIDTH), 0.02),
        'attn_sink': nrm((L, ATTN_Q_HEADS), 0.5),
        'dn_conv': nrm((L, DN_CONV, 3 * DN_W), DN_CONV ** -0.5),
        'dn_a_log': jnp.log(uni((L, 2, DN_HEADS), 1.0, 16.0)),
        'dn_dt_bias': dt_init + jnp.log(-jnp.expm1(-dt_init)),
        'dn_norm': 1.0 + nrm((L, DN_DV), 0.02),
        'gla_w2': nrm((L, 2, GLA_RANK, GLA_QK_W), GLA_RANK ** -0.5),
        'gla_b2': nrm((L, 2, GLA_QK_W), 0.1),
        'gla_norm': 1.0 + nrm((L, GLA_DV), 0.02),
        'w_br_pool': nrm((L, POOL_WIDTH, D), POOL_WIDTH ** -0.5),
        'w_br_attn': nrm((L, ATTN_Q_W, D), ATTN_Q_W ** -0.5),
        'w_br_delta': nrm((L, DN_W, D), DN_W ** -0.5),
        'w_br_gla': nrm((L, GLA_V_W, D), GLA_V_W ** -0.5),
        'w_out': nrm((L, D, D), D ** -0.5),
        'norm2': 1.0 + nrm((L, D), 0.02),
        'w_gate': nrm((L, D, FFN_HIDDEN), D ** -0.5),
        'w_up': nrm((L, D, FFN_HIDDEN), D ** -0.5),
        'w_down': nrm((L, FFN_HIDDEN, D), FFN_HIDDEN ** -0.5),
        'norm_f': 1.0 + nrm((D,), 0.02),
    }


def reference(x_prompt, x_sample, cache_k, cache_v, state_delta, state_gla, c, c_ctx,
              w_mod, b_mod, norm1, w_in, pool_w, pool_scale, attn_sink, dn_conv, dn_a_log, dn_dt_bias,
              dn_norm, gla_w2, gla_b2, gla_norm, w_br_pool, w_br_attn, w_br_delta, w_br_gla, w_out,
              norm2, w_gate, w_up, w_down, norm_f):
    xp, xs = x_prompt, x_sample
    ks, vs, sds, sgs = [], [], [], []
    for l in range(DEPTH):
        lw = {'w_mod': w_mod[l], 'b_mod': b_mod[l], 'norm1': norm1[l], 'w_in': w_in[l],
              'pool_w': pool_w[l], 'pool_scale': pool_scale[l], 'sink': attn_sink[l],
              'dn_conv': dn_conv[l], 'dn_a_log': dn_a_log[l], 'dn_dt_bias': dn_dt_bias[l], 'dn_norm': dn_norm[l],
              'gla_w2': gla_w2[l], 'gla_b2': gla_b2[l], 'gla_norm': gla_norm[l],
              'w_br_pool': w_br_pool[l], 'w_br_attn': w_br_attn[l], 'w_br_delta': w_br_delta[l],
              'w_br_gla': w_br_gla[l], 'w_out': w_out[l], 'norm2': norm2[l],
              'w_gate': w_gate[l], 'w_up': w_up[l], 'w_down': w_down[l]}
        xp, (k_l, v_l, sd_l, sg_l) = _layer(xp, c_ctx, lw, None)
        ks.append(k_l)
        vs.append(v_l)
        sds.append(sd_l)
        sgs.append(sg_l)
        xs, _ = _layer(xs, c, lw, (cache_k[:, l], cache_v[:, l], state_delta[:, l], state_gla[:, l]))
    y_prompt = _rms(xp, norm_f)
    y_sample = _rms(xs, norm_f)
    new_cache_k = jnp.stack(ks, axis=1)
    new_cache_v = jnp.stack(vs, axis=1)
    new_state_delta = jnp.stack(sds, axis=1)
    new_state_gla = jnp.stack(sgs, axis=1)
    return (y_prompt, y_sample, new_cache_k, new_cache_v, new_state_delta, new_state_gla)
```

```python
import numpy as np
from contextlib import ExitStack
import concourse.bass as bass
import concourse.mybir as mybir
from concourse.bass_utils import run_bass_kernel_spmd

F32 = mybir.dt.float32
BF16 = mybir.dt.bfloat16
AF = mybir.ActivationFunctionType
ALU = mybir.AluOpType
AX = mybir.AxisListType

COMPUTE = ("pe", "act", "dve", "pool")
SEM_EPOCH = 30000


class Buf:
    __slots__ = ("t", "lw", "rd", "name", "excl")

    def __init__(self, t, name="", excl=False):
        self.excl = excl
        self.t = t
        self.lw = None
        self.rd = {}
        self.name = name

    def __getitem__(self, idx):
        return self.t[idx]


class Sched:
    def __init__(self, nc, stack, n_dma_sems=16):
        self.nc = nc
        self.stack = stack
        self.q = {e: [] for e in COMPUTE + ("sp",)}
        self.cnt = {e: 0 for e in COMPUTE}
        self.sems = {}
        self.own = {e: set() for e in COMPUTE}
        self.semobj = {}
        self.nsem = 0
        for e in COMPUTE:
            self._new_epoch(e)
        self.dsem = {q: [self._mksem("d%s%d" % (q, i)) for i in range(n_dma_sems)] for q in ("sp", "pool")}
        self.dval = {q: [0] * n_dma_sems for q in ("sp", "pool")}
        self.drr = {"sp": 0, "pool": 0}
        self.waited = {e: {} for e in COMPUTE + ("sp",)}
        self.n_ops = 0
        self.last_dma_toks = {}
        self.last_tok = {}

    def _mksem(self, name):
        s = self.stack.enter_context(self.nc.semaphore(name))
        k = self.nsem
        self.nsem += 1
        self.semobj[k] = s
        return k

    def _new_epoch(self, e):
        self.sems[e] = self._mksem("s_%s_%d" % (e, self.nsem))
        self.own[e].add(self.sems[e])
        self.cnt[e] = 0

    def _collect(self, eng, R, W):
        deps = {}

        def add(tok):
            if tok is None:
                return
            k, v = tok
            if deps.get(k, 0) < v:
                deps[k] = v
        for b in R:
            add(b.lw)
            if b.excl:
                mine = self.own.get(eng, ())
                for t in b.rd.items():
                    if t[0] not in mine:
                        add(t)
        for b in W:
            add(b.lw)
            for t in b.rd.items():
                add(t)
        waits = []
        wd = self.waited[eng]
        own = self.own["pe"] if eng == "pe" else ()
        for k, v in deps.items():
            if k in own:
                continue
            if wd.get(k, 0) < v:
                wd[k] = v
                waits.append((k, v))
        return waits

    def _commit(self, tok, R, W):
        for b in R:
            if b.rd.get(tok[0], 0) < tok[1]:
                b.rd[tok[0]] = tok[1]
        for b in W:
            b.lw = tok
            b.rd = {}

    def _eng(self, e):
        nc = self.nc
        return {"pe": nc.tensor, "act": nc.scalar, "dve": nc.vector, "pool": nc.gpsimd, "sp": nc.sync}[e]

    def _issue(self, eng, waits, fn, inc):
        e = self._eng(eng)
        for k, v in waits:
            e.wait_ge(self.semobj[k], v)
        if fn is not None:
            ins = fn(e)
            ins.then_inc(self.semobj[inc[0]], inc[1])

    def op(self, eng, fn, R=(), W=()):
        waits = self._collect(eng, R, W)
        if self.cnt[eng] >= SEM_EPOCH:
            self._new_epoch(eng)
        self.cnt[eng] += 1
        tok = (self.sems[eng], self.cnt[eng])
        self.last_tok[eng] = tok
        self._issue(eng, waits, fn, (tok[0], 1))
        self._commit(tok, R, W)
        self.n_ops += 1
        return tok

    def dma(self, out_ap, in_ap, R=(), W=(), q="sp"):
        s = self.drr[q]
        self.drr[q] = (s + 1) % len(self.dsem[q])
        k = self.dsem[q][s]
        dv = self.dval[q]
        waits = self._collect(q, R, W)
        wd = self.waited[q]
        if dv[s] > 0 and wd.get(k, 0) < dv[s]:
            wd[k] = dv[s]
            waits.append((k, dv[s]))
        dv[s] += 16
        tok = (k, dv[s])
        self._issue(q, waits, lambda e: e.dma_start(out=out_ap, in_=in_ap), (k, 16))
        self._commit(tok, R, W)
        self.n_ops += 1
        self.last_dma_toks[k] = tok
        return tok

    def wait_tokens(self, toks, eng="sp"):
        waits = []
        wd = self.waited[eng]
        for k, v in toks:
            if wd.get(k, 0) < v:
                wd[k] = v
                waits.append((k, v))
        self._issue(eng, waits, None, None)

    def barrier(self):
        toks = [self.last_tok[e] for e in COMPUTE if e in self.last_tok]
        for q in ("sp", "pool"):
            toks += [(self.dsem[q][i], self.dval[q][i]) for i in range(len(self.dsem[q])) if self.dval[q][i] > 0]
        for e in COMPUTE + ("sp",):
            self.wait_tokens(toks, e)

    def final_wait(self, toks=None, eng="sp"):
        self.barrier()


D = 2048
NKC = 16
FH = 5632
NHC = 44
INW = 13872
EPS = 1e-6
OFF = dict(pool=0, q_at=512, k_at=1536, v_at=1792, qkv=2048, z=3584, a=4096, b=4104,
           q_gl=4112, k_gl=4368, v_gl=4624, r_gl=5136, lr=5648, gate=5680)
NYC = 20
SLOT = 8192


class Ctx:
    pass


def sb(K, st, shape, dt, name=None):
    K.uid += 1
    return Buf(st.enter_context(K.nc.sbuf_tensor("%s_%d" % (name or "t", K.uid), list(shape), dt)), name or "t")


def mm(K, pb, out_ap, lb, l_ap, rb, r_ap, start=True, stop=True):
    K.S.op("pe", lambda e: e.matmul(out_ap, l_ap, r_ap, start=start, stop=stop), R=[lb, rb], W=[pb])


def tr(K, pb, out_ap, ib, in_ap, idb, id_ap):
    K.S.op("pe", lambda e: e.transpose(out_ap, in_ap, id_ap), R=[ib, idb], W=[pb])


def act(K, out_ap, in_ap, func, R, W, bias=None, scale=None):
    kw = {}
    if bias is not None:
        kw["bias"] = bias
    if scale is not None:
        kw["scale"] = scale
    K.S.op("act", lambda e: e.activation(out=out_ap, in_=in_ap, func=func, **kw), R=R, W=W)


def tt(K, eng, out_ap, a_ap, b_ap, op, R, W):
    K.S.op(eng, lambda e: e.tensor_tensor(out=out_ap, in0=a_ap, in1=b_ap, op=op), R=R, W=W)


def ts(K, eng, out_ap, a_ap, s1, s2, op0, op1, R, W):
    if s2 is None:
        K.S.op(eng, lambda e: e.tensor_scalar(out=out_ap, in0=a_ap, scalar1=s1, scalar2=None, op0=op0), R=R, W=W)
    else:
        K.S.op(eng, lambda e: e.tensor_scalar(out=out_ap, in0=a_ap, scalar1=s1, scalar2=s2, op0=op0, op1=op1), R=R, W=W)


def stt(K, eng, out_ap, a_ap, s, b_ap, op0, op1, R, W):
    K.S.op(eng, lambda e: e.scalar_tensor_tensor(out=out_ap, in0=a_ap, scalar=s, in1=b_ap, op0=op0, op1=op1), R=R, W=W)


def rsqrt(K, ob, out_ap, ib, in_ap, scale):
    ts(K, "dve", out_ap, in_ap, scale, EPS, ALU.mult, ALU.add, R=[ib], W=[ob])
    act(K, out_ap, out_ap, AF.Sqrt, R=[ob], W=[ob])
    K.S.op("dve", lambda e: e.reciprocal(out=out_ap, in_=out_ap), R=[ob], W=[ob])


def cp(K, eng, out_ap, in_ap, R, W):
    if eng == "act":
        K.S.op("act", lambda e: e.activation(out=out_ap, in_=in_ap, func=AF.Copy), R=R, W=W)
    else:
        K.S.op(eng, lambda e: e.tensor_copy(out=out_ap, in_=in_ap), R=R, W=W)


def wslot(K, i, k, c):
    return K.ring[i][:, 0:k * c].rearrange("p (k c) -> p k c", k=k)


class Ring:
    def __init__(self, K):
        self.K = K
        self.i = 0

    def load(self, wbuf, w_ap, k, c):
        K = self.K
        i = self.i
        self.i = (self.i + 1) % len(K.ring)
        K.S.dma(wslot(K, i, k, c), w_ap, R=[wbuf], W=[K.ring[i]], q="pool")
        return K.ring[i], wslot(K, i, k, c)


def tiles_of(K):
    out = []
    t = 0
    while t < K.T_S:
        n = min(512, K.T_S - t)
        out.append((t, n, 0))
        t += n
    tot = K.NT
    while t < tot:
        n = min(512, tot - t)
        out.append((t, n, 1))
        t += n
    return out


def supertiles(K, maxtok=1024):
    out, cur, tot = [], [], 0
    for tl in tiles_of(K):
        if tot + tl[1] > maxtok and cur:
            out.append(cur)
            cur, tot = [], 0
        cur.append(tl)
        tot += tl[1]
    if cur:
        out.append(cur)
    return out


def phase_in(K):
    S = K.S
    with ExitStack() as st:
        xin = [sb(K, st, [128, D], F32, "xin") for _ in range(2)]
        xT = [sb(K, st, [128, NKC, 128], F32, "xT") for _ in range(2)]
        for blk in range(K.NT // 128):
            t0 = blk * 128
            if t0 < K.T_S:
                src, r0 = K.x_s, t0
            else:
                src, r0 = K.x_p, t0 - K.T_S
            xi = xin[blk % 2]
            xo = xT[blk % 2]
            S.dma(xi[:], src[r0:r0 + 128, :], R=[src], W=[xi])
            for g in range(4):
                pb = K.PS[g % 4]
                for c in range(4):
                    cc = g * 4 + c
                    tr(K, pb, pb[:, c * 128:(c + 1) * 128], xi, xi[:, cc * 128:(cc + 1) * 128], K.identF, K.identF[:])
                cp(K, "act" if g % 2 else "dve", xo[:, 4 * g:4 * g + 4, :],
                   pb[:, :].rearrange("p (c t) -> p c t", c=4), R=[pb], W=[xo])
            S.dma(K.XS[:, :, t0:t0 + 128].rearrange("c p t -> p c t"), xo[:], R=[xo], W=[K.XS])
        S.barrier()


def phase_mod(K, l):
    S = K.S
    with ExitStack() as st:
        ring = Ring(K)
        pb = K.PS[6]
        for cb in range(24):
            wb, w = ring.load(K.w_mod, K.w_mod[l, :, cb * 512:(cb + 1) * 512].rearrange("(k p) c -> p k c", p=128), NKC, 512)
            for m in range(4):
                j = cb * 4 + m
                for kc in range(NKC):
                    mm(K, pb, pb[:, j:j + 97:96], wb, w[:, kc, m * 128:(m + 1) * 128], K.sc, K.sc[:, kc, :],
                       start=(kc == 0), stop=(kc == NKC - 1))
        for s in range(2):
            tt(K, "dve", K.mod[:, s, :], pb[:, s * 96:(s + 1) * 96], K.bmodT[:, l, :], ALU.add,
               R=[pb, K.bmodT], W=[K.mod])
        for s in range(2):
            stt(K, "dve", K.A1[:, s, :], K.mod[:, s, 16:32], 1.0, K.norm1T[:, l, :], ALU.add, ALU.mult,
                R=[K.mod, K.norm1T], W=[K.A1])
            stt(K, "dve", K.A2[:, s, :], K.mod[:, s, 64:80], 1.0, K.norm2T[:, l, :], ALU.add, ALU.mult,
                R=[K.mod, K.norm2T], W=[K.A2])
        S.barrier()


def phase_norm(K, l, which):
    S = K.S
    with ExitStack() as st:
        xb = [sb(K, st, [128, NKC, 512], F32, "nx") for _ in range(2)]
        sq = sb(K, st, [128, NKC, 512], BF16, "nsq")
        rr = sb(K, st, [128, 512], F32, "nr")
        tmp = [sb(K, st, [128, 512], F32, "ntmp") for _ in range(2)]
        if which == "f":
            hf = sb(K, st, [128, NKC, 512], F32, "nhf")
            yo = [sb(K, st, [128, D], F32, "nyo") for _ in range(2)]
        else:
            hb = [sb(K, st, [128, NKC, 512], BF16, "nh") for _ in range(2)]
        for ti, (t0, n, s) in enumerate(tiles_of(K)):
            x = xb[ti % 2]
            S.dma(x[:, :, :n], K.XS[:, :, t0:t0 + n].rearrange("c p t -> p c t"), R=[K.XS], W=[x])
            for g in range(4):
                act(K, sq[:, 4 * g:4 * g + 4, :n], x[:, 4 * g:4 * g + 4, :n], AF.Square, R=[x], W=[sq])
            pb = K.PS[4 + ti % 2]
            for c in range(NKC):
                mm(K, pb, pb[:, :n], K.onesB, K.onesB[:], sq, sq[:, c, :n], start=(c == 0), stop=(c == NKC - 1))
            rsqrt(K, rr, rr[:, :n], pb, pb[:, :n], 1.0 / D)
            if which == "f":
                for c in range(NKC):
                    stt(K, "dve", hf[:, c, :n], x[:, c, :n], K.normfT[:, c:c + 1], rr[:, :n],
                        ALU.mult, ALU.mult, R=[x, rr, K.normfT], W=[hf])
                for b in range(n // 128):
                    y = yo[b % 2]
                    for g in range(4):
                        pb2 = K.PS[g % 4]
                        for c in range(4):
                            cc = 4 * g + c
                            tr(K, pb2, pb2[:, c * 128:(c + 1) * 128], hf, hf[:, cc, b * 128:(b + 1) * 128],
                               K.identF, K.identF[:])
                        cp(K, "act" if g % 2 else "dve", y[:, g * 512:(g + 1) * 512], pb2[:, :], R=[pb2], W=[y])
                    tg = t0 + b * 128
                    if tg < K.T_S:
                        S.dma(K.y_s[tg:tg + 128, :], y[:], R=[y], W=[K.y_s])
                    else:
                        S.dma(K.y_p[tg - K.T_S:tg - K.T_S + 128, :], y[:], R=[y], W=[K.y_p])
            else:
                A = K.A1 if which == 1 else K.A2
                bo = 0 if which == 1 else 48
                h = hb[ti % 2]
                for c in range(NKC):
                    tm = tmp[c % 2]
                    tt(K, "dve" if c % 2 else "pool", tm[:, :n], x[:, c, :n], rr[:, :n], ALU.mult, R=[x, rr], W=[tm])
                    act(K, h[:, c, :n], tm[:, :n], AF.Identity, R=[tm, A, K.mod], W=[h],
                        scale=A[:, s, c:c + 1], bias=K.mod[:, s, bo + c:bo + c + 1])
                S.dma(K.HD[:, :, t0:t0 + n].rearrange("c p t -> p c t"), h[:, :, :n], R=[h], W=[K.HD])
        S.barrier()


def phase_merge(K, l):
    S = K.S
    ring = Ring(K)
    brs = [(K.w_br_pool, 0, 4), (K.w_br_attn, 4, 8), (K.w_br_delta, 12, 4), (K.w_br_gla, 16, 4)]
    for stl in supertiles(K):
        T0 = stl[0][0]
        TN = sum(t[1] for t in stl)
        with ExitStack() as st:
            h = sb(K, st, [128, NKC, TN], BF16, "mh")
            y = sb(K, st, [128, NYC, TN], BF16, "my")
            mg = sb(K, st, [128, NKC, TN], BF16, "mmg")
            acc = sb(K, st, [128, 4, TN], F32, "macc")
            sg = [sb(K, st, [128, 512], F32, "msg") for _ in range(2)]
            t2 = [sb(K, st, [128, 512], F32, "mt2") for _ in range(2)]
            xt = [sb(K, st, [128, 512], F32, "mxt") for _ in range(2)]
            S.dma(h[:], K.HD[:, :, T0:T0 + TN].rearrange("c p t -> p c t"), R=[K.HD], W=[h])
            S.dma(y[:], K.YD[:, :, T0:T0 + TN].rearrange("c p t -> p c t"), R=[K.YD], W=[y])
            cnt = 0
            for mgp in range(4):
                for bi, (wbr, yc0, nk) in enumerate(brs):
                    c0 = OFF["gate"] + bi * D + mgp * 512
                    gb, gw = ring.load(K.w_in, K.w_in[l, :, c0:c0 + 512].rearrange("(k p) c -> p k c", p=128), NKC, 512)
                    bb, bw = ring.load(wbr, wbr[l, :, mgp * 512:(mgp + 1) * 512].rearrange("(k p) c -> p k c", p=128), nk, 512)
                    for m in range(4):
                        for (t0, n, s) in stl:
                            o = t0 - T0
                            pg = K.PS[cnt % 2]
                            pbr = K.PS[2 + cnt % 2]
                            for kc in range(NKC):
                                mm(K, pg, pg[:, :n], gb, gw[:, kc, m * 128:(m + 1) * 128], h, h[:, kc, o:o + n],
                                   start=(kc == 0), stop=(kc == NKC - 1))
                            for kc in range(nk):
                                mm(K, pbr, pbr[:, :n], bb, bw[:, kc, m * 128:(m + 1) * 128], y, y[:, yc0 + kc, o:o + n],
                                   start=(kc == 0), stop=(kc == nk - 1))
                            sgt = sg[cnt % 2]
                            act(K, sgt[:, :n], pg[:, :n], AF.Sigmoid, R=[pg], W=[sgt])
                            if bi == 0:
                                tt(K, "dve", acc[:, m, o:o + n], pbr[:, :n], sgt[:, :n], ALU.mult, R=[pbr, sgt], W=[acc])
                            else:
                                tq = t2[cnt % 2]
                                tt(K, "dve", tq[:, :n], pbr[:, :n], sgt[:, :n], ALU.mult, R=[pbr, sgt], W=[tq])
                                if bi < 3:
                                    tt(K, "pool", acc[:, m, o:o + n], acc[:, m, o:o + n], tq[:, :n], ALU.add,
                                       R=[acc, tq], W=[acc])
                                else:
                                    tt(K, "pool", mg[:, mgp * 4 + m, o:o + n], acc[:, m, o:o + n], tq[:, :n], ALU.add,
                                       R=[acc, tq], W=[mg])
                            cnt += 1
            for mgp in range(4):
                ob, ow = ring.load(K.w_out, K.w_out[l, :, mgp * 512:(mgp + 1) * 512].rearrange("(k p) c -> p k c", p=128), NKC, 512)
                for m in range(4):
                    mo = mgp * 4 + m
                    for (t0, n, s) in stl:
                        o = t0 - T0
                        po = K.PS[4 + cnt % 2]
                        x = xt[cnt % 2]
                        S.dma(x[:, :n], K.XS[mo, :, t0:t0 + n], R=[K.XS], W=[x])
                        for kc in range(NKC):
                            mm(K, po, po[:, :n], ob, ow[:, kc, m * 128:(m + 1) * 128], mg, mg[:, kc, o:o + n],
                               start=(kc == 0), stop=(kc == NKC - 1))
                        stt(K, "dve", x[:, :n], po[:, :n], K.mod[:, s, 32 + mo:33 + mo], x[:, :n], ALU.mult, ALU.add,
                            R=[po, x, K.mod], W=[x])
                        S.dma(K.XS[mo, :, t0:t0 + n], x[:, :n], R=[x], W=[K.XS])
                        cnt += 1
            S.barrier()


def phase_ffn(K, l):
    S = K.S
    ring = Ring(K)
    for stl in supertiles(K):
        T0 = stl[0][0]
        TN = sum(t[1] for t in stl)
        with ExitStack() as st:
            h = sb(K, st, [128, NKC, TN], BF16, "fh")
            a = sb(K, st, [128, NHC, TN], BF16, "fa")
            sg = [sb(K, st, [128, 512], F32, "fsg") for _ in range(2)]
            xt = [sb(K, st, [128, 512], F32, "fxt") for _ in range(2)]
            S.dma(h[:], K.HD[:, :, T0:T0 + TN].rearrange("c p t -> p c t"), R=[K.HD], W=[h])
            cnt = 0
            for hg in range(NHC // 4):
                gb, gw = ring.load(K.w_gate, K.w_gate[l, :, hg * 512:(hg + 1) * 512].rearrange("(k p) c -> p k c", p=128), NKC, 512)
                ub, uw = ring.load(K.w_up, K.w_up[l, :, hg * 512:(hg + 1) * 512].rearrange("(k p) c -> p k c", p=128), NKC, 512)
                for m in range(4):
                    hc = hg * 4 + m
                    for (t0, n, s) in stl:
                        o = t0 - T0
                        pg = K.PS[cnt % 2]
                        pu = K.PS[2 + cnt % 2]
                        for kc in range(NKC):
                            mm(K, pg, pg[:, :n], gb, gw[:, kc, m * 128:(m + 1) * 128], h, h[:, kc, o:o + n],
                               start=(kc == 0), stop=(kc == NKC - 1))
                        for kc in range(NKC):
                            mm(K, pu, pu[:, :n], ub, uw[:, kc, m * 128:(m + 1) * 128], h, h[:, kc, o:o + n],
                               start=(kc == 0), stop=(kc == NKC - 1))
                        sgt = sg[cnt % 2]
                        act(K, sgt[:, :n], pg[:, :n], AF.Silu, R=[pg], W=[sgt])
                        tt(K, "dve", a[:, hc, o:o + n], pu[:, :n], sgt[:, :n], ALU.mult, R=[pu, sgt], W=[a])
                        cnt += 1
            for mg2 in range(8):
                c0 = mg2 * 256
                d0b, d0w = ring.load(K.w_down, K.w_down[l, 0:22 * 128, c0:c0 + 256].rearrange("(k p) c -> p k c", p=128), 22, 256)
                d1b, d1w = ring.load(K.w_down, K.w_down[l, 22 * 128:44 * 128, c0:c0 + 256].rearrange("(k p) c -> p k c", p=128), 22, 256)
                for m in range(2):
                    mo = mg2 * 2 + m
                    for (t0, n, s) in stl:
                        o = t0 - T0
                        po = K.PS[4 + cnt % 2]
                        x = xt[cnt % 2]
                        S.dma(x[:, :n], K.XS[mo, :, t0:t0 + n], R=[K.XS], W=[x])
                        for kc in range(NHC):
                            db, dw = (d0b, d0w) if kc < 22 else (d1b, d1w)
                            mm(K, po, po[:, :n], db, dw[:, kc % 22, m * 128:(m + 1) * 128], a, a[:, kc, o:o + n],
                               start=(kc == 0), stop=(kc == NHC - 1))
                        stt(K, "dve", x[:, :n], po[:, :n], K.mod[:, s, 80 + mo:81 + mo], x[:, :n], ALU.mult, ALU.add,
                            R=[po, x, K.mod], W=[x])
                        S.dma(K.XS[mo, :, t0:t0 + n], x[:, :n], R=[x], W=[K.XS])
                        cnt += 1
            S.barrier()


def phase_zero_y(K):
    S = K.S
    with ExitStack() as st:
        z = sb(K, st, [128, NYC, 512], BF16, "zy")
        S.op("dve", lambda e: e.memset(z[:], 0.0), W=[z])
        for (t0, n, s) in tiles_of(K):
            S.dma(K.YD[:, :, t0:t0 + n].rearrange("c p t -> p c t"), z[:, :, :n], R=[z], W=K.YDB)
        S.barrier()


def build_program(cfg):
    nc = bass.Bass("TRN2", target_bir_lowering=False)
    K = Ctx()
    K.nc = nc
    K.uid = 0
    K.cfg = cfg
    K.T_S, K.NP, K.T_P, K.L = cfg["T_S"], cfg["NP"], cfg["T_P"], cfg["L"]
    K.NT = K.T_S + K.NP * K.T_P
    L = K.L

    def ein(name, shape, dt=F32):
        b = Buf(nc.dram_tensor(name, list(shape), dt, kind="ExternalInput").ap(), name)
        setattr(K, name, b)
        return b

    def eout(name, shape):
        b = Buf(nc.dram_tensor(name, list(shape), F32, kind="ExternalOutput").ap(), name)
        setattr(K, name, b)
        return b

    def scratch(name, shape, dt):
        b = Buf(nc.dram_tensor(name, list(shape), dt).ap(), name)
        setattr(K, name, b)
        return b

    ein("x_s", [K.T_S, D])
    ein("x_p", [K.NP * K.T_P, D])
    ein("condT", [128, NKC, 2])
    ein("bmodT_d", [128, L, 96])
    ein("norm1T_d", [128, L, NKC])
    ein("norm2T_d", [128, L, NKC])
    ein("normfT_d", [128, NKC])
    ein("identF_d", [128, 128])
    ein("w_mod", [L, D, 6 * D])
    ein("w_in", [L, D, INW])
    ein("w_br_pool", [L, 512, D])
    ein("w_br_attn", [L, 1024, D])
    ein("w_br_delta", [L, 512, D])
    ein("w_br_gla", [L, 512, D])
    ein("w_out", [L, D, D])
    ein("w_gate", [L, D, FH])
    ein("w_up", [L, D, FH])
    ein("w_down", [L, FH, D])
    ein("pool_w", [L, 4, 128, 128])
    ein("pscT_d", [128, L, 4])
    ein("rcnt_S", [4, K.T_S])
    ein("rcnt_P", [4, K.T_P])
    ein("cos_d", [128, K.T_S])
    ein("sin_d", [128, K.T_S])
    ein("RT_d", [128, 128])
    ein("maskL_d", [128, 4, 128])
    ein("maskR_d", [128, 4, 128])
    ein("sink_rep", [L, 1, 1024])
    ein("cache_k", [L, 256, 256])
    ein("cache_v", [L, 256, 256])
    ein("state_delta", [L, 2, 4, 128, 128])
    ein("state_gla", [L, 2, 4, 64, 128])
    mix_inputs(K, ein)
    eout("nk", [K.NP, L, K.T_P, 256])
    eout("nv", [K.NP, L, K.T_P, 256])
    eout("nsd", [K.NP, L, 2, 4, 128, 128])
    eout("nsg", [K.NP, L, 2, 4, 64, 128])
    eout("y_s", [K.T_S, D])
    eout("y_p", [K.NP * K.T_P, D])
    scratch("XS", [NKC, 128, K.NT], F32)
    scratch("HD", [NKC, 128, K.NT], BF16)
    scratch("YD", [NYC, 128, K.NT], BF16)
    K.YDB = [Buf(K.YD.t, "yd%d" % i) for i in range(NYC)]

    with ExitStack() as top:
        S = K.S = Sched(nc, top)
        K.PS = [Buf(top.enter_context(nc.psum_tensor("ps%d" % i, [128, 512], F32)), "ps%d" % i, excl=True) for i in range(7)]
        K.PB = Buf(top.enter_context(nc.psum_tensor("psb", [128, 1024], BF16)), "psb", excl=True)
        K.identF = sb(K, top, [128, 128], F32, "identF")
        K.onesB = sb(K, top, [128, 128], BF16, "onesB")
        K.sc = sb(K, top, [128, NKC, 2], BF16, "sc")
        K.mod = sb(K, top, [128, 2, 96], F32, "mod")
        K.A1 = sb(K, top, [128, 2, NKC], F32, "A1")
        K.A2 = sb(K, top, [128, 2, NKC], F32, "A2")
        K.bmodT = sb(K, top, [128, L, 96], F32, "bmodT")
        K.norm1T = sb(K, top, [128, L, NKC], F32, "norm1T")
        K.norm2T = sb(K, top, [128, L, NKC], F32, "norm2T")
        K.normfT = sb(K, top, [128, NKC], F32, "normfT")
        K.ring = [sb(K, top, [128, SLOT], BF16, "ring%d" % i) for i in range(3)]
        K.pscT = sb(K, top, [128, L, 4], F32, "pscT")
        K.RT = sb(K, top, [128, 128], BF16, "RT")
        K.maskL = sb(K, top, [128, 4, 128], BF16, "maskL")
        K.maskR = sb(K, top, [128, 4, 128], BF16, "maskR")
        S.dma(K.pscT[:], K.pscT_d[:, :, :], R=[K.pscT_d], W=[K.pscT])
        S.dma(K.RT[:], K.RT_d[:, :], R=[K.RT_d], W=[K.RT], q="pool")
        S.dma(K.maskL[:], K.maskL_d[:, :, :], R=[K.maskL_d], W=[K.maskL], q="pool")
        S.dma(K.maskR[:], K.maskR_d[:, :, :], R=[K.maskR_d], W=[K.maskR], q="pool")
        mix_consts(K, top)
        cT = sb(K, top, [128, NKC, 2], F32, "cT")
        S.dma(K.identF[:], K.identF_d[:, :], R=[K.identF_d], W=[K.identF])
        S.dma(cT[:], K.condT[:, :, :], R=[K.condT], W=[cT])
        S.dma(K.bmodT[:], K.bmodT_d[:, :, :], R=[K.bmodT_d], W=[K.bmodT])
        S.dma(K.norm1T[:], K.norm1T_d[:, :, :], R=[K.norm1T_d], W=[K.norm1T])
        S.dma(K.norm2T[:], K.norm2T_d[:, :, :], R=[K.norm2T_d], W=[K.norm2T])
        S.dma(K.normfT[:], K.normfT_d[:, :], R=[K.normfT_d], W=[K.normfT])
        S.op("dve", lambda e: e.memset(K.onesB[:], 1.0), W=[K.onesB])
        act(K, K.sc[:], cT[:], AF.Silu, R=[cT], W=[K.sc])

        phase_in(K)
        for l in range(L):
            phase_mod(K, l)
            phase_norm(K, l, 1)
            if cfg.get("mixers", True):
                phase_mixers(K, l)
            else:
                phase_zero_y(K)
            phase_merge(K, l)
            phase_norm(K, l, 2)
            phase_ffn(K, l)
        phase_norm(K, L, "f")
        S.final_wait()
    K.n_ops = S.n_ops
    return nc, K


def host_common(inp, L):
    f = np.float32
    d = {}
    d["bmodT_d"] = np.ascontiguousarray(inp["b_mod"][:L].reshape(L, 96, 128).transpose(2, 0, 1)).astype(f)
    d["norm1T_d"] = np.ascontiguousarray(inp["norm1"][:L].reshape(L, NKC, 128).transpose(2, 0, 1)).astype(f)
    d["norm2T_d"] = np.ascontiguousarray(inp["norm2"][:L].reshape(L, NKC, 128).transpose(2, 0, 1)).astype(f)
    d["normfT_d"] = np.ascontiguousarray(inp["norm_f"].reshape(NKC, 128).T).astype(f)
    d["identF_d"] = np.eye(128, dtype=f)
    for k in ("w_mod", "w_in", "w_br_pool", "w_br_attn", "w_br_delta", "w_br_gla", "w_out", "w_gate", "w_up", "w_down"):
        d[k] = np.ascontiguousarray(inp[k][:L])
    return d


def host_core(inp, common, b_s, p_list, L):
    d = dict(common)
    d["x_s"] = np.ascontiguousarray(inp["x_sample"][b_s])
    d["x_p"] = np.ascontiguousarray(np.concatenate([inp["x_prompt"][p] for p in p_list], axis=0))
    cond = np.stack([inp["c"][b_s], inp["c_ctx"]], axis=0)
    d["condT"] = np.ascontiguousarray(cond.reshape(2, NKC, 128).transpose(2, 1, 0)).astype(np.float32)
    return d


def seqs_of(K):
    out = [("S", 0, K.T_S, 0)]
    for p in range(K.NP):
        out.append(("P", K.T_S + p * K.T_P, K.T_P, p))
    return out


def seq_tiles(T):
    return [(t, min(512, T - t)) for t in range(0, T, 512)]


def load_h(K, st, t0, T):
    h = sb(K, st, [128, NKC, T], BF16, "H")
    K.S.dma(h[:], K.HD[:, :, t0:t0 + T].rearrange("c p t -> p c t"), R=[K.HD], W=[h])
    return h


def win_cols(K, l, c0, n):
    return K.w_in[l, :, c0:c0 + n].rearrange("(k p) c -> p k c", p=128)


def mixer_pool(K, l, seq):
    S = K.S
    kind, t0, T, idx = seq
    ring = Ring(K)
    rc_d = K.rcnt_S if kind == "S" else K.rcnt_P
    with ExitStack() as st:
        h = load_h(K, st, t0, T)
        u = sb(K, st, [128, T + 16], F32, "pu")
        s1 = sb(K, st, [128, T + 16], F32, "ps1")
        s2 = sb(K, st, [128, T + 16], F32, "ps2")
        rc = sb(K, st, [128, T], F32, "prc")
        pbf = sb(K, st, [128, T], BF16, "ppb")
        pw = sb(K, st, [128, 4, 128], BF16, "ppw")
        yo = [sb(K, st, [128, 512], BF16, "pyo") for _ in range(2)]
        S.dma(pw[:], K.pool_w[l].rearrange("g c d -> c g d"), R=[K.pool_w], W=[pw], q="pool")
        wb, w = ring.load(K.w_in, win_cols(K, l, OFF["pool"], 512), NKC, 512)
        cnt = 0
        for g, win in enumerate((2, 4, 8, 16)):
            S.op("pool", lambda e: e.memset(u[:, 0:8], 0.0), W=[u])
            S.op("pool", lambda e: e.memset(u[:, T + 8:T + 16], 0.0), W=[u])
            S.dma(rc[:], rc_d[g:g + 1, :].partition_broadcast(128), R=[rc_d], W=[rc])
            for (o, n) in seq_tiles(T):
                pb = K.PS[cnt % 2]
                cnt += 1
                for kc in range(NKC):
                    mm(K, pb, pb[:, :n], wb, w[:, kc, g * 128:(g + 1) * 128], h, h[:, kc, o:o + n],
                       start=(kc == 0), stop=(kc == NKC - 1))
                cp(K, "act", u[:, 8 + o:8 + o + n], pb[:, :n], R=[pb], W=[u])
            N = T + 16
            tt(K, "dve", s1[:, 1:N], u[:, 0:N - 1], u[:, 1:N], ALU.add, R=[u], W=[s1])
            cur = s1
            if win >= 4:
                tt(K, "dve", s2[:, 2:N - 1], s1[:, 1:N - 2], s1[:, 3:N], ALU.add, R=[s1], W=[s2])
                cur = s2
            if win >= 8:
                tt(K, "dve", s1[:, 4:N - 3], s2[:, 2:N - 5], s2[:, 6:N - 1], ALU.add, R=[s2], W=[s1])
                cur = s1
            if win >= 16:
                tt(K, "dve", s2[:, 8:N - 8], s1[:, 4:N - 12], s1[:, 12:N - 4], ALU.add, R=[s1], W=[s2])
                cur = s2
            oth = s1 if cur is s2 else s2
            tt(K, "dve", oth[:, 8:8 + T], cur[:, 8:8 + T], rc[:], ALU.mult, R=[cur, rc], W=[oth])
            tt(K, "dve", pbf[:], oth[:, 8:8 + T], u[:, 8:8 + T], ALU.subtract, R=[oth, u], W=[pbf])
            for (o, n) in seq_tiles(T):
                pb = K.PS[2 + cnt % 2]
                y = yo[cnt % 2]
                cnt += 1
                mm(K, pb, pb[:, :n], pw, pw[:, g, :], pbf, pbf[:, o:o + n])
                ts(K, "dve", y[:, :n], pb[:, :n], K.pscT[:, l, g:g + 1], None, ALU.mult, None, R=[pb, K.pscT], W=[y])
                S.dma(K.YD[g, :, t0 + o:t0 + o + n], y[:, :n], R=[y], W=[K.YDB[g]])
        S.barrier()


def mixer_attn(K, l, seq):
    S = K.S
    kind, t0, T, idx = seq
    ring = Ring(K)
    rope = (kind == "S")
    nb = T // 128
    with ExitStack() as outer:
        qT = sb(K, outer, [128, 8, T], BF16, "aq")
        kT = sb(K, outer, [128, 2, T], BF16, "ak")
        vt = sb(K, outer, [128, nb, 256], BF16, "av")
        with ExitStack() as st:
            h = load_h(K, st, t0, T)
            qb = [sb(K, st, [128, 512], BF16, "aqb") for _ in range(2)]
            t1 = [sb(K, st, [128, 512], F32, "at1") for _ in range(2)]
            t2 = [sb(K, st, [128, 512], F32, "at2") for _ in range(2)]
            kvo = [sb(K, st, [128, 512], F32, "akvo") for _ in range(2)]
            if rope:
                cs = sb(K, st, [128, T], BF16, "acos")
                sn = sb(K, st, [128, T], BF16, "asin")
                S.dma(cs[:], K.cos_d[:, :], R=[K.cos_d], W=[cs], q="pool")
                S.dma(sn[:], K.sin_d[:, :], R=[K.sin_d], W=[sn], q="pool")
            cnt = 0
            for hh in range(10):
                if hh % 4 == 0:
                    ncols = 512 if hh < 8 else 256
                    wb, w = ring.load(K.w_in, win_cols(K, l, OFF["q_at"] + hh * 128, ncols), NKC, ncols)
                m = hh % 4
                dst = qT[:, hh, :] if hh < 8 else kT[:, hh - 8, :]
                dbuf = qT if hh < 8 else kT
                scale = float(128 ** -0.5) if hh < 8 else 1.0
                for (o, n) in seq_tiles(T):
                    pb = K.PS[cnt % 2]
                    for kc in range(NKC):
                        mm(K, pb, pb[:, :n], wb, w[:, kc, m * 128:(m + 1) * 128], h, h[:, kc, o:o + n],
                           start=(kc == 0), stop=(kc == NKC - 1))
                    if not rope:
                        q_ = qb[cnt % 2]
                        K.S.op("act", lambda e, d=q_[:, :n], p=pb[:, :n], s=scale: e.activation(out=d, in_=p, func=AF.Copy, scale=s),
                               R=[pb], W=[q_])
                        cp(K, "dve", dst[:, o:o + n], q_[:, :n], R=[q_], W=[dbuf])
                    else:
                        q_ = qb[cnt % 2]
                        K.S.op("act", lambda e, d=q_[:, :n], p=pb[:, :n], s=scale: e.activation(out=d, in_=p, func=AF.Copy, scale=s),
                               R=[pb], W=[q_])
                        pr = K.PS[2 + cnt % 2]
                        mm(K, pr, pr[:, :n], K.RT, K.RT[:], q_, q_[:, :n])
                        a1, a2 = t1[cnt % 2], t2[cnt % 2]
                        tt(K, "pool", a1[:, :n], q_[:, :n], cs[:, o:o + n], ALU.mult, R=[q_, cs], W=[a1])
                        tt(K, "dve", a2[:, :n], pr[:, :n], sn[:, o:o + n], ALU.mult, R=[pr, sn], W=[a2])
                        tt(K, "dve", dst[:, o:o + n], a1[:, :n], a2[:, :n], ALU.add, R=[a1, a2], W=[dbuf])
                    cnt += 1
            wb, w = ring.load(K.w_in, win_cols(K, l, OFF["k_at"], 512), NKC, 512)
            for b in range(nb):
                pb = K.PS[4 + b % 2]
                for kc in range(NKC):
                    mm(K, pb, pb[:, :], h, h[:, kc, b * 128:(b + 1) * 128], wb, w[:, kc, :],
                       start=(kc == 0), stop=(kc == NKC - 1))
                if kind == "P" and K.cfg.get("kvout", 1):
                    ko = kvo[b % 2]
                    cp(K, "dve", ko[:], pb[:, :], R=[pb], W=[ko])
                    cp(K, "act", vt[:, b, :], ko[:, 256:512], R=[ko], W=[vt])
                else:
                    cp(K, "act", vt[:, b, :], pb[:, 256:512], R=[pb], W=[vt])
                if kind == "P" and K.cfg.get("kvout", 1):
                    S.dma(K.nk[idx, l, b * 128:(b + 1) * 128, :], ko[:, 0:256], R=[ko], W=[K.nk])
                    S.dma(K.nv[idx, l, b * 128:(b + 1) * 128, :], ko[:, 256:512], R=[ko], W=[K.nv])
            S.barrier()
        with ExitStack() as st:
            E = [sb(K, st, [128, 512], BF16, "aE") for _ in range(3)]
            rd = [sb(K, st, [128, 512], F32, "ard") for _ in range(2)]
            ob = [sb(K, st, [128, 512], BF16, "aob") for _ in range(2)]
            esr = sb(K, st, [1, 1024], BF16, "aes")
            esf = sb(K, st, [1, 1024], F32, "aesf")
            S.dma(esf[:], K.sink_rep[l, :, :], R=[K.sink_rep], W=[esf])
            act(K, esr[:], esf[:], AF.Exp, R=[esf], W=[esr])
            if kind == "S" and K.cfg.get("attn_dbg", 3) >= 2:
                kc_f = sb(K, st, [128, 2, 256], F32, "akcf")
                kcT = sb(K, st, [128, 2, 256], BF16, "akcT")
                vc = sb(K, st, [128, 2, 256], BF16, "avc")
                S.dma(kc_f[:], K.cache_k[l].rearrange("(b p) c -> p b c", p=128), R=[K.cache_k], W=[kc_f])
                S.dma(vc[:], K.cache_v[l].rearrange("(b p) c -> p b c", p=128), R=[K.cache_v], W=[vc], q="pool")
                for n in range(2):
                    pb = K.PS[6]
                    for b in range(2):
                        tr(K, pb, pb[:, b * 128:(b + 1) * 128], kc_f, kc_f[:, b, n * 128:(n + 1) * 128], K.identF, K.identF[:])
                    cp(K, "dve", kcT[:, n, :], pb[:, 0:256], R=[pb], W=[kcT])
            cnt = 0
            ecnt = 0
            dbg = K.cfg.get("attn_dbg", 3)
            for n in range(2 if dbg >= 3 else 0):
                for i in range(nb):
                    kbs = []
                    if kind == "S":
                        for b in range(2):
                            kbs.append((kcT, kcT[:, n, b * 128:(b + 1) * 128], vc, vc[:, b, n * 128:(n + 1) * 128], None))
                        for j, mk in ((i - 1, K.maskL), (i, None), (i + 1, K.maskR)):
                            if 0 <= j < nb:
                                kbs.append((kT, kT[:, n, j * 128:(j + 1) * 128], vt, vt[:, j, n * 128:(n + 1) * 128], mk))
                    else:
                        for b in range(nb):
                            kbs.append((kT, kT[:, n, b * 128:(b + 1) * 128], vt, vt[:, b, n * 128:(n + 1) * 128], None))
                    po = K.PS[2 + cnt % 2]
                    pd = K.PS[4 + cnt % 2]
                    qap = qT[:, 4 * n:4 * n + 4, i * 128:(i + 1) * 128]
                    for bi, (kb_, kap, vb_, vap, mk) in enumerate(kbs):
                        pS = K.PS[ecnt % 2]
                        e_ = E[ecnt % 3]
                        ecnt += 1
                        mm(K, pS, pS[:, :].rearrange("p (a b) -> p a b", a=4), kb_, kap, qT, qap)
                        act(K, e_[:], pS[:, :], AF.Exp, R=[pS], W=[e_])
                        if mk is not None:
                            tt(K, "pool", e_[:].rearrange("p (a b) -> p a b", a=4), e_[:].rearrange("p (a b) -> p a b", a=4),
                               mk[:], ALU.mult, R=[e_, mk], W=[e_])
                        mm(K, po, po[:, :], vb_, vap, e_, e_[:], start=(bi == 0), stop=(bi == len(kbs) - 1))
                        mm(K, pd, pd[:, :], K.onesB, K.onesB[:], e_, e_[:], start=(bi == 0), stop=False)
                    mm(K, pd, pd[:, :], K.onesB, K.onesB[0:1, :], esr, esr[0:1, n * 512:(n + 1) * 512], start=False, stop=True)
                    r_ = rd[cnt % 2]
                    o_ = ob[cnt % 2]
                    K.S.op("dve", lambda e, a=r_[:], b=pd[:, :]: e.reciprocal(out=a, in_=b), R=[pd], W=[r_])
                    tt(K, "dve", o_[:], po[:, :], r_[:], ALU.mult, R=[po, r_], W=[o_])
                    S.dma(K.YD[4 + 4 * n:8 + 4 * n, :, t0 + i * 128:t0 + (i + 1) * 128].rearrange("c p t -> p c t"),
                          o_[:].rearrange("p (a b) -> p a b", a=4), R=[o_], W=K.YDB[4 + 4 * n:8 + 4 * n])
                    cnt += 1
            S.barrier()


def zero_y(K, chunks, seq):
    S = K.S
    kind, t0, T, idx = seq
    with ExitStack() as st:
        z = sb(K, st, [128, 512], BF16, "zy")
        S.op("dve", lambda e: e.memset(z[:], 0.0), W=[z])
        for c in chunks:
            for (o, n) in seq_tiles(T):
                S.dma(K.YD[c, :, t0 + o:t0 + o + n], z[:, :n], R=[z], W=[K.YDB[c]])
        S.barrier()


def phase_mixers(K, l):
    mix = K.cfg.get("mix", ("pool", "attn", "dn", "gla"))
    for seq in seqs_of(K):
        if "pool" in mix:
            mixer_pool(K, l, seq)
        else:
            zero_y(K, range(0, 4), seq)
        if "attn" in mix and seq[0] in K.cfg.get("attn_kinds", "SP"):
            mixer_attn(K, l, seq)
        else:
            zero_y(K, range(4, 12), seq)
        if "dn" in mix:
            mixer_dn(K, l, seq)
        else:
            zero_y(K, range(12, 16), seq)
        if "gla" in mix:
            mixer_gla(K, l, seq)
        else:
            zero_y(K, range(16, 20), seq)


def host_mix_common(inp, cfg):
    f = np.float32
    L, T_S, T_P = cfg["L"], cfg["T_S"], cfg["T_P"]
    d = {}
    d["pool_w"] = np.ascontiguousarray(inp["pool_w"][:L])
    d["pscT_d"] = np.ascontiguousarray(inp["pool_scale"][:L].reshape(L, 4, 128).transpose(2, 0, 1)).astype(f)

    def rcnt(T):
        pos = np.arange(T)
        out = np.zeros((4, T), f)
        for g, w in enumerate((2, 4, 8, 16)):
            lo = np.clip(pos - w // 2, 0, T)
            hi = np.clip(pos + w // 2, 0, T)
            out[g] = 1.0 / (hi - lo)
        return out
    d["rcnt_S"] = rcnt(T_S)
    d["rcnt_P"] = rcnt(T_P)
    t = np.arange(T_S)
    row = (t // 64).astype(np.float64)
    col = (t % 64).astype(np.float64)
    inv = 10000.0 ** (-np.arange(32, dtype=np.float64) / 32)
    ang = np.zeros((128, T_S))
    for fi in range(128):
        pos = row if fi < 64 else col
        ang[fi] = pos * inv[fi % 32]
    d["cos_d"] = np.cos(ang).astype(f)
    d["sin_d"] = np.sin(ang).astype(f)
    R = np.zeros((128, 128), f)
    for fi in range(128):
        if fi % 64 < 32:
            R[fi, fi + 32] = -1.0
        else:
            R[fi, fi - 32] = 1.0
    d["RT_d"] = np.ascontiguousarray(R.T)
    j = np.arange(128)[:, None]
    r = np.arange(128)[None, :]
    d["maskL_d"] = np.ascontiguousarray(np.broadcast_to((j >= r).astype(f)[:, None, :], (128, 4, 128)))
    d["maskR_d"] = np.ascontiguousarray(np.broadcast_to((j <= r).astype(f)[:, None, :], (128, 4, 128)))
    d["sink_rep"] = np.ascontiguousarray(np.repeat(inp["attn_sink"][:L], 128, axis=1).reshape(L, 1, 1024)).astype(f)
    same = (j // 64) == (r // 64)
    d["triF_d"] = np.stack([((j <= r) & same), ((j >= r) & same)]).astype(f)
    d["blk1_d"] = same.astype(f)
    d["negoff_d"] = (np.eye(128) - 1.0).astype(f)
    es = np.zeros((3, 3, 128), f)
    for k in range(3):
        es[k, k, :] = 1.0
    d["esel_d"] = es
    w2p = np.zeros((L, 32, 512), f)
    w2p[:, 0:16, 0:256] = inp["gla_w2"][:L, 0]
    w2p[:, 16:32, 256:512] = inp["gla_w2"][:L, 1]
    d["w2pad_d"] = w2p
    d["b2row_d"] = np.ascontiguousarray(inp["gla_b2"][:L].reshape(L, 1, 512)).astype(f)
    d["gnT_d"] = np.ascontiguousarray(inp["gla_norm"][:L].T).astype(f)
    d["dnT_d"] = np.ascontiguousarray(inp["dn_norm"][:L].T).astype(f)
    d["convT_d"] = np.ascontiguousarray(inp["dn_conv"][:L].reshape(L, 4, 12, 128).transpose(3, 0, 2, 1)).astype(f)
    d["dtb_d"] = np.ascontiguousarray(np.broadcast_to(inp["dn_dt_bias"][:L].reshape(1, L, 8), (128, L, 8))).astype(f)
    d["alog_d"] = np.ascontiguousarray(np.broadcast_to(inp["dn_a_log"][:L].reshape(1, L, 8), (128, L, 8))).astype(f)
    return d


def host_mix_core(inp, b_s, p_list, cfg):
    L = cfg["L"]
    d = {}
    d["cache_k"] = np.ascontiguousarray(inp["cache_k"][b_s, :L].reshape(L, 256, 256))
    d["cache_v"] = np.ascontiguousarray(inp["cache_v"][b_s, :L].reshape(L, 256, 256))
    d["state_delta"] = np.ascontiguousarray(inp["state_delta"][b_s, :L])
    d["state_gla"] = np.ascontiguousarray(inp["state_gla"][b_s, :L])
    return d


def mix_inputs(K, ein):
    L = K.L
    ein("triF_d", [2, 128, 128])
    ein("blk1_d", [128, 128])
    ein("negoff_d", [128, 128])
    ein("esel_d", [3, 3, 128])
    ein("w2pad_d", [L, 32, 512])
    ein("b2row_d", [L, 1, 512])
    ein("gnT_d", [128, L])
    ein("dnT_d", [128, L])
    ein("convT_d", [128, L, 12, 4])
    ein("dtb_d", [128, L, 8])
    ein("alog_d", [128, L, 8])
    K.DS = Buf(K.nc.dram_tensor("DS", [16, 128, K.NT], BF16).ap(), "DS")


def mix_consts(K, top):
    S = K.S
    L = K.L
    K.triF = sb(K, top, [128, 2, 128], F32, "triF")
    K.blk1 = sb(K, top, [128, 128], F32, "blk1")
    K.negoff = sb(K, top, [128, 128], F32, "negoff")
    K.esel = sb(K, top, [3, 3, 128], F32, "esel")
    K.identB = sb(K, top, [128, 128], BF16, "identB")
    K.gnT = sb(K, top, [128, L], F32, "gnT")
    K.dnT = sb(K, top, [128, L], F32, "dnT")
    K.convT = sb(K, top, [128, L, 12, 4], F32, "convT")
    K.dtb = sb(K, top, [128, L, 8], F32, "dtb")
    K.nea = sb(K, top, [128, L, 8], F32, "nea")
    K.onesF = sb(K, top, [128, 128], F32, "onesF")
    S.dma(K.triF[:], K.triF_d.t.rearrange("d j i -> j d i"), R=[K.triF_d], W=[K.triF])
    S.dma(K.blk1[:], K.blk1_d[:, :], R=[K.blk1_d], W=[K.blk1])
    S.dma(K.negoff[:], K.negoff_d[:, :], R=[K.negoff_d], W=[K.negoff])
    S.dma(K.esel[:], K.esel_d[:, :, :], R=[K.esel_d], W=[K.esel])
    S.dma(K.gnT[:], K.gnT_d[:, :], R=[K.gnT_d], W=[K.gnT])
    S.dma(K.dnT[:], K.dnT_d[:, :], R=[K.dnT_d], W=[K.dnT])
    S.dma(K.convT[:], K.convT_d[:, :, :, :], R=[K.convT_d], W=[K.convT])
    S.dma(K.dtb[:], K.dtb_d[:, :, :], R=[K.dtb_d], W=[K.dtb])
    S.dma(K.nea[:], K.alog_d[:, :, :], R=[K.alog_d], W=[K.nea])
    S.dma(K.identB[:], K.identF_d[:, :], R=[K.identF_d], W=[K.identB], q="pool")
    S.op("dve", lambda e: e.memset(K.onesF[:], 1.0), W=[K.onesF])
    act(K, K.nea[:], K.nea[:], AF.Exp, R=[K.nea], W=[K.nea])
    ts(K, "dve", K.nea[:], K.nea[:], -1.0, None, ALU.mult, None, R=[K.nea], W=[K.nea])


def final_gate_norm(K, l, seq, oacc, gbuf, gsil, nw, ychunk, st):
    S = K.S
    kind, t0, T, idx = seq
    sq = sb(K, st, [128, 512], BF16, "fsq")
    rr = sb(K, st, [128, 512], F32, "frr")
    tm = sb(K, st, [128, 512], F32, "ftm")
    yo = [sb(K, st, [128, 512], BF16, "fyo") for _ in range(2)]
    for ti, (o, n) in enumerate(seq_tiles(T)):
        act(K, sq[:, :n], oacc[:, o:o + n], AF.Square, R=[oacc], W=[sq])
        pb = K.PS[6]
        mm(K, pb, pb[:, :n], K.onesB, K.onesB[:], sq, sq[:, :n])
        rsqrt(K, rr, rr[:, :n], pb, pb[:, :n], 1.0 / 128)
        tt(K, "dve", tm[:, :n], oacc[:, o:o + n], rr[:, :n], ALU.mult, R=[oacc, rr], W=[tm])
        y = yo[ti % 2]
        stt(K, "dve", y[:, :n], tm[:, :n], nw, gsil[:, o:o + n], ALU.mult, ALU.mult, R=[tm, gbuf], W=[y])
        S.dma(K.YD[ychunk, :, t0 + o:t0 + o + n], y[:, :n], R=[y], W=[K.YDB[ychunk]])


def proj_fm(K, h, wb, w_ap, T, pbanks, consume):
    for ti, (o, n) in enumerate(seq_tiles(T)):
        pb = pbanks[ti % len(pbanks)]
        for kc in range(NKC):
            mm(K, pb, pb[0:w_ap.shape[-1], :n], wb, w_ap[:, kc, :], h, h[:, kc, o:o + n],
               start=(kc == 0), stop=(kc == NKC - 1))
        consume(o, n, pb)


def mixer_gla(K, l, seq):
    S = K.S
    kind, t0, T, idx = seq
    ring = Ring(K)
    nb = T // 128
    with ExitStack() as outer:
        qk = sb(K, outer, [128, 4, T], BF16, "gqk")
        rs = sb(K, outer, [128, 4, T], BF16, "grs")
        vt = sb(K, outer, [128, nb, 512], BF16, "gvt")
        lrT = sb(K, outer, [32, T], BF16, "glr")
        w2p = sb(K, outer, [32, 512], BF16, "gw2")
        b2r = sb(K, outer, [1, 512], BF16, "gb2")
        S.dma(w2p[:], K.w2pad_d[l], R=[K.w2pad_d], W=[w2p], q="pool")
        S.dma(b2r[:], K.b2row_d[l], R=[K.b2row_d], W=[b2r], q="pool")
        with ExitStack() as st:
            h = load_h(K, st, t0, T)
            wb, w = ring.load(K.w_in, win_cols(K, l, OFF["q_gl"], 512), NKC, 512)
            for c in range(4):
                proj_fm(K, h, wb, w[:, :, c * 128:(c + 1) * 128], T, [K.PS[0], K.PS[1]],
                        lambda o, n, pb, c=c: cp(K, "act", qk[:, c, o:o + n], pb[:, :n], R=[pb], W=[qk]))
            wb, w = ring.load(K.w_in, win_cols(K, l, OFF["r_gl"], 512), NKC, 512)
            for c in range(4):
                proj_fm(K, h, wb, w[:, :, c * 128:(c + 1) * 128], T, [K.PS[0], K.PS[1]],
                        lambda o, n, pb, c=c: act(K, rs[:, c, o:o + n], pb[:, :n], AF.Silu, R=[pb], W=[rs]))
            wb, w = ring.load(K.w_in, win_cols(K, l, OFF["lr"], 32), NKC, 32)
            proj_fm(K, h, wb, w[:, :, 0:32], T, [K.PS[0], K.PS[1]],
                    lambda o, n, pb: cp(K, "act", lrT[:, o:o + n], pb[0:32, :n], R=[pb], W=[lrT]))
            wb, w = ring.load(K.w_in, win_cols(K, l, OFF["v_gl"], 512), NKC, 512)
            for b in range(nb):
                pb = K.PS[2 + b % 2]
                for kc in range(NKC):
                    mm(K, pb, pb[:, :], h, h[:, kc, b * 128:(b + 1) * 128], wb, w[:, kc, :],
                       start=(kc == 0), stop=(kc == NKC - 1))
                cp(K, "dve", vt[:, b, :], pb[:, :], R=[pb], W=[vt])
            S.barrier()
        for hd in range(4):
            with ExitStack() as st:
                hp = (hd % 2) * 64
                qv = qk[hp:hp + 64, hd // 2, :]
                kv = qk[hp:hp + 64, 2 + hd // 2, :]
                oacc = sb(K, st, [128, T], F32, "goacc")
                S.op("pool", lambda e: e.memset(oacc[:], 0.0), W=[oacc])
                Sf = [sb(K, st, [128, 128], F32, "gS") for _ in range(2)]
                P = slice(hp, hp + 64)
                Sb = [sb(K, st, [128, 128], BF16, "gSb") for _ in range(2)]
                for d in range(2):
                    if kind == "S":
                        S.dma(Sf[d][P, :], K.state_gla[l, d, hd], R=[K.state_gla], W=[Sf[d]])
                    else:
                        S.op("pool", lambda e, d=d: e.memset(Sf[d][:], 0.0), W=[Sf[d]])
                    cp(K, "act", Sb[d][P, :], Sf[d][P, :], R=[Sf[d]], W=[Sb[d]])
                U = [dict(e1=sb(K, st, [128, 64], F32, "ge1"), sp=sb(K, st, [128, 64], F32, "gsp"),
                          gcp=sb(K, st, [128, 128], F32, "ggcp"), egc=sb(K, st, [128, 128], F32, "gegc"),
                          engc=sb(K, st, [128, 128], F32, "gengc"), ekd=sb(K, st, [128, 128], F32, "gekd"),
                          nb_=sb(K, st, [128, 2], F32, "gnb"), qg=sb(K, st, [128, 128], BF16, "gqg"),
                          kg=sb(K, st, [128, 128], BF16, "gkg"), kdT=sb(K, st, [128, 128], BF16, "gkdT"),
                          kd=sb(K, st, [128, 64], BF16, "gkd"), aT=sb(K, st, [128, 128], BF16, "gaT"))
                     for _ in range(2)]
                for s_ in range(nb):
                    for d in range(2):
                        b = s_ if d == 0 else nb - 1 - s_
                        u = U[d]
                        pA, pB_, pC = K.PS[3 * d], K.PS[3 * d + 1], K.PS[3 * d + 2]
                        tk = slice(b * 128, (b + 1) * 128)
                        cw = slice(d * 256 + hd * 64, d * 256 + hd * 64 + 64)
                        mm(K, pA, pA[:, 0:64], lrT, lrT[:, tk], w2p, w2p[:, cw], start=True, stop=False)
                        mm(K, pA, pA[:, 0:64], K.onesB, K.onesB[0:1, :], b2r, b2r[0:1, cw], start=False, stop=True)
                        act(K, u["e1"][:], pA[:, 0:64], AF.Exp, R=[pA], W=[u["e1"]], scale=-1.0)
                        act(K, u["sp"][:], u["e1"][:], AF.Ln, R=[u["e1"]], W=[u["sp"]], bias=1.0)
                        mm(K, pA, pA[P, 128:256], u["sp"], u["sp"][:], K.triF, K.triF[:, d, :])
                        cp(K, "act", u["gcp"][P, :], pA[P, 128:256], R=[pA], W=[u["gcp"]])
                        act(K, u["egc"][P, :], u["gcp"][P, :], AF.Exp, R=[u["gcp"]], W=[u["egc"]], scale=-1.0 / 16)
                        act(K, u["engc"][P, :], u["gcp"][P, :], AF.Exp, R=[u["gcp"]], W=[u["engc"]], scale=1.0 / 16)
                        c0 = 63 if d == 0 else 0
                        ts(K, "dve", u["nb_"][P, :], u["gcp"][P, c0:c0 + 65:64], -1.0 / 16, None, ALU.mult, None,
                           R=[u["gcp"]], W=[u["nb_"]])
                        for c in range(2):
                            act(K, u["ekd"][P, c * 64:(c + 1) * 64], u["gcp"][P, c * 64:(c + 1) * 64], AF.Exp,
                                R=[u["gcp"], u["nb_"]], W=[u["ekd"]], scale=1.0 / 16, bias=u["nb_"][P, c:c + 1])
                        stt(K, "dve", u["qg"][P, :], qv[:, tk], 0.125, u["egc"][P, :], ALU.mult, ALU.mult, R=[qk, u["egc"]], W=[u["qg"]])
                        tt(K, "dve", u["kg"][P, :], kv[:, tk], u["engc"][P, :], ALU.mult, R=[qk, u["engc"]], W=[u["kg"]])
                        tt(K, "dve", u["kdT"][P, :], kv[:, tk], u["ekd"][P, :], ALU.mult, R=[qk, u["ekd"]], W=[u["kdT"]])
                        pT = K.PB
                        tr(K, pT, pT[:, d * 64:(d + 1) * 64], u["kdT"], u["kdT"][P, :], K.identB, K.identB[P, hp:hp + 64])
                        cp(K, "act", u["kd"][:], pT[:, d * 64:(d + 1) * 64], R=[pT], W=[u["kd"]])
                        mm(K, pB_, pB_[:, 0:128], u["kg"], u["kg"][P, :], u["qg"], u["qg"][P, :])
                        tt(K, "dve", u["aT"][:], pB_[:, 0:128], K.triF[:, d, :], ALU.mult, R=[pB_, K.triF], W=[u["aT"]])
                        for c in ((0, 1) if d == 0 else (1, 0)):
                            cs_ = slice(c * 64, (c + 1) * 64)
                            vb = vt[:, b, hd * 128:(hd + 1) * 128]
                            mm(K, pC, pC[:, 0:64], Sb[d], Sb[d][P, :], u["qg"], u["qg"][P, cs_], start=True, stop=False)
                            mm(K, pC, pC[:, 0:64], vt, vb, u["aT"], u["aT"][:, cs_], start=False, stop=True)
                            ot = oacc[:, b * 128 + c * 64:b * 128 + (c + 1) * 64]
                            tt(K, "dve", ot, pC[:, 0:64], ot, ALU.add, R=[pC, oacc], W=[oacc])
                            mm(K, pC, pC[P, 128:256], u["kd"], u["kd"][cs_, :], vt, vt[cs_, b, hd * 128:(hd + 1) * 128])
                            col = c * 64 + (63 if d == 0 else 0)
                            stt(K, "dve", Sf[d][P, :], Sf[d][P, :], u["egc"][P, col:col + 1], pC[P, 128:256], ALU.mult, ALU.add,
                                R=[Sf[d], u["egc"], pC], W=[Sf[d]])
                            cp(K, "act", Sb[d][P, :], Sf[d][P, :], R=[Sf[d]], W=[Sb[d]])
                if kind == "P":
                    for d in range(2):
                        S.dma(K.nsg[idx, l, d, hd], Sf[d][P, :], R=[Sf[d]], W=[K.nsg])
                final_gate_norm(K, l, seq, oacc, rs, rs[:, hd, :], K.gnT[:, l:l + 1], 16 + hd, st)
                S.barrier()


def mixer_dn(K, l, seq):
    S = K.S
    kind, t0, T, idx = seq
    ring = Ring(K)
    nb = T // 128
    with ExitStack() as outer:
        gg = sb(K, outer, [128, nb, 8], F32, "dg")
        be = sb(K, outer, [128, nb, 8], F32, "dbeta")
        with ExitStack() as st:
            h = load_h(K, st, t0, T)
            xp = [sb(K, st, [128, T + 3], F32, "dxp") for _ in range(2)]
            cv = [sb(K, st, [128, T], F32, "dcv") for _ in range(2)]
            sq = sb(K, st, [128, 512], BF16, "dsq")
            rn = sb(K, st, [128, 512], F32, "drn")
            ob = [sb(K, st, [128, T], BF16, "dob") for _ in range(2)]
            e1 = sb(K, st, [128, 8], F32, "de1")
            for x in xp:
                S.op("pool", lambda e, x=x: e.memset(x[:, 0:2], 0.0), W=[x])
                S.op("pool", lambda e, x=x: e.memset(x[:, T + 2:T + 3], 0.0), W=[x])
            for c in range(16):
                if c % 4 == 0:
                    wb, w = ring.load(K.w_in, win_cols(K, l, OFF["qkv"] + c * 128, 512), NKC, 512)
                wa = w[:, :, (c % 4) * 128:(c % 4 + 1) * 128]
                o_ = ob[c % 2]
                if c >= 12:
                    proj_fm(K, h, wb, wa, T, [K.PS[0], K.PS[1]],
                            lambda o, n, pb, o_=o_: act(K, o_[:, o:o + n], pb[:, :n], AF.Silu, R=[pb], W=[o_]))
                else:
                    x = xp[c % 2]
                    y = cv[c % 2]
                    proj_fm(K, h, wb, wa, T, [K.PS[0], K.PS[1]],
                            lambda o, n, pb, x=x: cp(K, "act", x[:, 2 + o:2 + o + n], pb[:, :n], R=[pb], W=[x]))
                    ts(K, "pool", y[:], x[:, 0:T], K.convT[:, l, c, 0:1], None, ALU.mult, None, R=[x, K.convT], W=[y])
                    for j in range(1, 4):
                        stt(K, "dve", y[:], x[:, j:j + T], K.convT[:, l, c, j:j + 1], y[:], ALU.mult, ALU.add,
                            R=[x, y, K.convT], W=[y])
                    if c >= 8:
                        act(K, o_[:], y[:], AF.Silu, R=[y], W=[o_])
                    else:
                        act(K, y[:], y[:], AF.Silu, R=[y], W=[y])
                        for (o, n) in seq_tiles(T):
                            act(K, sq[:, :n], y[:, o:o + n], AF.Square, R=[y], W=[sq])
                            pb = K.PS[2]
                            mm(K, pb, pb[:, :n], K.onesB, K.onesB[:], sq, sq[:, :n])
                            rsqrt(K, rn, rn[:, :n], pb, pb[:, :n], 1.0)
                            if c < 4:
                                stt(K, "dve", o_[:, o:o + n], y[:, o:o + n], float(128 ** -0.5), rn[:, :n], ALU.mult, ALU.mult,
                                    R=[y, rn], W=[o_])
                            else:
                                tt(K, "dve", o_[:, o:o + n], y[:, o:o + n], rn[:, :n], ALU.mult, R=[y, rn], W=[o_])
                S.dma(K.DS[c, :, t0:t0 + T], o_[:], R=[o_], W=[K.DS])
            wb, w = ring.load(K.w_in, win_cols(K, l, OFF["a"], 16), NKC, 16)
            for b in range(nb):
                pb = K.PS[3]
                for kc in range(NKC):
                    mm(K, pb, pb[:, 0:16], h, h[:, kc, b * 128:(b + 1) * 128], wb, w[:, kc, :],
                       start=(kc == 0), stop=(kc == NKC - 1))
                tt(K, "dve", e1[:], pb[:, 0:8], K.dtb[:, l, :], ALU.add, R=[pb, K.dtb], W=[e1])
                act(K, be[:, b, :], pb[:, 8:16], AF.Sigmoid, R=[pb], W=[be])
                act(K, e1[:], e1[:], AF.Exp, R=[e1], W=[e1])
                act(K, e1[:], e1[:], AF.Ln, R=[e1], W=[e1], bias=1.0)
                tt(K, "dve", gg[:, b, :], e1[:], K.nea[:, l, :], ALU.mult, R=[e1, K.nea], W=[gg])
            S.barrier()
        for hd in range(4):
            with ExitStack() as st:
                qh = sb(K, st, [128, T], BF16, "dqh")
                kh = sb(K, st, [128, T], BF16, "dkh")
                vv = sb(K, st, [128, T], BF16, "dvv")
                zs = sb(K, st, [128, T], BF16, "dzs")
                for buf, c in ((qh, hd), (kh, 4 + hd), (vv, 8 + hd), (zs, 12 + hd)):
                    S.dma(buf[:], K.DS[c, :, t0:t0 + T], R=[K.DS], W=[buf])
                oacc = sb(K, st, [128, T], F32, "doacc")
                S.op("pool", lambda e: e.memset(oacc[:], 0.0), W=[oacc])
                Sf = [sb(K, st, [128, 128], F32, "dS") for _ in range(2)]
                for d in range(2):
                    if kind == "S":
                        S.dma(Sf[d][:], K.state_delta[l, d, hd], R=[K.state_delta], W=[Sf[d]])
                    else:
                        S.op("pool", lambda e, d=d: e.memset(Sf[d][:], 0.0), W=[Sf[d]])

                def mk():
                    f = lambda sh, dt, nm: sb(K, st, sh, dt, nm)
                    return dict(g3=f([128, 3], F32, "g3"), g3T=f([3, 128], F32, "g3T"), sc=f([128, 4], F32, "dsc"),
                                dl=f([128, 2], F32, "ddl"), dmt=f([128, 128], F32, "dmt"), dmm=f([128, 128], F32, "dmm"),
                                tn=f([128, 128], F32, "dtn"), m1=f([128, 128], F32, "dm1"), aqk=f([128, 128], F32, "daqk"),
                                X=[f([128, 128], F32, "dX") for _ in range(2)], XT=[f([128, 128], F32, "dXT") for _ in range(2)],
                                R=[f([128, 128], F32, "dR") for _ in range(2)], bv=f([128, 128], F32, "dbv"),
                                kbg=f([128, 128], F32, "dkbg"), kd=f([128, 128], F32, "dkd"), u=f([128, 128], F32, "du"),
                                wT=f([128, 128], F32, "dwT"), egr=f([128, 128], F32, "degr"), qe=f([128, 128], F32, "dqe"),
                                vn=f([128, 128], F32, "dvn"))
                U = [mk(), mk()]
                for d in range(2):
                    S.op("pool", lambda e, d=d: e.memset(U[d]["vn"][:], 0.0), W=[U[d]["vn"]])
                for s_ in range(nb):
                    for d in range(2):
                        b = s_ if d == 0 else nb - 1 - s_
                        u = U[d]
                        pA, pB_, pC = K.PS[3 * d], K.PS[3 * d + 1], K.PS[3 * d + 2]
                        pT = K.PB
                        tk = slice(b * 128, (b + 1) * 128)
                        col = d * 4 + hd
                        gcol = gg[:, b, col:col + 1]
                        bcol = be[:, b, col:col + 1]
                        mm(K, pA, pA[:, 0:1], K.triF, K.triF[:, d, :], gg, gcol)
                        mm(K, pA, pA[:, 1:2], K.blk1, K.blk1[:], gg, gcol)
                        cp(K, "act", u["g3"][:, 0:1], pA[:, 0:1], R=[pA], W=[u["g3"]])
                        cp(K, "act", u["g3"][:, 2:3], pA[:, 1:2], R=[pA], W=[u["g3"]])
                        cp(K, "act", u["g3"][:, 1:2], bcol, R=[be], W=[u["g3"]])
                        act(K, u["sc"][:, 0:1], u["g3"][:, 0:1], AF.Exp, R=[u["g3"]], W=[u["sc"]])
                        tt(K, "dve", u["sc"][:, 1:2], u["sc"][:, 0:1], u["g3"][:, 1:2], ALU.mult, R=[u["sc"], u["g3"]], W=[u["sc"]])
                        act(K, u["sc"][:, 2:3], u["g3"][:, 0:1], AF.Exp, R=[u["g3"]], W=[u["sc"]], scale=-1.0, bias=u["g3"][:, 2:3])
                        tr(K, pA, pA[0:3, 2:130], u["g3"], u["g3"][:], K.identF, K.identF[:])
                        cp(K, "act", u["g3T"][:], pA[0:3, 2:130], R=[pA], W=[u["g3T"]])
                        for r in range(3):
                            mm(K, pA, pA[:, 128 * (r + 1):128 * (r + 2)], K.esel, K.esel[:, r, :], u["g3T"], u["g3T"][:])
                        Grow = pA[:, 128:256]
                        Brow = pA[:, 256:384]
                        Trow = pA[:, 384:512]
                        act(K, u["dl"][:], Trow[:, 0:65:64], AF.Exp, R=[pA], W=[u["dl"]])
                        act(K, u["egr"][:], Grow, AF.Exp, R=[pA], W=[u["egr"]])
                        ts(K, "dve", u["dmt"][:], Grow, u["g3"][:, 0:1], 0.0, ALU.subtract, ALU.min, R=[pA, u["g3"]], W=[u["dmt"]])
                        act(K, u["dmt"][:], u["dmt"][:], AF.Exp, R=[u["dmt"]], W=[u["dmt"]])
                        tt(K, "pool", u["dmm"][:], u["dmt"][:], K.triF[:, d, :], ALU.mult, R=[u["dmt"], K.triF], W=[u["dmm"]])
                        tt(K, "pool", u["tn"][:], u["dmm"][:], K.negoff[:], ALU.mult, R=[u["dmm"], K.negoff], W=[u["tn"]])
                        mm(K, pB_, pB_[:, 0:128], kh, kh[:, tk], kh, kh[:, tk])
                        mm(K, pB_, pB_[:, 128:256], kh, kh[:, tk], qh, qh[:, tk])
                        tt(K, "dve", u["aqk"][:], pB_[:, 128:256], u["dmm"][:], ALU.mult, R=[pB_, u["dmm"]], W=[u["aqk"]])
                        tt(K, "dve", u["m1"][:], pB_[:, 0:128], u["tn"][:], ALU.mult, R=[pB_, u["tn"]], W=[u["m1"]])
                        X, XT, R_ = u["X"], u["XT"], u["R"]
                        tt(K, "dve", X[0][:], Brow, u["m1"][:], ALU.mult, R=[pA, u["m1"]], W=[X[0]])
                        tt(K, "pool", R_[0][:], X[0][:], K.identF[:], ALU.add, R=[X[0], K.identF], W=[R_[0]])
                        tr(K, pC, pC[:, 384:512], X[0], X[0][:], K.identF, K.identF[:])
                        cp(K, "dve", XT[0][:], pC[:, 384:512], R=[pC], W=[XT[0]])
                        ci, ri = 0, 0
                        for k in range(1, 6):
                            ni = 1 - ci
                            if k < 5:
                                mm(K, pB_, pB_[:, 256:384], XT[ci], XT[ci][:], X[ci], X[ci][:])
                                cp(K, "dve", X[ni][:], pB_[:, 256:384], R=[pB_], W=[X[ni]])
                            mm(K, pB_, pB_[:, 384:512], X[ci], X[ci][:], XT[ci], XT[ci][:])
                            cp(K, "dve", XT[ni][:], pB_[:, 384:512], R=[pB_], W=[XT[ni]])
                            mm(K, pB_, pB_[:, 0:128], XT[ni], XT[ni][:], R_[ri], R_[ri][:])
                            tt(K, "dve", R_[1 - ri][:], pB_[:, 0:128], R_[ri][:], ALU.add, R=[pB_, R_[ri]], W=[R_[1 - ri]])
                            ci, ri = ni, 1 - ri
                        Rf = R_[ri]
                        tr(K, pT, pT[:, 128:256], kh, kh[:, tk], K.identB, K.identB[:])
                        tr(K, pT, pT[:, 256:384], vv, vv[:, tk], K.identB, K.identB[:])
                        act(K, u["kbg"][:], pT[:, 128:256], AF.Copy, R=[pT, u["sc"]], W=[u["kbg"]], scale=u["sc"][:, 1:2])
                        act(K, u["kd"][:], pT[:, 128:256], AF.Copy, R=[pT, u["sc"]], W=[u["kd"]], scale=u["sc"][:, 2:3])
                        act(K, u["bv"][:], pT[:, 256:384], AF.Copy, R=[pT, u["g3"]], W=[u["bv"]], scale=u["g3"][:, 1:2])
                        mm(K, pB_, pB_[:, 128:256], Rf, Rf[:], u["bv"], u["bv"][:])
                        cp(K, "dve", u["u"][:], pB_[:, 128:256], R=[pB_], W=[u["u"]])
                        mm(K, pB_, pB_[:, 256:384], u["kbg"], u["kbg"][:], Rf, Rf[:])
                        cp(K, "dve", u["wT"][:], pB_[:, 256:384], R=[pB_], W=[u["wT"]])
                        tt(K, "dve", u["qe"][:], qh[:, tk], u["egr"][:], ALU.mult, R=[qh, u["egr"]], W=[u["qe"]])
                        for c in ((0, 1) if d == 0 else (1, 0)):
                            cs_ = slice(c * 64, (c + 1) * 64)
                            mm(K, pC, pC[:, 0:128], u["wT"], u["wT"][:], Sf[d], Sf[d][:])
                            tt(K, "dve", u["vn"][cs_, :], u["u"][cs_, :], pC[cs_, 0:128], ALU.subtract, R=[u["u"], pC], W=[u["vn"]])
                            mm(K, pC, pC[:, 128:192], Sf[d], Sf[d][:], u["qe"], u["qe"][:, cs_], start=True, stop=False)
                            mm(K, pC, pC[:, 128:192], u["vn"], u["vn"][cs_, :], u["aqk"], u["aqk"][cs_, cs_], start=False, stop=True)
                            ot = oacc[:, b * 128 + c * 64:b * 128 + (c + 1) * 64]
                            tt(K, "dve", ot, pC[:, 128:192], ot, ALU.add, R=[pC, oacc], W=[oacc])
                            mm(K, pC, pC[:, 256:384], u["kd"], u["kd"][cs_, :], u["vn"], u["vn"][cs_, :])
                            stt(K, "dve", Sf[d][:], Sf[d][:], u["dl"][:, c:c + 1], pC[:, 256:384], ALU.mult, ALU.add,
                                R=[Sf[d], u["dl"], pC], W=[Sf[d]])
                if kind == "P":
                    for d in range(2):
                        S.dma(K.nsd[idx, l, d, hd], Sf[d][:], R=[Sf[d]], W=[K.nsd])
                final_gate_norm(K, l, seq, oacc, zs, zs[:, :], K.dnT[:, l:l + 1], 12 + hd, st)
                S.barrier()


CFG = dict(T_S=2048, NP=2, T_P=256, L=4)


def kernel(**inputs):
    inp = {k: np.asarray(v) for k, v in inputs.items()}
    cfg = dict(CFG)
    L = cfg["L"]
    nc, K = build_program(cfg)
    common = host_common(inp, L)
    common.update(host_mix_common(inp, cfg))
    in_maps = []
    for c in range(8):
        pl = [2 * c, 2 * c + 1]
        d = host_core(inp, common, c // 2, pl, L)
        d.update(host_mix_core(inp, c // 2, pl, cfg))
        in_maps.append(d)
    res = run_bass_kernel_spmd(nc, in_maps, core_ids=list(range(8)))
    r = res.results
    f = np.float32
    y_prompt = np.stack([r[p // 2]["y_p"].reshape(2, 256, D)[p % 2] for p in range(16)]).astype(f)
    y_sample = np.stack([r[2 * b]["y_s"] for b in range(4)]).astype(f)
    nk = np.concatenate([r[c]["nk"].reshape(2, L, 256, 2, 128) for c in range(8)]).astype(f)
    nv = np.concatenate([r[c]["nv"].reshape(2, L, 256, 2, 128) for c in range(8)]).astype(f)
    nsd = np.concatenate([r[c]["nsd"] for c in range(8)]).astype(f)
    nsg = np.concatenate([r[c]["nsg"] for c in range(8)]).astype(f)
    return (y_prompt, y_sample, nk, nv, nsd, nsg)
```

```python
import numpy as np
from contextlib import ExitStack
import concourse.bass as bass
import concourse.mybir as mybir
from concourse.bass_utils import run_bass_kernel_spmd

F32 = mybir.dt.float32
BF16 = mybir.dt.bfloat16
AF = mybir.ActivationFunctionType
ALU = mybir.AluOpType
AX = mybir.AxisListType

COMPUTE = ("pe", "act", "dve", "pool")
SEM_EPOCH = 30000


class Buf:
    __slots__ = ("t", "lw", "rd", "name", "excl")

    def __init__(self, t, name="", excl=False):
        self.excl = excl
        self.t = t
        self.lw = None
        self.rd = {}
        self.name = name

    def __getitem__(self, idx):
        return self.t[idx]


class Sched:
    def __init__(self, nc, stack, n_dma_sems=16):
        self.nc = nc
        self.stack = stack
        self.q = {e: [] for e in COMPUTE + ("sp",)}
        self.cnt = {e: 0 for e in COMPUTE}
        self.sems = {}
        self.own = {e: set() for e in COMPUTE}
        self.semobj = {}
        self.nsem = 0
        for e in COMPUTE:
            self._new_epoch(e)
        self.dsem = {q: [self._mksem("d%s%d" % (q, i)) for i in range(n_dma_sems)] for q in ("sp", "pool")}
        self.dval = {q: [0] * n_dma_sems for q in ("sp", "pool")}
        self.drr = {"sp": 0, "pool": 0}
        self.waited = {e: {} for e in COMPUTE + ("sp",)}
        self.n_ops = 0
        self.last_dma_toks = {}
        self.last_tok = {}

    def _mksem(self, name):
        s = self.stack.enter_context(self.nc.semaphore(name))
        k = self.nsem
        self.nsem += 1
        self.semobj[k] = s
        return k

    def _new_epoch(self, e):
        self.sems[e] = self._mksem("s_%s_%d" % (e, self.nsem))
        self.own[e].add(self.sems[e])
        self.cnt[e] = 0

    def _collect(self, eng, R, W):
        deps = {}

        def add(tok):
            if tok is None:
                return
            k, v = tok
            if deps.get(k, 0) < v:
                deps[k] = v
        for b in R:
            add(b.lw)
            if b.excl:
                mine = self.own.get(eng, ())
                for t in b.rd.items():
                    if t[0] not in mine:
                        add(t)
        for b in W:
            add(b.lw)
            for t in b.rd.items():
                add(t)
        waits = []
        wd = self.waited[eng]
        own = self.own["pe"] if eng == "pe" else ()
        for k, v in deps.items():
            if k in own:
                continue
            if wd.get(k, 0) < v:
                wd[k] = v
                waits.append((k, v))
        return waits

    def _commit(self, tok, R, W):
        for b in R:
            if b.rd.get(tok[0], 0) < tok[1]:
                b.rd[tok[0]] = tok[1]
        for b in W:
            b.lw = tok
            b.rd = {}

    def _eng(self, e):
        nc = self.nc
        return {"pe": nc.tensor, "act": nc.scalar, "dve": nc.vector, "pool": nc.gpsimd, "sp": nc.sync}[e]

    def _issue(self, eng, waits, fn, inc):
        e = self._eng(eng)
        for k, v in waits:
            e.wait_ge(self.semobj[k], v)
        if fn is not None:
            ins = fn(e)
            ins.then_inc(self.semobj[inc[0]], inc[1])

    def op(self, eng, fn, R=(), W=()):
        waits = self._collect(eng, R, W)
        if self.cnt[eng] >= SEM_EPOCH:
            self._new_epoch(eng)
        self.cnt[eng] += 1
        tok = (self.sems[eng], self.cnt[eng])
        self.last_tok[eng] = tok
        self._issue(eng, waits, fn, (tok[0], 1))
        self._commit(tok, R, W)
        self.n_ops += 1
        return tok

    def dma(self, out_ap, in_ap, R=(), W=(), q="sp"):
        s = self.drr[q]
        self.drr[q] = (s + 1) % len(self.dsem[q])
        k = self.dsem[q][s]
        dv = self.dval[q]
        waits = self._collect(q, R, W)
        wd = self.waited[q]
        if dv[s] > 0 and wd.get(k, 0) < dv[s]:
            wd[k] = dv[s]
            waits.append((k, dv[s]))
        dv[s] += 16
        tok = (k, dv[s])
        self._issue(q, waits, lambda e: e.dma_start(out=out_ap, in_=in_ap), (k, 16))
        self._commit(tok, R, W)
        self.n_ops += 1
        self.last_dma_toks[k] = tok
        return tok

    def wait_tokens(self, toks, eng="sp"):
        waits = []
        wd = self.waited[eng]
        for k, v in toks:
            if wd.get(k, 0) < v:
                wd[k] = v
                waits.append((k, v))
        self._issue(eng, waits, None, None)

    def barrier(self):
        toks = [self.last_tok[e] for e in COMPUTE if e in self.last_tok]
        for q in ("sp", "pool"):
            toks += [(self.dsem[q][i], self.dval[q][i]) for i in range(len(self.dsem[q])) if self.dval[q][i] > 0]
        for e in COMPUTE + ("sp",):
            self.wait_tokens(toks, e)

    def final_wait(self, toks=None, eng="sp"):
        self.barrier()


D = 2048
NKC = 16
FH = 5632
NHC = 44
INW = 13872
EPS = 1e-6
OFF = dict(pool=0, q_at=512, k_at=1536, v_at=1792, qkv=2048, z=3584, a=4096, b=4104,
           q_gl=4112, k_gl=4368, v_gl=4624, r_gl=5136, lr=5648, gate=5680)
NYC = 20
SLOT = 8192


class Ctx:
    pass


def sb(K, st, shape, dt, name=None):
    K.uid += 1
    return Buf(st.enter_context(K.nc.sbuf_tensor("%s_%d" % (name or "t", K.uid), list(shape), dt)), name or "t")


def mm(K, pb, out_ap, lb, l_ap, rb, r_ap, start=True, stop=True):
    K.S.op("pe", lambda e: e.matmul(out_ap, l_ap, r_ap, start=start, stop=stop), R=[lb, rb], W=[pb])


def tr(K, pb, out_ap, ib, in_ap, idb, id_ap):
    K.S.op("pe", lambda e: e.transpose(out_ap, in_ap, id_ap), R=[ib, idb], W=[pb])


def act(K, out_ap, in_ap, func, R, W, bias=None, scale=None):
    kw = {}
    if bias is not None:
        kw["bias"] = bias
    if scale is not None:
        kw["scale"] = scale
    K.S.op("act", lambda e: e.activation(out=out_ap, in_=in_ap, func=func, **kw), R=R, W=W)


def tt(K, eng, out_ap, a_ap, b_ap, op, R, W):
    K.S.op(eng, lambda e: e.tensor_tensor(out=out_ap, in0=a_ap, in1=b_ap, op=op), R=R, W=W)


def ts(K, eng, out_ap, a_ap, s1, s2, op0, op1, R, W):
    if s2 is None:
        K.S.op(eng, lambda e: e.tensor_scalar(out=out_ap, in0=a_ap, scalar1=s1, scalar2=None, op0=op0), R=R, W=W)
    else:
        K.S.op(eng, lambda e: e.tensor_scalar(out=out_ap, in0=a_ap, scalar1=s1, scalar2=s2, op0=op0, op1=op1), R=R, W=W)


def stt(K, eng, out_ap, a_ap, s, b_ap, op0, op1, R, W):
    K.S.op(eng, lambda e: e.scalar_tensor_tensor(out=out_ap, in0=a_ap, scalar=s, in1=b_ap, op0=op0, op1=op1), R=R, W=W)


def rsqrt(K, ob, out_ap, ib, in_ap, scale):
    ts(K, "dve", out_ap, in_ap, scale, EPS, ALU.mult, ALU.add, R=[ib], W=[ob])
    act(K, out_ap, out_ap, AF.Sqrt, R=[ob], W=[ob])
    K.S.op("dve", lambda e: e.reciprocal(out=out_ap, in_=out_ap), R=[ob], W=[ob])


def cp(K, eng, out_ap, in_ap, R, W):
    if eng == "act":
        K.S.op("act", lambda e: e.activation(out=out_ap, in_=in_ap, func=AF.Copy), R=R, W=W)
    else:
        K.S.op(eng, lambda e: e.tensor_copy(out=out_ap, in_=in_ap), R=R, W=W)


def wslot(K, i, k, c):
    return K.ring[i][:, 0:k * c].rearrange("p (k c) -> p k c", k=k)


class Ring:
    def __init__(self, K, plan, slots=None, group=1):
        self.K = K
        self.slots = slots if slots is not None else K.ring
        self.i = 0
        self.plan = list(plan)
        self.issued = []
        self.inuse = []
        self.group = group
        self.nxt = 0
        self._fill()

    def _fill(self):
        while self.nxt < len(self.plan) and len(self.issued) + len(self.inuse) < len(self.slots):
            wbuf, w_ap, k, c = self.plan[self.nxt]
            self.nxt += 1
            slot = self.slots[self.i]
            self.i = (self.i + 1) % len(self.slots)
            v = slot[:, 0:k * c].rearrange("p (k c) -> p k c", k=k)
            self.K.S.dma(v, w_ap, R=[wbuf], W=[slot], q="pool")
            self.issued.append((slot, v, k, c))

    def load(self, wbuf=None, w_ap=None, k=None, c=None):
        if len(self.inuse) == self.group:
            self.inuse.pop(0)
        self._fill()
        slot, v, k_, c_ = self.issued.pop(0)
        assert (k is None or (k, c) == (k_, c_)), ((k, c), (k_, c_))
        self.inuse.append(slot)
        return slot, v


def tiles_of(K):
    out = []
    t = 0
    while t < K.T_S:
        n = min(512, K.T_S - t)
        out.append((t, n, 0))
        t += n
    tot = K.NT
    while t < tot:
        n = min(512, tot - t)
        out.append((t, n, 1))
        t += n
    return out


def supertiles(K, maxtok=1024):
    out, cur, tot = [], [], 0
    for tl in tiles_of(K):
        if tot + tl[1] > maxtok and cur:
            out.append(cur)
            cur, tot = [], 0
        cur.append(tl)
        tot += tl[1]
    if cur:
        out.append(cur)
    return out


def phase_in(K):
    S = K.S
    with ExitStack() as st:
        xin = [sb(K, st, [128, D], F32, "xin") for _ in range(2)]
        xT = [sb(K, st, [128, NKC, 128], F32, "xT") for _ in range(2)]
        for blk in range(K.NT // 128):
            t0 = blk * 128
            if t0 < K.T_S:
                src, r0 = K.x_s, t0
            else:
                src, r0 = K.x_p, t0 - K.T_S
            xi = xin[blk % 2]
            xo = xT[blk % 2]
            S.dma(xi[:], src[r0:r0 + 128, :], R=[src], W=[xi])
            for g in range(4):
                pb = K.PS[g % 4]
                for c in range(4):
                    cc = g * 4 + c
                    tr(K, pb, pb[:, c * 128:(c + 1) * 128], xi, xi[:, cc * 128:(cc + 1) * 128], K.identF, K.identF[:])
                cp(K, "act" if g % 2 else "dve", xo[:, 4 * g:4 * g + 4, :],
                   pb[:, :].rearrange("p (c t) -> p c t", c=4), R=[pb], W=[xo])
            S.dma(K.XS[:, :, t0:t0 + 128].rearrange("c p t -> p c t"), xo[:], R=[xo], W=K.XSB)
        S.barrier()


def phase_mod(K, l):
    S = K.S
    with ExitStack() as st:
        ring = Ring(K, [(K.w_mod, K.w_mod[l, :, cb * 512:(cb + 1) * 512].rearrange("(k p) c -> p k c", p=128), NKC, 512)
                        for cb in range(24)])
        pb = K.PS[6]
        for cb in range(24):
            wb, w = ring.load(K.w_mod, K.w_mod[l, :, cb * 512:(cb + 1) * 512].rearrange("(k p) c -> p k c", p=128), NKC, 512)
            for m in range(4):
                j = cb * 4 + m
                for kc in range(NKC):
                    mm(K, pb, pb[:, j:j + 97:96], wb, w[:, kc, m * 128:(m + 1) * 128], K.sc, K.sc[:, kc, :],
                       start=(kc == 0), stop=(kc == NKC - 1))
        for s in range(2):
            tt(K, "dve", K.mod[:, s, :], pb[:, s * 96:(s + 1) * 96], K.bmodT[:, l, :], ALU.add,
               R=[pb, K.bmodT], W=[K.mod])
        for s in range(2):
            stt(K, "dve", K.A1[:, s, :], K.mod[:, s, 16:32], 1.0, K.norm1T[:, l, :], ALU.add, ALU.mult,
                R=[K.mod, K.norm1T], W=[K.A1])
            stt(K, "dve", K.A2[:, s, :], K.mod[:, s, 64:80], 1.0, K.norm2T[:, l, :], ALU.add, ALU.mult,
                R=[K.mod, K.norm2T], W=[K.A2])
        S.barrier()


def phase_norm(K, l, which):
    S = K.S
    with ExitStack() as st:
        xb = [sb(K, st, [128, NKC, 512], F32, "nx") for _ in range(2)]
        sq = sb(K, st, [128, NKC, 512], BF16, "nsq")
        rr = sb(K, st, [128, 512], F32, "nr")
        tmp = [sb(K, st, [128, 512], F32, "ntmp") for _ in range(2)]
        if which == "f":
            hf = sb(K, st, [128, NKC, 512], F32, "nhf")
            yo = [sb(K, st, [128, D], F32, "nyo") for _ in range(2)]
        else:
            hb = [sb(K, st, [128, NKC, 512], BF16, "nh") for _ in range(2)]
        for ti, (t0, n, s) in enumerate(tiles_of(K)):
            x = xb[ti % 2]
            S.dma(x[:, :, :n], K.XS[:, :, t0:t0 + n].rearrange("c p t -> p c t"), R=K.XSB, W=[x])
            for g in range(4):
                act(K, sq[:, 4 * g:4 * g + 4, :n], x[:, 4 * g:4 * g + 4, :n], AF.Square, R=[x], W=[sq])
            pb = K.PS[4 + ti % 2]
            for c in range(NKC):
                mm(K, pb, pb[:, :n], K.onesB, K.onesB[:], sq, sq[:, c, :n], start=(c == 0), stop=(c == NKC - 1))
            rsqrt(K, rr, rr[:, :n], pb, pb[:, :n], 1.0 / D)
            if which == "f":
                for c in range(NKC):
                    stt(K, "dve", hf[:, c, :n], x[:, c, :n], K.normfT[:, c:c + 1], rr[:, :n],
                        ALU.mult, ALU.mult, R=[x, rr, K.normfT], W=[hf])
                for b in range(n // 128):
                    y = yo[b % 2]
                    for g in range(4):
                        pb2 = K.PS[g % 4]
                        for c in range(4):
                            cc = 4 * g + c
                            tr(K, pb2, pb2[:, c * 128:(c + 1) * 128], hf, hf[:, cc, b * 128:(b + 1) * 128],
                               K.identF, K.identF[:])
                        cp(K, "act" if g % 2 else "dve", y[:, g * 512:(g + 1) * 512], pb2[:, :], R=[pb2], W=[y])
                    tg = t0 + b * 128
                    if tg < K.T_S:
                        S.dma(K.y_s[tg:tg + 128, :], y[:], R=[y], W=[K.y_s])
                    else:
                        S.dma(K.y_p[tg - K.T_S:tg - K.T_S + 128, :], y[:], R=[y], W=[K.y_p])
            else:
                A = K.A1 if which == 1 else K.A2
                bo = 0 if which == 1 else 48
                h = hb[ti % 2]
                for c in range(NKC):
                    tm = tmp[c % 2]
                    tt(K, "dve", tm[:, :n], x[:, c, :n], rr[:, :n], ALU.mult, R=[x, rr], W=[tm])
                    act(K, h[:, c, :n], tm[:, :n], AF.Identity, R=[tm, A, K.mod], W=[h],
                        scale=A[:, s, c:c + 1], bias=K.mod[:, s, bo + c:bo + c + 1])
                S.dma(K.HD[:, :, t0:t0 + n].rearrange("c p t -> p c t"), h[:, :, :n], R=[h], W=[K.HD])
        S.barrier()


def phase_merge(K, l):
    S = K.S
    brs = [(K.w_br_pool, 0, 4), (K.w_br_attn, 4, 8), (K.w_br_delta, 12, 4), (K.w_br_gla, 16, 4)]
    steps = [("m", mgp, bi) for mgp in range(4) for bi in range(4)] + [("o", mgp, 0) for mgp in range(4)]

    def views(i):
        kind, mgp, bi = steps[i]
        slot = K.ring2[i % 2]
        gw = slot[:, 0:8192].rearrange("p (k c) -> p k c", k=NKC)
        if kind == "m":
            nk = brs[bi][2]
            bw = slot[:, 8192:8192 + nk * 512].rearrange("p (k c) -> p k c", k=nk)
        else:
            bw = None
        return slot, gw, bw

    def load(i):
        kind, mgp, bi = steps[i]
        slot, gw, bw = views(i)
        if kind == "m":
            wbr = brs[bi][0]
            c0 = OFF["gate"] + bi * D + mgp * 512
            S.dma(gw, K.w_in[l, :, c0:c0 + 512].rearrange("(k p) c -> p k c", p=128), R=[K.w_in], W=[slot], q="pool")
            S.dma(bw, wbr[l, :, mgp * 512:(mgp + 1) * 512].rearrange("(k p) c -> p k c", p=128), R=[wbr], W=[slot], q="pool")
        else:
            S.dma(gw, K.w_out[l, :, mgp * 512:(mgp + 1) * 512].rearrange("(k p) c -> p k c", p=128), R=[K.w_out], W=[slot], q="pool")

    for stl in supertiles(K):
        T0 = stl[0][0]
        TN = sum(t[1] for t in stl)
        with ExitStack() as st:
            h = sb(K, st, [128, NKC, TN], BF16, "mh")
            y = sb(K, st, [128, NYC, TN], BF16, "my")
            mg = sb(K, st, [128, NKC, TN], BF16, "mmg")
            acc = sb(K, st, [128, 4, TN], F32, "macc")
            sg = [sb(K, st, [128, 512], F32, "msg") for _ in range(2)]
            t2 = [sb(K, st, [128, 512], F32, "mt2") for _ in range(2)]
            xt = [sb(K, st, [128, 512], F32, "mxt") for _ in range(2)]
            load(0)
            S.dma(h[:], K.HD[:, :, T0:T0 + TN].rearrange("c p t -> p c t"), R=[K.HD], W=[h])
            S.dma(y[:], K.YD[:, :, T0:T0 + TN].rearrange("c p t -> p c t"), R=K.YDB, W=[y])
            cnt = 0
            for i, (kind, mgp, bi) in enumerate(steps):
                if i + 1 < len(steps):
                    load(i + 1)
                slot, gw, bw = views(i)
                if kind == "m":
                    wbr, yc0, nk = brs[bi]
                    for m in range(4):
                        for (t0, n, s) in stl:
                            o = t0 - T0
                            pg = K.PS[cnt % 2]
                            pbr = K.PS[2 + cnt % 2]
                            for kc in range(NKC):
                                mm(K, pg, pg[:, :n], slot, gw[:, kc, m * 128:(m + 1) * 128], h, h[:, kc, o:o + n],
                                   start=(kc == 0), stop=(kc == NKC - 1))
                            for kc in range(nk):
                                mm(K, pbr, pbr[:, :n], slot, bw[:, kc, m * 128:(m + 1) * 128], y, y[:, yc0 + kc, o:o + n],
                                   start=(kc == 0), stop=(kc == nk - 1))
                            sgt = sg[cnt % 2]
                            act(K, sgt[:, :n], pg[:, :n], AF.Sigmoid, R=[pg], W=[sgt])
                            if bi == 0:
                                tt(K, "dve", acc[:, m, o:o + n], pbr[:, :n], sgt[:, :n], ALU.mult, R=[pbr, sgt], W=[acc])
                            else:
                                tq = t2[cnt % 2]
                                tt(K, "dve", tq[:, :n], pbr[:, :n], sgt[:, :n], ALU.mult, R=[pbr, sgt], W=[tq])
                                if bi < 3:
                                    tt(K, "dve", acc[:, m, o:o + n], acc[:, m, o:o + n], tq[:, :n], ALU.add,
                                       R=[acc, tq], W=[acc])
                                else:
                                    tt(K, "dve", mg[:, mgp * 4 + m, o:o + n], acc[:, m, o:o + n], tq[:, :n], ALU.add,
                                       R=[acc, tq], W=[mg])
                            cnt += 1
                else:
                    for m in range(4):
                        mo = mgp * 4 + m
                        for (t0, n, s) in stl:
                            o = t0 - T0
                            po = K.PS[4 + cnt % 2]
                            x = xt[cnt % 2]
                            S.dma(x[:, :n], K.XS[mo, :, t0:t0 + n], R=[K.XSB[mo]], W=[x])
                            for kc in range(NKC):
                                mm(K, po, po[:, :n], slot, gw[:, kc, m * 128:(m + 1) * 128], mg, mg[:, kc, o:o + n],
                                   start=(kc == 0), stop=(kc == NKC - 1))
                            stt(K, "dve", x[:, :n], po[:, :n], K.mod[:, s, 32 + mo:33 + mo], x[:, :n], ALU.mult, ALU.add,
                                R=[po, x, K.mod], W=[x])
                            S.dma(K.XS[mo, :, t0:t0 + n], x[:, :n], R=[x], W=[K.XSB[mo]])
                            cnt += 1
            S.barrier()


def phase_ffn(K, l):
    S = K.S
    wv = lambda w, r0, r1, c0, c1: w[l, r0:r1, c0:c1].rearrange("(k p) c -> p k c", p=128)
    planA = []
    for hp in range(NHC // 2):
        planA.append((K.w_gate, wv(K.w_gate, 0, D, hp * 256, (hp + 1) * 256), NKC, 256))
        planA.append((K.w_up, wv(K.w_up, 0, D, hp * 256, (hp + 1) * 256), NKC, 256))
    planB = []
    for mg2 in range(8):
        planB.append((K.w_down, wv(K.w_down, 0, 22 * 128, mg2 * 256, (mg2 + 1) * 256), 22, 256))
        planB.append((K.w_down, wv(K.w_down, 22 * 128, 44 * 128, mg2 * 256, (mg2 + 1) * 256), 22, 256))
    for stl in supertiles(K):
        T0 = stl[0][0]
        TN = sum(t[1] for t in stl)
        with ExitStack() as st:
            h = sb(K, st, [128, NKC, TN], BF16, "fh")
            a = sb(K, st, [128, NHC, TN], BF16, "fa")
            sg = [sb(K, st, [128, 512], F32, "fsg") for _ in range(2)]
            xt = [sb(K, st, [128, 512], F32, "fxt") for _ in range(2)]
            ring = Ring(K, planA, slots=K.ring6, group=2)
            S.dma(h[:], K.HD[:, :, T0:T0 + TN].rearrange("c p t -> p c t"), R=[K.HD], W=[h])
            cnt = 0
            for hp in range(NHC // 2):
                gb, gw = ring.load()
                ub, uw = ring.load()
                for m in range(2):
                    hc = hp * 2 + m
                    for (t0, n, s) in stl:
                        o = t0 - T0
                        pg = K.PS[cnt % 2]
                        pu = K.PS[2 + cnt % 2]
                        for kc in range(NKC):
                            mm(K, pg, pg[:, :n], gb, gw[:, kc, m * 128:(m + 1) * 128], h, h[:, kc, o:o + n],
                               start=(kc == 0), stop=(kc == NKC - 1))
                        for kc in range(NKC):
                            mm(K, pu, pu[:, :n], ub, uw[:, kc, m * 128:(m + 1) * 128], h, h[:, kc, o:o + n],
                               start=(kc == 0), stop=(kc == NKC - 1))
                        sgt = sg[cnt % 2]
                        act(K, sgt[:, :n], pg[:, :n], AF.Silu, R=[pg], W=[sgt])
                        tt(K, "dve", a[:, hc, o:o + n], pu[:, :n], sgt[:, :n], ALU.mult, R=[pu, sgt], W=[a])
                        cnt += 1
            S.barrier()
            ring = Ring(K, planB, slots=K.ring4, group=2)
            for mg2 in range(8):
                d0b, d0w = ring.load()
                d1b, d1w = ring.load()
                for m in range(2):
                    mo = mg2 * 2 + m
                    for (t0, n, s) in stl:
                        o = t0 - T0
                        po = K.PS[4 + cnt % 2]
                        x = xt[cnt % 2]
                        S.dma(x[:, :n], K.XS[mo, :, t0:t0 + n], R=[K.XSB[mo]], W=[x])
                        for kc in range(NHC):
                            db, dw = (d0b, d0w) if kc < 22 else (d1b, d1w)
                            mm(K, po, po[:, :n], db, dw[:, kc % 22, m * 128:(m + 1) * 128], a, a[:, kc, o:o + n],
                               start=(kc == 0), stop=(kc == NHC - 1))
                        stt(K, "dve", x[:, :n], po[:, :n], K.mod[:, s, 80 + mo:81 + mo], x[:, :n], ALU.mult, ALU.add,
                            R=[po, x, K.mod], W=[x])
                        S.dma(K.XS[mo, :, t0:t0 + n], x[:, :n], R=[x], W=[K.XSB[mo]])
                        cnt += 1
            S.barrier()


def phase_zero_y(K):
    S = K.S
    with ExitStack() as st:
        z = sb(K, st, [128, NYC, 512], BF16, "zy")
        S.op("dve", lambda e: e.memset(z[:], 0.0), W=[z])
        for (t0, n, s) in tiles_of(K):
            S.dma(K.YD[:, :, t0:t0 + n].rearrange("c p t -> p c t"), z[:, :, :n], R=[z], W=K.YDB)
        S.barrier()


def build_program(cfg):
    nc = bass.Bass("TRN2", target_bir_lowering=False)
    K = Ctx()
    K.nc = nc
    K.uid = 0
    K.cfg = cfg
    K.T_S, K.NP, K.T_P, K.L = cfg["T_S"], cfg["NP"], cfg["T_P"], cfg["L"]
    K.NT = K.T_S + K.NP * K.T_P
    L = K.L

    def ein(name, shape, dt=F32):
        b = Buf(nc.dram_tensor(name, list(shape), dt, kind="ExternalInput").ap(), name)
        setattr(K, name, b)
        return b

    def eout(name, shape):
        b = Buf(nc.dram_tensor(name, list(shape), F32, kind="ExternalOutput").ap(), name)
        setattr(K, name, b)
        return b

    def scratch(name, shape, dt):
        b = Buf(nc.dram_tensor(name, list(shape), dt).ap(), name)
        setattr(K, name, b)
        return b

    ein("x_s", [K.T_S, D])
    ein("x_p", [K.NP * K.T_P, D])
    ein("condT", [128, NKC, 2])
    ein("bmodT_d", [128, L, 96])
    ein("norm1T_d", [128, L, NKC])
    ein("norm2T_d", [128, L, NKC])
    ein("normfT_d", [128, NKC])
    ein("identF_d", [128, 128])
    ein("w_mod", [L, D, 6 * D])
    ein("w_in", [L, D, INW])
    ein("w_br_pool", [L, 512, D])
    ein("w_br_attn", [L, 1024, D])
    ein("w_br_delta", [L, 512, D])
    ein("w_br_gla", [L, 512, D])
    ein("w_out", [L, D, D])
    ein("w_gate", [L, D, FH])
    ein("w_up", [L, D, FH])
    ein("w_down", [L, FH, D])
    ein("pool_w", [L, 4, 128, 128])
    ein("pscT_d", [128, L, 4])
    ein("rcnt_S", [4, K.T_S])
    ein("rcnt_P", [4, K.T_P])
    ein("cos_d", [128, K.T_S])
    ein("sin_d", [128, K.T_S])
    ein("RT_d", [128, 128])
    ein("maskL_d", [128, 4, 128])
    ein("maskR_d", [128, 4, 128])
    ein("sink_rep", [L, 1, 1024])
    ein("cache_k", [L, 256, 256])
    ein("cache_v", [L, 256, 256])
    ein("state_delta", [L, 2, 4, 128, 128])
    ein("state_gla", [L, 2, 4, 64, 128])
    mix_inputs(K, ein)
    eout("nk", [K.NP, L, K.T_P, 256])
    eout("nv", [K.NP, L, K.T_P, 256])
    eout("nsd", [K.NP, L, 2, 4, 128, 128])
    eout("nsg", [K.NP, L, 2, 4, 64, 128])
    eout("y_s", [K.T_S, D])
    eout("y_p", [K.NP * K.T_P, D])
    scratch("XS", [NKC, 128, K.NT], F32)
    scratch("HD", [NKC, 128, K.NT], BF16)
    scratch("YD", [NYC, 128, K.NT], BF16)
    K.YDB = [Buf(K.YD.t, "yd%d" % i) for i in range(NYC)]
    K.XSB = [Buf(K.XS.t, "xs%d" % i) for i in range(NKC)]

    with ExitStack() as top:
        S = K.S = Sched(nc, top)
        K.PS = [Buf(top.enter_context(nc.psum_tensor("ps%d" % i, [128, 512], F32)), "ps%d" % i, excl=True) for i in range(7)]
        K.PSQ = [[Buf(K.PS[i].t, "ps%dq%d" % (i, q)) for q in range(4)] for i in range(7)]
        K.PB = Buf(top.enter_context(nc.psum_tensor("psb", [128, 1024], BF16)), "psb", excl=True)
        K.identF = sb(K, top, [128, 128], F32, "identF")
        K.onesB = sb(K, top, [128, 128], BF16, "onesB")
        K.sc = sb(K, top, [128, NKC, 2], BF16, "sc")
        K.mod = sb(K, top, [128, 2, 96], F32, "mod")
        K.A1 = sb(K, top, [128, 2, NKC], F32, "A1")
        K.A2 = sb(K, top, [128, 2, NKC], F32, "A2")
        K.bmodT = sb(K, top, [128, L, 96], F32, "bmodT")
        K.norm1T = sb(K, top, [128, L, NKC], F32, "norm1T")
        K.norm2T = sb(K, top, [128, L, NKC], F32, "norm2T")
        K.normfT = sb(K, top, [128, NKC], F32, "normfT")
        ringmem = sb(K, top, [128, 3 * SLOT], BF16, "ringmem")
        K.ring = [Buf(ringmem.t[:, i * SLOT:(i + 1) * SLOT], "ring%d" % i) for i in range(3)]
        K.ring2 = [Buf(ringmem.t[:, i * 12288:(i + 1) * 12288], "ringb%d" % i) for i in range(2)]
        K.ring6 = [Buf(ringmem.t[:, i * 4096:(i + 1) * 4096], "ringc%d" % i) for i in range(6)]
        K.ring4 = [Buf(ringmem.t[:, i * 6144:(i + 1) * 6144], "ringd%d" % i) for i in range(4)]
        K.pscT = sb(K, top, [128, L, 4], F32, "pscT")
        K.RT = sb(K, top, [128, 128], BF16, "RT")
        K.maskL = sb(K, top, [128, 4, 128], BF16, "maskL")
        K.maskR = sb(K, top, [128, 4, 128], BF16, "maskR")
        S.dma(K.pscT[:], K.pscT_d[:, :, :], R=[K.pscT_d], W=[K.pscT])
        S.dma(K.RT[:], K.RT_d[:, :], R=[K.RT_d], W=[K.RT], q="pool")
        S.dma(K.maskL[:], K.maskL_d[:, :, :], R=[K.maskL_d], W=[K.maskL], q="pool")
        S.dma(K.maskR[:], K.maskR_d[:, :, :], R=[K.maskR_d], W=[K.maskR], q="pool")
        mix_consts(K, top)
        cT = sb(K, top, [128, NKC, 2], F32, "cT")
        S.dma(K.identF[:], K.identF_d[:, :], R=[K.identF_d], W=[K.identF])
        S.dma(cT[:], K.condT[:, :, :], R=[K.condT], W=[cT])
        S.dma(K.bmodT[:], K.bmodT_d[:, :, :], R=[K.bmodT_d], W=[K.bmodT])
        S.dma(K.norm1T[:], K.norm1T_d[:, :, :], R=[K.norm1T_d], W=[K.norm1T])
        S.dma(K.norm2T[:], K.norm2T_d[:, :, :], R=[K.norm2T_d], W=[K.norm2T])
        S.dma(K.normfT[:], K.normfT_d[:, :], R=[K.normfT_d], W=[K.normfT])
        S.op("dve", lambda e: e.memset(K.onesB[:], 1.0), W=[K.onesB])
        act(K, K.sc[:], cT[:], AF.Silu, R=[cT], W=[K.sc])

        phase_in(K)
        for l in range(L):
            phase_mod(K, l)
            phase_norm(K, l, 1)
            if cfg.get("mixers", True):
                phase_mixers(K, l)
            else:
                phase_zero_y(K)
            phase_merge(K, l)
            phase_norm(K, l, 2)
            phase_ffn(K, l)
        phase_norm(K, L, "f")
        S.final_wait()
    K.n_ops = S.n_ops
    return nc, K


def host_common(inp, L):
    f = np.float32
    d = {}
    d["bmodT_d"] = np.ascontiguousarray(inp["b_mod"][:L].reshape(L, 96, 128).transpose(2, 0, 1)).astype(f)
    d["norm1T_d"] = np.ascontiguousarray(inp["norm1"][:L].reshape(L, NKC, 128).transpose(2, 0, 1)).astype(f)
    d["norm2T_d"] = np.ascontiguousarray(inp["norm2"][:L].reshape(L, NKC, 128).transpose(2, 0, 1)).astype(f)
    d["normfT_d"] = np.ascontiguousarray(inp["norm_f"].reshape(NKC, 128).T).astype(f)
    d["identF_d"] = np.eye(128, dtype=f)
    for k in ("w_mod", "w_in", "w_br_pool", "w_br_attn", "w_br_delta", "w_br_gla", "w_out", "w_gate", "w_up", "w_down"):
        d[k] = np.ascontiguousarray(inp[k][:L])
    return d


def host_core(inp, common, b_s, p_list, L):
    d = dict(common)
    d["x_s"] = np.ascontiguousarray(inp["x_sample"][b_s])
    d["x_p"] = np.ascontiguousarray(np.concatenate([inp["x_prompt"][p] for p in p_list], axis=0))
    cond = np.stack([inp["c"][b_s], inp["c_ctx"]], axis=0)
    d["condT"] = np.ascontiguousarray(cond.reshape(2, NKC, 128).transpose(2, 1, 0)).astype(np.float32)
    return d


def seqs_of(K):
    out = [("S", 0, K.T_S, 0)]
    for p in range(K.NP):
        out.append(("P", K.T_S + p * K.T_P, K.T_P, p))
    return out


def seq_tiles(T):
    return [(t, min(512, T - t)) for t in range(0, T, 512)]


def load_h(K, st, t0, T):
    h = sb(K, st, [128, NKC, T], BF16, "H")
    K.S.dma(h[:], K.HD[:, :, t0:t0 + T].rearrange("c p t -> p c t"), R=[K.HD], W=[h])
    return h


def win_cols(K, l, c0, n):
    return K.w_in[l, :, c0:c0 + n].rearrange("(k p) c -> p k c", p=128)


def mixer_pool(K, l, seq):
    S = K.S
    kind, t0, T, idx = seq
    ring = Ring(K, [(K.w_in, win_cols(K, l, OFF["pool"], 512), NKC, 512)])
    rc_d = K.rcnt_S if kind == "S" else K.rcnt_P
    with ExitStack() as st:
        h = load_h(K, st, t0, T)
        u = sb(K, st, [128, T + 16], F32, "pu")
        s1 = sb(K, st, [128, T + 16], F32, "ps1")
        s2 = sb(K, st, [128, T + 16], F32, "ps2")
        rc = sb(K, st, [128, T], F32, "prc")
        pbf = sb(K, st, [128, T], BF16, "ppb")
        pw = sb(K, st, [128, 4, 128], BF16, "ppw")
        yo = [sb(K, st, [128, 512], BF16, "pyo") for _ in range(2)]
        S.dma(pw[:], K.pool_w[l].rearrange("g c d -> c g d"), R=[K.pool_w], W=[pw], q="pool")
        wb, w = ring.load(K.w_in, win_cols(K, l, OFF["pool"], 512), NKC, 512)
        cnt = 0
        for g, win in enumerate((2, 4, 8, 16)):
            S.op("pool", lambda e: e.memset(u[:, 0:8], 0.0), W=[u])
            S.op("pool", lambda e: e.memset(u[:, T + 8:T + 16], 0.0), W=[u])
            S.dma(rc[:], rc_d[g:g + 1, :].partition_broadcast(128), R=[rc_d], W=[rc])
            for (o, n) in seq_tiles(T):
                pb = K.PS[cnt % 2]
                cnt += 1
                for kc in range(NKC):
                    mm(K, pb, pb[:, :n], wb, w[:, kc, g * 128:(g + 1) * 128], h, h[:, kc, o:o + n],
                       start=(kc == 0), stop=(kc == NKC - 1))
                cp(K, "act", u[:, 8 + o:8 + o + n], pb[:, :n], R=[pb], W=[u])
            N = T + 16
            tt(K, "dve", s1[:, 1:N], u[:, 0:N - 1], u[:, 1:N], ALU.add, R=[u], W=[s1])
            cur = s1
            if win >= 4:
                tt(K, "dve", s2[:, 2:N - 1], s1[:, 1:N - 2], s1[:, 3:N], ALU.add, R=[s1], W=[s2])
                cur = s2
            if win >= 8:
                tt(K, "dve", s1[:, 4:N - 3], s2[:, 2:N - 5], s2[:, 6:N - 1], ALU.add, R=[s2], W=[s1])
                cur = s1
            if win >= 16:
                tt(K, "dve", s2[:, 8:N - 8], s1[:, 4:N - 12], s1[:, 12:N - 4], ALU.add, R=[s1], W=[s2])
                cur = s2
            oth = s1 if cur is s2 else s2
            tt(K, "dve", oth[:, 8:8 + T], cur[:, 8:8 + T], rc[:], ALU.mult, R=[cur, rc], W=[oth])
            tt(K, "dve", pbf[:], oth[:, 8:8 + T], u[:, 8:8 + T], ALU.subtract, R=[oth, u], W=[pbf])
            for (o, n) in seq_tiles(T):
                pb = K.PS[2 + cnt % 2]
                y = yo[cnt % 2]
                cnt += 1
                mm(K, pb, pb[:, :n], pw, pw[:, g, :], pbf, pbf[:, o:o + n])
                ts(K, "dve", y[:, :n], pb[:, :n], K.pscT[:, l, g:g + 1], None, ALU.mult, None, R=[pb, K.pscT], W=[y])
                S.dma(K.YD[g, :, t0 + o:t0 + o + n], y[:, :n], R=[y], W=[K.YDB[g]])
        S.barrier()


def mixer_attn(K, l, seq):
    S = K.S
    kind, t0, T, idx = seq
    ring = Ring(K, [(K.w_in, win_cols(K, l, OFF["q_at"], 512), NKC, 512),
                    (K.w_in, win_cols(K, l, OFF["q_at"] + 512, 512), NKC, 512),
                    (K.w_in, win_cols(K, l, OFF["k_at"], 256), NKC, 256),
                    (K.w_in, win_cols(K, l, OFF["k_at"], 512), NKC, 512)])
    rope = (kind == "S")
    nb = T // 128
    with ExitStack() as outer:
        qT = sb(K, outer, [128, 8, T], BF16, "aq")
        kT = sb(K, outer, [128, 2, T], BF16, "ak")
        vt = sb(K, outer, [128, nb, 256], BF16, "av")
        with ExitStack() as st:
            h = load_h(K, st, t0, T)
            qb = [sb(K, st, [128, 512], BF16, "aqb") for _ in range(2)]
            t1 = [sb(K, st, [128, 512], F32, "at1") for _ in range(2)]
            t2 = [sb(K, st, [128, 512], F32, "at2") for _ in range(2)]
            kvo = [sb(K, st, [128, 512], F32, "akvo") for _ in range(2)]
            if rope:
                cs = sb(K, st, [128, T], BF16, "acos")
                sn = sb(K, st, [128, T], BF16, "asin")
                S.dma(cs[:], K.cos_d[:, :], R=[K.cos_d], W=[cs], q="pool")
                S.dma(sn[:], K.sin_d[:, :], R=[K.sin_d], W=[sn], q="pool")
            cnt = 0
            for hh in range(10):
                if hh % 4 == 0:
                    ncols = 512 if hh < 8 else 256
                    wb, w = ring.load(K.w_in, win_cols(K, l, OFF["q_at"] + hh * 128, ncols), NKC, ncols)
                m = hh % 4
                dst = qT[:, hh, :] if hh < 8 else kT[:, hh - 8, :]
                dbuf = qT if hh < 8 else kT
                scale = float(128 ** -0.5) if hh < 8 else 1.0
                for (o, n) in seq_tiles(T):
                    pb = K.PS[cnt % 2]
                    for kc in range(NKC):
                        mm(K, pb, pb[:, :n], wb, w[:, kc, m * 128:(m + 1) * 128], h, h[:, kc, o:o + n],
                           start=(kc == 0), stop=(kc == NKC - 1))
                    if not rope:
                        q_ = qb[cnt % 2]
                        K.S.op("act", lambda e, d=q_[:, :n], p=pb[:, :n], s=scale: e.activation(out=d, in_=p, func=AF.Copy, scale=s),
                               R=[pb], W=[q_])
                        cp(K, "dve", dst[:, o:o + n], q_[:, :n], R=[q_], W=[dbuf])
                    else:
                        q_ = qb[cnt % 2]
                        K.S.op("act", lambda e, d=q_[:, :n], p=pb[:, :n], s=scale: e.activation(out=d, in_=p, func=AF.Copy, scale=s),
                               R=[pb], W=[q_])
                        pr = K.PS[2 + cnt % 2]
                        mm(K, pr, pr[:, :n], K.RT, K.RT[:], q_, q_[:, :n])
                        a1, a2 = t1[cnt % 2], t2[cnt % 2]
                        tt(K, "dve", a1[:, :n], q_[:, :n], cs[:, o:o + n], ALU.mult, R=[q_, cs], W=[a1])
                        tt(K, "dve", a2[:, :n], pr[:, :n], sn[:, o:o + n], ALU.mult, R=[pr, sn], W=[a2])
                        tt(K, "dve", dst[:, o:o + n], a1[:, :n], a2[:, :n], ALU.add, R=[a1, a2], W=[dbuf])
                    cnt += 1
            wb, w = ring.load(K.w_in, win_cols(K, l, OFF["k_at"], 512), NKC, 512)
            for b in range(nb):
                pb = K.PS[4 + b % 2]
                for kc in range(NKC):
                    mm(K, pb, pb[:, :], h, h[:, kc, b * 128:(b + 1) * 128], wb, w[:, kc, :],
                       start=(kc == 0), stop=(kc == NKC - 1))
                if kind == "P" and K.cfg.get("kvout", 1):
                    ko = kvo[b % 2]
                    cp(K, "dve", ko[:], pb[:, :], R=[pb], W=[ko])
                    cp(K, "act", vt[:, b, :], ko[:, 256:512], R=[ko], W=[vt])
                else:
                    cp(K, "act", vt[:, b, :], pb[:, 256:512], R=[pb], W=[vt])
                if kind == "P" and K.cfg.get("kvout", 1):
                    S.dma(K.nk[idx, l, b * 128:(b + 1) * 128, :], ko[:, 0:256], R=[ko], W=[K.nk])
                    S.dma(K.nv[idx, l, b * 128:(b + 1) * 128, :], ko[:, 256:512], R=[ko], W=[K.nv])
            S.barrier()
        with ExitStack() as st:
            E = [sb(K, st, [128, 512], BF16, "aE") for _ in range(3)]
            rd = [sb(K, st, [128, 512], F32, "ard") for _ in range(2)]
            ob = [sb(K, st, [128, 512], BF16, "aob") for _ in range(2)]
            esr = sb(K, st, [1, 1024], BF16, "aes")
            esf = sb(K, st, [1, 1024], F32, "aesf")
            S.dma(esf[:], K.sink_rep[l, :, :], R=[K.sink_rep], W=[esf])
            act(K, esr[:], esf[:], AF.Exp, R=[esf], W=[esr])
            if kind == "S" and K.cfg.get("attn_dbg", 3) >= 2:
                kc_f = sb(K, st, [128, 2, 256], F32, "akcf")
                kcT = sb(K, st, [128, 2, 256], BF16, "akcT")
                vc = sb(K, st, [128, 2, 256], BF16, "avc")
                S.dma(kc_f[:], K.cache_k[l].rearrange("(b p) c -> p b c", p=128), R=[K.cache_k], W=[kc_f])
                S.dma(vc[:], K.cache_v[l].rearrange("(b p) c -> p b c", p=128), R=[K.cache_v], W=[vc], q="pool")
                for n in range(2):
                    pb = K.PS[6]
                    for b in range(2):
                        tr(K, pb, pb[:, b * 128:(b + 1) * 128], kc_f, kc_f[:, b, n * 128:(n + 1) * 128], K.identF, K.identF[:])
                    cp(K, "dve", kcT[:, n, :], pb[:, 0:256], R=[pb], W=[kcT])
            cnt = 0
            ecnt = 0
            dbg = K.cfg.get("attn_dbg", 3)
            for n in range(2 if dbg >= 3 else 0):
                for i in range(nb):
                    kbs = []
                    if kind == "S":
                        for b in range(2):
                            kbs.append((kcT, kcT[:, n, b * 128:(b + 1) * 128], vc, vc[:, b, n * 128:(n + 1) * 128], None))
                        for j, mk in ((i - 1, K.maskL), (i, None), (i + 1, K.maskR)):
                            if 0 <= j < nb:
                                kbs.append((kT, kT[:, n, j * 128:(j + 1) * 128], vt, vt[:, j, n * 128:(n + 1) * 128], mk))
                    else:
                        for b in range(nb):
                            kbs.append((kT, kT[:, n, b * 128:(b + 1) * 128], vt, vt[:, b, n * 128:(n + 1) * 128], None))
                    po = K.PS[2 + cnt % 2]
                    pd = K.PS[4 + cnt % 2]
                    qap = qT[:, 4 * n:4 * n + 4, i * 128:(i + 1) * 128]
                    for bi, (kb_, kap, vb_, vap, mk) in enumerate(kbs):
                        pS = K.PS[ecnt % 2]
                        e_ = E[ecnt % 3]
                        ecnt += 1
                        mm(K, pS, pS[:, :].rearrange("p (a b) -> p a b", a=4), kb_, kap, qT, qap)
                        act(K, e_[:], pS[:, :], AF.Exp, R=[pS], W=[e_])
                        if mk is not None:
                            tt(K, "dve", e_[:].rearrange("p (a b) -> p a b", a=4), e_[:].rearrange("p (a b) -> p a b", a=4),
                               mk[:], ALU.mult, R=[e_, mk], W=[e_])
                        mm(K, po, po[:, :], vb_, vap, e_, e_[:], start=(bi == 0), stop=(bi == len(kbs) - 1))
                        mm(K, pd, pd[:, :], K.onesB, K.onesB[:], e_, e_[:], start=(bi == 0), stop=False)
                    mm(K, pd, pd[:, :], K.onesB, K.onesB[0:1, :], esr, esr[0:1, n * 512:(n + 1) * 512], start=False, stop=True)
                    r_ = rd[cnt % 2]
                    o_ = ob[cnt % 2]
                    K.S.op("dve", lambda e, a=r_[:], b=pd[:, :]: e.reciprocal(out=a, in_=b), R=[pd], W=[r_])
                    tt(K, "dve", o_[:], po[:, :], r_[:], ALU.mult, R=[po, r_], W=[o_])
                    S.dma(K.YD[4 + 4 * n:8 + 4 * n, :, t0 + i * 128:t0 + (i + 1) * 128].rearrange("c p t -> p c t"),
                          o_[:].rearrange("p (a b) -> p a b", a=4), R=[o_], W=K.YDB[4 + 4 * n:8 + 4 * n])
                    cnt += 1
            S.barrier()


def zero_y(K, chunks, seq):
    S = K.S
    kind, t0, T, idx = seq
    with ExitStack() as st:
        z = sb(K, st, [128, 512], BF16, "zy")
        S.op("dve", lambda e: e.memset(z[:], 0.0), W=[z])
        for c in chunks:
            for (o, n) in seq_tiles(T):
                S.dma(K.YD[c, :, t0 + o:t0 + o + n], z[:, :n], R=[z], W=[K.YDB[c]])
        S.barrier()


def phase_mixers(K, l):
    mix = K.cfg.get("mix", ("pool", "attn", "dn", "gla"))
    for seq in seqs_of(K):
        if "pool" in mix:
            mixer_pool(K, l, seq)
        else:
            zero_y(K, range(0, 4), seq)
        if "attn" in mix and seq[0] in K.cfg.get("attn_kinds", "SP"):
            mixer_attn(K, l, seq)
        else:
            zero_y(K, range(4, 12), seq)
        if "dn" in mix:
            mixer_dn(K, l, seq)
        else:
            zero_y(K, range(12, 16), seq)
        if "gla" in mix:
            mixer_gla(K, l, seq)
        else:
            zero_y(K, range(16, 20), seq)


def host_mix_common(inp, cfg):
    f = np.float32
    L, T_S, T_P = cfg["L"], cfg["T_S"], cfg["T_P"]
    d = {}
    d["pool_w"] = np.ascontiguousarray(inp["pool_w"][:L])
    d["pscT_d"] = np.ascontiguousarray(inp["pool_scale"][:L].reshape(L, 4, 128).transpose(2, 0, 1)).astype(f)

    def rcnt(T):
        pos = np.arange(T)
        out = np.zeros((4, T), f)
        for g, w in enumerate((2, 4, 8, 16)):
            lo = np.clip(pos - w // 2, 0, T)
            hi = np.clip(pos + w // 2, 0, T)
            out[g] = 1.0 / (hi - lo)
        return out
    d["rcnt_S"] = rcnt(T_S)
    d["rcnt_P"] = rcnt(T_P)
    t = np.arange(T_S)
    row = (t // 64).astype(np.float64)
    col = (t % 64).astype(np.float64)
    inv = 10000.0 ** (-np.arange(32, dtype=np.float64) / 32)
    ang = np.zeros((128, T_S))
    for fi in range(128):
        pos = row if fi < 64 else col
        ang[fi] = pos * inv[fi % 32]
    d["cos_d"] = np.cos(ang).astype(f)
    d["sin_d"] = np.sin(ang).astype(f)
    R = np.zeros((128, 128), f)
    for fi in range(128):
        if fi % 64 < 32:
            R[fi, fi + 32] = -1.0
        else:
            R[fi, fi - 32] = 1.0
    d["RT_d"] = np.ascontiguousarray(R.T)
    j = np.arange(128)[:, None]
    r = np.arange(128)[None, :]
    d["maskL_d"] = np.ascontiguousarray(np.broadcast_to((j >= r).astype(f)[:, None, :], (128, 4, 128)))
    d["maskR_d"] = np.ascontiguousarray(np.broadcast_to((j <= r).astype(f)[:, None, :], (128, 4, 128)))
    d["sink_rep"] = np.ascontiguousarray(np.repeat(inp["attn_sink"][:L], 128, axis=1).reshape(L, 1, 1024)).astype(f)
    same = (j // 64) == (r // 64)
    d["triF_d"] = np.stack([((j <= r) & same), ((j >= r) & same)]).astype(f)
    d["blk1_d"] = same.astype(f)
    d["negoff_d"] = (np.eye(128) - 1.0).astype(f)
    es = np.zeros((3, 3, 128), f)
    for k in range(3):
        es[k, k, :] = 1.0
    d["esel_d"] = es
    w2p = np.zeros((L, 32, 512), f)
    w2p[:, 0:16, 0:256] = inp["gla_w2"][:L, 0]
    w2p[:, 16:32, 256:512] = inp["gla_w2"][:L, 1]
    d["w2pad_d"] = w2p
    d["b2row_d"] = np.ascontiguousarray(inp["gla_b2"][:L].reshape(L, 1, 512)).astype(f)
    d["gnT_d"] = np.ascontiguousarray(inp["gla_norm"][:L].T).astype(f)
    d["dnT_d"] = np.ascontiguousarray(inp["dn_norm"][:L].T).astype(f)
    d["convT_d"] = np.ascontiguousarray(inp["dn_conv"][:L].reshape(L, 4, 12, 128).transpose(3, 0, 2, 1)).astype(f)
    d["dtb_d"] = np.ascontiguousarray(np.broadcast_to(inp["dn_dt_bias"][:L].reshape(1, L, 8), (128, L, 8))).astype(f)
    d["alog_d"] = np.ascontiguousarray(np.broadcast_to(inp["dn_a_log"][:L].reshape(1, L, 8), (128, L, 8))).astype(f)
    return d


def host_mix_core(inp, b_s, p_list, cfg):
    L = cfg["L"]
    d = {}
    d["cache_k"] = np.ascontiguousarray(inp["cache_k"][b_s, :L].reshape(L, 256, 256))
    d["cache_v"] = np.ascontiguousarray(inp["cache_v"][b_s, :L].reshape(L, 256, 256))
    d["state_delta"] = np.ascontiguousarray(inp["state_delta"][b_s, :L])
    d["state_gla"] = np.ascontiguousarray(inp["state_gla"][b_s, :L])
    return d


def mix_inputs(K, ein):
    L = K.L
    ein("triF_d", [2, 128, 128])
    ein("blk1_d", [128, 128])
    ein("negoff_d", [128, 128])
    ein("esel_d", [3, 3, 128])
    ein("w2pad_d", [L, 32, 512])
    ein("b2row_d", [L, 1, 512])
    ein("gnT_d", [128, L])
    ein("dnT_d", [128, L])
    ein("convT_d", [128, L, 12, 4])
    ein("dtb_d", [128, L, 8])
    ein("alog_d", [128, L, 8])
    K.DS = Buf(K.nc.dram_tensor("DS", [16, 128, K.NT], BF16).ap(), "DS")


def mix_consts(K, top):
    S = K.S
    L = K.L
    K.triF = sb(K, top, [128, 2, 128], F32, "triF")
    K.blk1 = sb(K, top, [128, 128], F32, "blk1")
    K.negoff = sb(K, top, [128, 128], F32, "negoff")
    K.esel = sb(K, top, [3, 3, 128], F32, "esel")
    K.identB = sb(K, top, [128, 128], BF16, "identB")
    K.gnT = sb(K, top, [128, L], F32, "gnT")
    K.dnT = sb(K, top, [128, L], F32, "dnT")
    K.convT = sb(K, top, [128, L, 12, 4], F32, "convT")
    K.dtb = sb(K, top, [128, L, 8], F32, "dtb")
    K.nea = sb(K, top, [128, L, 8], F32, "nea")
    K.onesF = sb(K, top, [128, 128], F32, "onesF")
    S.dma(K.triF[:], K.triF_d.t.rearrange("d j i -> j d i"), R=[K.triF_d], W=[K.triF])
    S.dma(K.blk1[:], K.blk1_d[:, :], R=[K.blk1_d], W=[K.blk1])
    S.dma(K.negoff[:], K.negoff_d[:, :], R=[K.negoff_d], W=[K.negoff])
    S.dma(K.esel[:], K.esel_d[:, :, :], R=[K.esel_d], W=[K.esel])
    S.dma(K.gnT[:], K.gnT_d[:, :], R=[K.gnT_d], W=[K.gnT])
    S.dma(K.dnT[:], K.dnT_d[:, :], R=[K.dnT_d], W=[K.dnT])
    S.dma(K.convT[:], K.convT_d[:, :, :, :], R=[K.convT_d], W=[K.convT])
    S.dma(K.dtb[:], K.dtb_d[:, :, :], R=[K.dtb_d], W=[K.dtb])
    S.dma(K.nea[:], K.alog_d[:, :, :], R=[K.alog_d], W=[K.nea])
    S.dma(K.identB[:], K.identF_d[:, :], R=[K.identF_d], W=[K.identB], q="pool")
    S.op("dve", lambda e: e.memset(K.onesF[:], 1.0), W=[K.onesF])
    act(K, K.nea[:], K.nea[:], AF.Exp, R=[K.nea], W=[K.nea])
    ts(K, "dve", K.nea[:], K.nea[:], -1.0, None, ALU.mult, None, R=[K.nea], W=[K.nea])


def final_gate_norm(K, l, seq, oacc, gbuf, gsil, nw, ychunk, st):
    S = K.S
    kind, t0, T, idx = seq
    sq = sb(K, st, [128, 512], BF16, "fsq")
    rr = sb(K, st, [128, 512], F32, "frr")
    tm = sb(K, st, [128, 512], F32, "ftm")
    yo = [sb(K, st, [128, 512], BF16, "fyo") for _ in range(2)]
    for ti, (o, n) in enumerate(seq_tiles(T)):
        act(K, sq[:, :n], oacc[:, o:o + n], AF.Square, R=[oacc], W=[sq])
        pb = K.PS[6]
        mm(K, pb, pb[:, :n], K.onesB, K.onesB[:], sq, sq[:, :n])
        rsqrt(K, rr, rr[:, :n], pb, pb[:, :n], 1.0 / 128)
        tt(K, "dve", tm[:, :n], oacc[:, o:o + n], rr[:, :n], ALU.mult, R=[oacc, rr], W=[tm])
        y = yo[ti % 2]
        stt(K, "dve", y[:, :n], tm[:, :n], nw, gsil[:, o:o + n], ALU.mult, ALU.mult, R=[tm, gbuf], W=[y])
        S.dma(K.YD[ychunk, :, t0 + o:t0 + o + n], y[:, :n], R=[y], W=[K.YDB[ychunk]])


def proj_fm(K, h, wb, w_ap, T, pbanks, consume):
    for ti, (o, n) in enumerate(seq_tiles(T)):
        pb = pbanks[ti % len(pbanks)]
        for kc in range(NKC):
            mm(K, pb, pb[0:w_ap.shape[-1], :n], wb, w_ap[:, kc, :], h, h[:, kc, o:o + n],
               start=(kc == 0), stop=(kc == NKC - 1))
        consume(o, n, pb)


def mixer_gla(K, l, seq):
    S = K.S
    kind, t0, T, idx = seq
    ring = Ring(K, [(K.w_in, win_cols(K, l, OFF["q_gl"], 512), NKC, 512),
                    (K.w_in, win_cols(K, l, OFF["r_gl"], 512), NKC, 512),
                    (K.w_in, win_cols(K, l, OFF["lr"], 32), NKC, 32),
                    (K.w_in, win_cols(K, l, OFF["v_gl"], 512), NKC, 512)])
    nb = T // 128
    with ExitStack() as outer:
        qk = sb(K, outer, [128, 4, T], BF16, "gqk")
        rs = sb(K, outer, [128, 4, T], BF16, "grs")
        vt = sb(K, outer, [128, nb, 512], BF16, "gvt")
        lrT = sb(K, outer, [32, T], BF16, "glr")
        w2p = sb(K, outer, [32, 512], BF16, "gw2")
        b2r = sb(K, outer, [1, 512], BF16, "gb2")
        S.dma(w2p[:], K.w2pad_d[l], R=[K.w2pad_d], W=[w2p], q="pool")
        S.dma(b2r[:], K.b2row_d[l], R=[K.b2row_d], W=[b2r], q="pool")
        with ExitStack() as st:
            h = load_h(K, st, t0, T)
            wb, w = ring.load(K.w_in, win_cols(K, l, OFF["q_gl"], 512), NKC, 512)
            for c in range(4):
                proj_fm(K, h, wb, w[:, :, c * 128:(c + 1) * 128], T, [K.PS[0], K.PS[1]],
                        lambda o, n, pb, c=c: cp(K, "act", qk[:, c, o:o + n], pb[:, :n], R=[pb], W=[qk]))
            wb, w = ring.load(K.w_in, win_cols(K, l, OFF["r_gl"], 512), NKC, 512)
            for c in range(4):
                proj_fm(K, h, wb, w[:, :, c * 128:(c + 1) * 128], T, [K.PS[0], K.PS[1]],
                        lambda o, n, pb, c=c: act(K, rs[:, c, o:o + n], pb[:, :n], AF.Silu, R=[pb], W=[rs]))
            wb, w = ring.load(K.w_in, win_cols(K, l, OFF["lr"], 32), NKC, 32)
            proj_fm(K, h, wb, w[:, :, 0:32], T, [K.PS[0], K.PS[1]],
                    lambda o, n, pb: cp(K, "act", lrT[:, o:o + n], pb[0:32, :n], R=[pb], W=[lrT]))
            wb, w = ring.load(K.w_in, win_cols(K, l, OFF["v_gl"], 512), NKC, 512)
            for b in range(nb):
                pb = K.PS[2 + b % 2]
                for kc in range(NKC):
                    mm(K, pb, pb[:, :], h, h[:, kc, b * 128:(b + 1) * 128], wb, w[:, kc, :],
                       start=(kc == 0), stop=(kc == NKC - 1))
                cp(K, "dve", vt[:, b, :], pb[:, :], R=[pb], W=[vt])
            S.barrier()
        for hd in range(4):
            with ExitStack() as st:
                hp = (hd % 2) * 64
                qv = qk[hp:hp + 64, hd // 2, :]
                kv = qk[hp:hp + 64, 2 + hd // 2, :]
                oacc = sb(K, st, [128, T], F32, "goacc")
                S.op("pool", lambda e: e.memset(oacc[:], 0.0), W=[oacc])
                Sf = [sb(K, st, [128, 128], F32, "gS") for _ in range(2)]
                P = slice(hp, hp + 64)
                Sb = [sb(K, st, [128, 128], BF16, "gSb") for _ in range(2)]
                for d in range(2):
                    if kind == "S":
                        S.dma(Sf[d][P, :], K.state_gla[l, d, hd], R=[K.state_gla], W=[Sf[d]])
                    else:
                        S.op("pool", lambda e, d=d: e.memset(Sf[d][:], 0.0), W=[Sf[d]])
                    cp(K, "act", Sb[d][P, :], Sf[d][P, :], R=[Sf[d]], W=[Sb[d]])
                U = [dict(e1=sb(K, st, [128, 64], F32, "ge1"), sp=sb(K, st, [128, 64], F32, "gsp"),
                          gcp=sb(K, st, [128, 128], F32, "ggcp"), egc=sb(K, st, [128, 128], F32, "gegc"),
                          engc=sb(K, st, [128, 128], F32, "gengc"), ekd=sb(K, st, [128, 128], F32, "gekd"),
                          nb_=sb(K, st, [128, 2], F32, "gnb"), qg=sb(K, st, [128, 128], BF16, "gqg"),
                          kg=sb(K, st, [128, 128], BF16, "gkg"), kdT=sb(K, st, [128, 128], BF16, "gkdT"),
                          kd=sb(K, st, [128, 64], BF16, "gkd"), aT=sb(K, st, [128, 128], BF16, "gaT"))
                     for _ in range(2)]
                def gla_unit(d, b):
                    u = U[d]
                    pA, pB_, pC = K.PS[3 * d], K.PS[3 * d + 1], K.PS[3 * d + 2]
                    tk = slice(b * 128, (b + 1) * 128)
                    cw = slice(d * 256 + hd * 64, d * 256 + hd * 64 + 64)
                    mm(K, pA, pA[:, 0:64], lrT, lrT[:, tk], w2p, w2p[:, cw], start=True, stop=False)
                    yield
                    mm(K, pA, pA[:, 0:64], K.onesB, K.onesB[0:1, :], b2r, b2r[0:1, cw], start=False, stop=True)
                    yield
                    act(K, u["e1"][:], pA[:, 0:64], AF.Exp, R=[pA], W=[u["e1"]], scale=-1.0)
                    yield
                    act(K, u["sp"][:], u["e1"][:], AF.Ln, R=[u["e1"]], W=[u["sp"]], bias=1.0)
                    yield
                    mm(K, pA, pA[P, 128:256], u["sp"], u["sp"][:], K.triF, K.triF[:, d, :])
                    yield
                    cp(K, "act", u["gcp"][P, :], pA[P, 128:256], R=[pA], W=[u["gcp"]])
                    yield
                    act(K, u["egc"][P, :], u["gcp"][P, :], AF.Exp, R=[u["gcp"]], W=[u["egc"]], scale=-1.0 / 16)
                    yield
                    act(K, u["engc"][P, :], u["gcp"][P, :], AF.Exp, R=[u["gcp"]], W=[u["engc"]], scale=1.0 / 16)
                    yield
                    c0 = 63 if d == 0 else 0
                    ts(K, "dve", u["nb_"][P, :], u["gcp"][P, c0:c0 + 65:64], -1.0 / 16, None, ALU.mult, None,
                       R=[u["gcp"]], W=[u["nb_"]])
                    yield
                    for c in range(2):
                        act(K, u["ekd"][P, c * 64:(c + 1) * 64], u["gcp"][P, c * 64:(c + 1) * 64], AF.Exp,
                            R=[u["gcp"], u["nb_"]], W=[u["ekd"]], scale=1.0 / 16, bias=u["nb_"][P, c:c + 1])
                        yield
                    stt(K, "dve", u["qg"][P, :], qv[:, tk], 0.125, u["egc"][P, :], ALU.mult, ALU.mult, R=[qk, u["egc"]], W=[u["qg"]])
                    yield
                    tt(K, "dve", u["kg"][P, :], kv[:, tk], u["engc"][P, :], ALU.mult, R=[qk, u["engc"]], W=[u["kg"]])
                    yield
                    tt(K, "dve", u["kdT"][P, :], kv[:, tk], u["ekd"][P, :], ALU.mult, R=[qk, u["ekd"]], W=[u["kdT"]])
                    yield
                    pT = K.PB
                    tr(K, pT, pT[:, d * 64:(d + 1) * 64], u["kdT"], u["kdT"][P, :], K.identB, K.identB[P, hp:hp + 64])
                    yield
                    cp(K, "act", u["kd"][:], pT[:, d * 64:(d + 1) * 64], R=[pT], W=[u["kd"]])
                    yield
                    mm(K, pB_, pB_[:, 0:128], u["kg"], u["kg"][P, :], u["qg"], u["qg"][P, :])
                    yield
                    tt(K, "dve", u["aT"][:], pB_[:, 0:128], K.triF[:, d, :], ALU.mult, R=[pB_, K.triF], W=[u["aT"]])
                    yield
                    for c in ((0, 1) if d == 0 else (1, 0)):
                        cs_ = slice(c * 64, (c + 1) * 64)
                        vb = vt[:, b, hd * 128:(hd + 1) * 128]
                        mm(K, pC, pC[:, 0:64], Sb[d], Sb[d][P, :], u["qg"], u["qg"][P, cs_], start=True, stop=False)
                        yield
                        mm(K, pC, pC[:, 0:64], vt, vb, u["aT"], u["aT"][:, cs_], start=False, stop=True)
                        yield
                        ot = oacc[:, b * 128 + c * 64:b * 128 + (c + 1) * 64]
                        tt(K, "dve", ot, pC[:, 0:64], ot, ALU.add, R=[pC, oacc], W=[oacc])
                        yield
                        mm(K, pC, pC[P, 128:256], u["kd"], u["kd"][cs_, :], vt, vt[cs_, b, hd * 128:(hd + 1) * 128])
                        yield
                        col = c * 64 + (63 if d == 0 else 0)
                        stt(K, "dve", Sf[d][P, :], Sf[d][P, :], u["egc"][P, col:col + 1], pC[P, 128:256], ALU.mult, ALU.add,
                            R=[Sf[d], u["egc"], pC], W=[Sf[d]])
                        yield
                        cp(K, "act", Sb[d][P, :], Sf[d][P, :], R=[Sf[d]], W=[Sb[d]])
                        yield
                for s_ in range(nb):
                    gens = [gla_unit(0, s_), gla_unit(1, nb - 1 - s_)]
                    while gens:
                        for g_ in list(gens):
                            try:
                                next(g_)
                            except StopIteration:
                                gens.remove(g_)
                if kind == "P":
                    for d in range(2):
                        S.dma(K.nsg[idx, l, d, hd], Sf[d][P, :], R=[Sf[d]], W=[K.nsg])
                final_gate_norm(K, l, seq, oacc, rs, rs[:, hd, :], K.gnT[:, l:l + 1], 16 + hd, st)
                S.barrier()


def mixer_dn(K, l, seq):
    S = K.S
    kind, t0, T, idx = seq
    ring = Ring(K, [(K.w_in, win_cols(K, l, OFF["qkv"] + c * 128, 512), NKC, 512) for c in (0, 4, 8, 12)]
                + [(K.w_in, win_cols(K, l, OFF["a"], 16), NKC, 16)])
    nb = T // 128
    with ExitStack() as outer:
        gg = sb(K, outer, [128, nb, 8], F32, "dg")
        be = sb(K, outer, [128, nb, 8], F32, "dbeta")
        with ExitStack() as st:
            h = load_h(K, st, t0, T)
            xp = [sb(K, st, [128, T + 3], F32, "dxp") for _ in range(2)]
            cv = [sb(K, st, [128, T], F32, "dcv") for _ in range(2)]
            sq = sb(K, st, [128, 512], BF16, "dsq")
            rn = sb(K, st, [128, 512], F32, "drn")
            ob = [sb(K, st, [128, T], BF16, "dob") for _ in range(2)]
            e1 = sb(K, st, [128, 8], F32, "de1")
            for x in xp:
                S.op("pool", lambda e, x=x: e.memset(x[:, 0:2], 0.0), W=[x])
                S.op("pool", lambda e, x=x: e.memset(x[:, T + 2:T + 3], 0.0), W=[x])
            for c in range(16):
                if c % 4 == 0:
                    wb, w = ring.load(K.w_in, win_cols(K, l, OFF["qkv"] + c * 128, 512), NKC, 512)
                wa = w[:, :, (c % 4) * 128:(c % 4 + 1) * 128]
                o_ = ob[c % 2]
                if c >= 12:
                    proj_fm(K, h, wb, wa, T, [K.PS[0], K.PS[1]],
                            lambda o, n, pb, o_=o_: act(K, o_[:, o:o + n], pb[:, :n], AF.Silu, R=[pb], W=[o_]))
                else:
                    x = xp[c % 2]
                    y = cv[c % 2]
                    proj_fm(K, h, wb, wa, T, [K.PS[0], K.PS[1]],
                            lambda o, n, pb, x=x: cp(K, "act", x[:, 2 + o:2 + o + n], pb[:, :n], R=[pb], W=[x]))
                    act(K, y[:], x[:, 0:T], AF.Copy, R=[x, K.convT], W=[y], scale=K.convT[:, l, c, 0:1])
                    for j in range(1, 4):
                        stt(K, "dve", y[:], x[:, j:j + T], K.convT[:, l, c, j:j + 1], y[:], ALU.mult, ALU.add,
                            R=[x, y, K.convT], W=[y])
                    if c >= 8:
                        act(K, o_[:], y[:], AF.Silu, R=[y], W=[o_])
                    else:
                        act(K, y[:], y[:], AF.Silu, R=[y], W=[y])
                        for (o, n) in seq_tiles(T):
                            act(K, sq[:, :n], y[:, o:o + n], AF.Square, R=[y], W=[sq])
                            pb = K.PS[2]
                            mm(K, pb, pb[:, :n], K.onesB, K.onesB[:], sq, sq[:, :n])
                            rsqrt(K, rn, rn[:, :n], pb, pb[:, :n], 1.0)
                            if c < 4:
                                stt(K, "dve", o_[:, o:o + n], y[:, o:o + n], float(128 ** -0.5), rn[:, :n], ALU.mult, ALU.mult,
                                    R=[y, rn], W=[o_])
                            else:
                                tt(K, "dve", o_[:, o:o + n], y[:, o:o + n], rn[:, :n], ALU.mult, R=[y, rn], W=[o_])
                S.dma(K.DS[c, :, t0:t0 + T], o_[:], R=[o_], W=[K.DS])
            wb, w = ring.load(K.w_in, win_cols(K, l, OFF["a"], 16), NKC, 16)
            for b in range(nb):
                pb = K.PS[3]
                for kc in range(NKC):
                    mm(K, pb, pb[:, 0:16], h, h[:, kc, b * 128:(b + 1) * 128], wb, w[:, kc, :],
                       start=(kc == 0), stop=(kc == NKC - 1))
                tt(K, "dve", e1[:], pb[:, 0:8], K.dtb[:, l, :], ALU.add, R=[pb, K.dtb], W=[e1])
                act(K, be[:, b, :], pb[:, 8:16], AF.Sigmoid, R=[pb], W=[be])
                act(K, e1[:], e1[:], AF.Exp, R=[e1], W=[e1])
                act(K, e1[:], e1[:], AF.Ln, R=[e1], W=[e1], bias=1.0)
                tt(K, "dve", gg[:, b, :], e1[:], K.nea[:, l, :], ALU.mult, R=[e1, K.nea], W=[gg])
            S.barrier()
        for hd in range(4):
            with ExitStack() as st:
                qh = sb(K, st, [128, T], BF16, "dqh")
                kh = sb(K, st, [128, T], BF16, "dkh")
                vv = sb(K, st, [128, T], BF16, "dvv")
                zs = sb(K, st, [128, T], BF16, "dzs")
                for buf, c in ((qh, hd), (kh, 4 + hd), (vv, 8 + hd), (zs, 12 + hd)):
                    S.dma(buf[:], K.DS[c, :, t0:t0 + T], R=[K.DS], W=[buf])
                oacc = sb(K, st, [128, T], F32, "doacc")
                S.op("pool", lambda e: e.memset(oacc[:], 0.0), W=[oacc])
                Sf = [sb(K, st, [128, 128], F32, "dS") for _ in range(2)]
                for d in range(2):
                    if kind == "S":
                        S.dma(Sf[d][:], K.state_delta[l, d, hd], R=[K.state_delta], W=[Sf[d]])
                    else:
                        S.op("pool", lambda e, d=d: e.memset(Sf[d][:], 0.0), W=[Sf[d]])

                def mk():
                    f = lambda sh, dt, nm: sb(K, st, sh, dt, nm)
                    return dict(g3=f([128, 3], F32, "g3"), g3T=f([3, 128], F32, "g3T"), sc=f([128, 4], F32, "dsc"),
                                dl=f([128, 2], F32, "ddl"), dmt=f([128, 128], F32, "dmt"), dmm=f([128, 128], F32, "dmm"),
                                tn=f([128, 128], F32, "dtn"), m1=f([128, 128], F32, "dm1"), aqk=f([128, 128], F32, "daqk"),
                                X=[f([128, 128], F32, "dX") for _ in range(2)], XT=[f([128, 128], F32, "dXT") for _ in range(2)],
                                R=[f([128, 128], F32, "dR") for _ in range(2)], bv=f([128, 128], F32, "dbv"),
                                kbg=f([128, 128], F32, "dkbg"), kd=f([128, 128], F32, "dkd"), u=f([128, 128], F32, "du"),
                                wT=f([128, 128], F32, "dwT"), egr=f([128, 128], F32, "degr"), qe=f([128, 128], F32, "dqe"),
                                vn=f([128, 128], F32, "dvn"))
                U = [mk(), mk()]
                for d in range(2):
                    S.op("pool", lambda e, d=d: e.memset(U[d]["vn"][:], 0.0), W=[U[d]["vn"]])
                def dn_unit(d, b):
                    u = U[d]
                    pA, pB_, pC = K.PS[3 * d], K.PS[3 * d + 1], K.PS[3 * d + 2]
                    qB, qC = [pB_] * 4, [pC] * 4
                    pT = K.PB
                    tk = slice(b * 128, (b + 1) * 128)
                    col = d * 4 + hd
                    gcol = gg[:, b, col:col + 1]
                    bcol = be[:, b, col:col + 1]
                    mm(K, pA, pA[:, 0:1], K.triF, K.triF[:, d, :], gg, gcol)
                    yield
                    mm(K, pA, pA[:, 1:2], K.blk1, K.blk1[:], gg, gcol)
                    yield
                    cp(K, "act", u["g3"][:, 0:1], pA[:, 0:1], R=[pA], W=[u["g3"]])
                    yield
                    cp(K, "act", u["g3"][:, 2:3], pA[:, 1:2], R=[pA], W=[u["g3"]])
                    yield
                    cp(K, "act", u["g3"][:, 1:2], bcol, R=[be], W=[u["g3"]])
                    yield
                    act(K, u["sc"][:, 0:1], u["g3"][:, 0:1], AF.Exp, R=[u["g3"]], W=[u["sc"]])
                    yield
                    tt(K, "dve", u["sc"][:, 1:2], u["sc"][:, 0:1], u["g3"][:, 1:2], ALU.mult, R=[u["sc"], u["g3"]], W=[u["sc"]])
                    yield
                    act(K, u["sc"][:, 2:3], u["g3"][:, 0:1], AF.Exp, R=[u["g3"]], W=[u["sc"]], scale=-1.0, bias=u["g3"][:, 2:3])
                    yield
                    tr(K, pA, pA[0:3, 2:130], u["g3"], u["g3"][:], K.identF, K.identF[:])
                    yield
                    cp(K, "act", u["g3T"][:], pA[0:3, 2:130], R=[pA], W=[u["g3T"]])
                    yield
                    for r in range(3):
                        mm(K, pA, pA[:, 128 * (r + 1):128 * (r + 2)], K.esel, K.esel[:, r, :], u["g3T"], u["g3T"][:])
                        yield
                    Grow = pA[:, 128:256]
                    Brow = pA[:, 256:384]
                    Trow = pA[:, 384:512]
                    act(K, u["dl"][:], Trow[:, 0:65:64], AF.Exp, R=[pA], W=[u["dl"]])
                    yield
                    act(K, u["egr"][:], Grow, AF.Exp, R=[pA], W=[u["egr"]])
                    yield
                    ts(K, "dve", u["dmt"][:], Grow, u["g3"][:, 0:1], 0.0, ALU.subtract, ALU.min, R=[pA, u["g3"]], W=[u["dmt"]])
                    yield
                    act(K, u["dmt"][:], u["dmt"][:], AF.Exp, R=[u["dmt"]], W=[u["dmt"]])
                    yield
                    tt(K, "dve", u["dmm"][:], u["dmt"][:], K.triF[:, d, :], ALU.mult, R=[u["dmt"], K.triF], W=[u["dmm"]])
                    yield
                    tt(K, "dve", u["tn"][:], u["dmm"][:], K.negoff[:], ALU.mult, R=[u["dmm"], K.negoff], W=[u["tn"]])
                    yield
                    mm(K, qB[0], pB_[:, 0:128], kh, kh[:, tk], kh, kh[:, tk])
                    yield
                    mm(K, qB[1], pB_[:, 128:256], kh, kh[:, tk], qh, qh[:, tk])
                    yield
                    tt(K, "dve", u["aqk"][:], pB_[:, 128:256], u["dmm"][:], ALU.mult, R=[qB[1], u["dmm"]], W=[u["aqk"]])
                    yield
                    tt(K, "dve", u["m1"][:], pB_[:, 0:128], u["tn"][:], ALU.mult, R=[qB[0], u["tn"]], W=[u["m1"]])
                    yield
                    X, XT, R_ = u["X"], u["XT"], u["R"]
                    tt(K, "dve", X[0][:], Brow, u["m1"][:], ALU.mult, R=[pA, u["m1"]], W=[X[0]])
                    yield
                    tt(K, "dve", R_[0][:], X[0][:], K.identF[:], ALU.add, R=[X[0], K.identF], W=[R_[0]])
                    yield
                    tr(K, qC[3], pC[:, 384:512], X[0], X[0][:], K.identF, K.identF[:])
                    yield
                    cp(K, "dve", XT[0][:], pC[:, 384:512], R=[qC[3]], W=[XT[0]])
                    yield
                    ci, ri = 0, 0
                    for k in range(1, 6):
                        ni = 1 - ci
                        if k < 5:
                            mm(K, qB[2], pB_[:, 256:384], XT[ci], XT[ci][:], X[ci], X[ci][:])
                            yield
                            cp(K, "dve", X[ni][:], pB_[:, 256:384], R=[qB[2]], W=[X[ni]])
                            yield
                        mm(K, qB[3], pB_[:, 384:512], X[ci], X[ci][:], XT[ci], XT[ci][:])
                        yield
                        cp(K, "dve", XT[ni][:], pB_[:, 384:512], R=[qB[3]], W=[XT[ni]])
                        yield
                        mm(K, qB[0], pB_[:, 0:128], XT[ni], XT[ni][:], R_[ri], R_[ri][:])
                        yield
                        tt(K, "dve", R_[1 - ri][:], pB_[:, 0:128], R_[ri][:], ALU.add, R=[qB[0], R_[ri]], W=[R_[1 - ri]])
                        yield
                        ci, ri = ni, 1 - ri
                    Rf = R_[ri]
                    tr(K, pT, pT[:, d * 256:d * 256 + 128], kh, kh[:, tk], K.identB, K.identB[:])
                    yield
                    tr(K, pT, pT[:, d * 256 + 128:d * 256 + 256], vv, vv[:, tk], K.identB, K.identB[:])
                    yield
                    act(K, u["kbg"][:], pT[:, d * 256:d * 256 + 128], AF.Copy, R=[pT, u["sc"]], W=[u["kbg"]], scale=u["sc"][:, 1:2])
                    yield
                    act(K, u["kd"][:], pT[:, d * 256:d * 256 + 128], AF.Copy, R=[pT, u["sc"]], W=[u["kd"]], scale=u["sc"][:, 2:3])
                    yield
                    act(K, u["bv"][:], pT[:, d * 256 + 128:d * 256 + 256], AF.Copy, R=[pT, u["g3"]], W=[u["bv"]], scale=u["g3"][:, 1:2])
                    yield
                    mm(K, qB[1], pB_[:, 128:256], Rf, Rf[:], u["bv"], u["bv"][:])
                    yield
                    cp(K, "dve", u["u"][:], pB_[:, 128:256], R=[qB[1]], W=[u["u"]])
                    yield
                    mm(K, qB[2], pB_[:, 256:384], u["kbg"], u["kbg"][:], Rf, Rf[:])
                    yield
                    cp(K, "dve", u["wT"][:], pB_[:, 256:384], R=[qB[2]], W=[u["wT"]])
                    yield
                    tt(K, "dve", u["qe"][:], qh[:, tk], u["egr"][:], ALU.mult, R=[qh, u["egr"]], W=[u["qe"]])
                    yield
                    for c in ((0, 1) if d == 0 else (1, 0)):
                        cs_ = slice(c * 64, (c + 1) * 64)
                        mm(K, qC[0], pC[:, 0:128], u["wT"], u["wT"][:], Sf[d], Sf[d][:])
                        yield
                        tt(K, "dve", u["vn"][cs_, :], u["u"][cs_, :], pC[cs_, 0:128], ALU.subtract, R=[u["u"], qC[0]], W=[u["vn"]])
                        yield
                        mm(K, qC[1], pC[:, 128:192], Sf[d], Sf[d][:], u["qe"], u["qe"][:, cs_], start=True, stop=False)
                        yield
                        mm(K, qC[1], pC[:, 128:192], u["vn"], u["vn"][cs_, :], u["aqk"], u["aqk"][cs_, cs_], start=False, stop=True)
                        yield
                        ot = oacc[:, b * 128 + c * 64:b * 128 + (c + 1) * 64]
                        tt(K, "dve", ot, pC[:, 128:192], ot, ALU.add, R=[qC[1], oacc], W=[oacc])
                        yield
                        mm(K, qC[2], pC[:, 256:384], u["kd"], u["kd"][cs_, :], u["vn"], u["vn"][cs_, :])
                        yield
                        stt(K, "dve", Sf[d][:], Sf[d][:], u["dl"][:, c:c + 1], pC[:, 256:384], ALU.mult, ALU.add,
                            R=[Sf[d], u["dl"], qC[2]], W=[Sf[d]])
                        yield
                for s_ in range(nb):
                    gens = [dn_unit(0, s_), dn_unit(1, nb - 1 - s_)]
                    while gens:
                        for g_ in list(gens):
                            try:
                                next(g_)
                            except StopIteration:
                                gens.remove(g_)
                if kind == "P":
                    for d in range(2):
                        S.dma(K.nsd[idx, l, d, hd], Sf[d][:], R=[Sf[d]], W=[K.nsd])
                final_gate_norm(K, l, seq, oacc, zs, zs[:, :], K.dnT[:, l:l + 1], 12 + hd, st)
                S.barrier()


CFG = dict(T_S=2048, NP=2, T_P=256, L=4)


def kernel(**inputs):
    inp = {k: np.asarray(v) for k, v in inputs.items()}
    cfg = dict(CFG)
    L = cfg["L"]
    nc, K = build_program(cfg)
    common = host_common(inp, L)
    common.update(host_mix_common(inp, cfg))
    in_maps = []
    for c in range(8):
        pl = [2 * c, 2 * c + 1]
        d = host_core(inp, common, c // 2, pl, L)
        d.update(host_mix_core(inp, c // 2, pl, cfg))
        in_maps.append(d)
    res = run_bass_kernel_spmd(nc, in_maps, core_ids=list(range(8)))
    r = res.results
    f = np.float32
    y_prompt = np.stack([r[p // 2]["y_p"].reshape(2, 256, D)[p % 2] for p in range(16)]).astype(f)
    y_sample = np.stack([r[2 * b]["y_s"] for b in range(4)]).astype(f)
    nk = np.concatenate([r[c]["nk"].reshape(2, L, 256, 2, 128) for c in range(8)]).astype(f)
    nv = np.concatenate([r[c]["nv"].reshape(2, L, 256, 2, 128) for c in range(8)]).astype(f)
    nsd = np.concatenate([r[c]["nsd"] for c in range(8)]).astype(f)
    nsg = np.concatenate([r[c]["nsg"] for c in range(8)]).astype(f)
    return (y_prompt, y_sample, nk, nv, nsd, nsg)
```

```python
import numpy as np
from contextlib import ExitStack
import concourse.bass as bass
import concourse.mybir as mybir
from concourse.bass_utils import run_bass_kernel_spmd

F32 = mybir.dt.float32
BF16 = mybir.dt.bfloat16
AF = mybir.ActivationFunctionType
ALU = mybir.AluOpType
AX = mybir.AxisListType

COMPUTE = ("pe", "act", "dve", "pool")
SEM_EPOCH = 30000


class Buf:
    __slots__ = ("t", "lw", "rd", "name", "excl")

    def __init__(self, t, name="", excl=False):
        self.excl = excl
        self.t = t
        self.lw = None
        self.rd = {}
        self.name = name

    def __getitem__(self, idx):
        return self.t[idx]


class Sched:
    def __init__(self, nc, stack, n_dma_sems=16):
        self.nc = nc
        self.stack = stack
        self.q = {e: [] for e in COMPUTE + ("sp",)}
        self.cnt = {e: 0 for e in COMPUTE}
        self.sems = {}
        self.own = {e: set() for e in COMPUTE}
        self.semobj = {}
        self.nsem = 0
        for e in COMPUTE:
            self._new_epoch(e)
        self.dsem = {q: [self._mksem("d%s%d" % (q, i)) for i in range(n_dma_sems)] for q in ("sp", "pool")}
        self.dval = {q: [0] * n_dma_sems for q in ("sp", "pool")}
        self.drr = {"sp": 0, "pool": 0}
        self.waited = {e: {} for e in COMPUTE + ("sp",)}
        self.n_ops = 0
        self.last_dma_toks = {}
        self.last_tok = {}

    def _mksem(self, name):
        s = self.stack.enter_context(self.nc.semaphore(name))
        k = self.nsem
        self.nsem += 1
        self.semobj[k] = s
        return k

    def _new_epoch(self, e):
        self.sems[e] = self._mksem("s_%s_%d" % (e, self.nsem))
        self.own[e].add(self.sems[e])
        self.cnt[e] = 0

    def _collect(self, eng, R, W):
        deps = {}

        def add(tok):
            if tok is None:
                return
            k, v = tok
            if deps.get(k, 0) < v:
                deps[k] = v
        for b in R:
            add(b.lw)
            if b.excl:
                mine = self.own.get(eng, ())
                for t in b.rd.items():
                    if t[0] not in mine:
                        add(t)
        for b in W:
            add(b.lw)
            for t in b.rd.items():
                add(t)
        waits = []
        wd = self.waited[eng]
        own = self.own["pe"] if eng == "pe" else ()
        for k, v in deps.items():
            if k in own:
                continue
            if wd.get(k, 0) < v:
                wd[k] = v
                waits.append((k, v))
        return waits

    def _commit(self, tok, R, W):
        for b in R:
            if b.rd.get(tok[0], 0) < tok[1]:
                b.rd[tok[0]] = tok[1]
        for b in W:
            b.lw = tok
            b.rd = {}

    def _eng(self, e):
        nc = self.nc
        return {"pe": nc.tensor, "act": nc.scalar, "dve": nc.vector, "pool": nc.gpsimd, "sp": nc.sync}[e]

    def _issue(self, eng, waits, fn, inc):
        e = self._eng(eng)
        for k, v in waits:
            e.wait_ge(self.semobj[k], v)
        if fn is not None:
            ins = fn(e)
            ins.then_inc(self.semobj[inc[0]], inc[1])

    def op(self, eng, fn, R=(), W=()):
        waits = self._collect(eng, R, W)
        if self.cnt[eng] >= SEM_EPOCH:
            self._new_epoch(eng)
        self.cnt[eng] += 1
        tok = (self.sems[eng], self.cnt[eng])
        self.last_tok[eng] = tok
        self._issue(eng, waits, fn, (tok[0], 1))
        self._commit(tok, R, W)
        self.n_ops += 1
        return tok

    def dma(self, out_ap, in_ap, R=(), W=(), q="sp"):
        s = self.drr[q]
        self.drr[q] = (s + 1) % len(self.dsem[q])
        k = self.dsem[q][s]
        dv = self.dval[q]
        waits = self._collect(q, R, W)
        wd = self.waited[q]
        if dv[s] > 0 and wd.get(k, 0) < dv[s]:
            wd[k] = dv[s]
            waits.append((k, dv[s]))
        dv[s] += 16
        tok = (k, dv[s])
        self._issue(q, waits, lambda e: e.dma_start(out=out_ap, in_=in_ap), (k, 16))
        self._commit(tok, R, W)
        self.n_ops += 1
        self.last_dma_toks[k] = tok
        return tok

    def wait_tokens(self, toks, eng="sp"):
        waits = []
        wd = self.waited[eng]
        for k, v in toks:
            if wd.get(k, 0) < v:
                wd[k] = v
                waits.append((k, v))
        self._issue(eng, waits, None, None)

    def barrier(self):
        toks = [self.last_tok[e] for e in COMPUTE if e in self.last_tok]
        for q in ("sp", "pool"):
            toks += [(self.dsem[q][i], self.dval[q][i]) for i in range(len(self.dsem[q])) if self.dval[q][i] > 0]
        for e in COMPUTE + ("sp",):
            self.wait_tokens(toks, e)

    def final_wait(self, toks=None, eng="sp"):
        self.barrier()


D = 2048
NKC = 16
FH = 5632
NHC = 44
INW = 13872
EPS = 1e-6
OFF = dict(pool=0, q_at=512, k_at=1536, v_at=1792, qkv=2048, z=3584, a=4096, b=4104,
           q_gl=4112, k_gl=4368, v_gl=4624, r_gl=5136, lr=5648, gate=5680)
NYC = 20
SLOT = 8192


class Ctx:
    pass


def sb(K, st, shape, dt, name=None):
    K.uid += 1
    return Buf(st.enter_context(K.nc.sbuf_tensor("%s_%d" % (name or "t", K.uid), list(shape), dt)), name or "t")


def mm(K, pb, out_ap, lb, l_ap, rb, r_ap, start=True, stop=True):
    K.S.op("pe", lambda e: e.matmul(out_ap, l_ap, r_ap, start=start, stop=stop), R=[lb, rb], W=[pb])


def tr(K, pb, out_ap, ib, in_ap, idb, id_ap):
    K.S.op("pe", lambda e: e.transpose(out_ap, in_ap, id_ap), R=[ib, idb], W=[pb])


def act(K, out_ap, in_ap, func, R, W, bias=None, scale=None):
    kw = {}
    if bias is not None:
        kw["bias"] = bias
    if scale is not None:
        kw["scale"] = scale
    K.S.op("act", lambda e: e.activation(out=out_ap, in_=in_ap, func=func, **kw), R=R, W=W)


def tt(K, eng, out_ap, a_ap, b_ap, op, R, W):
    K.S.op(eng, lambda e: e.tensor_tensor(out=out_ap, in0=a_ap, in1=b_ap, op=op), R=R, W=W)


def ts(K, eng, out_ap, a_ap, s1, s2, op0, op1, R, W):
    if s2 is None:
        K.S.op(eng, lambda e: e.tensor_scalar(out=out_ap, in0=a_ap, scalar1=s1, scalar2=None, op0=op0), R=R, W=W)
    else:
        K.S.op(eng, lambda e: e.tensor_scalar(out=out_ap, in0=a_ap, scalar1=s1, scalar2=s2, op0=op0, op1=op1), R=R, W=W)


def stt(K, eng, out_ap, a_ap, s, b_ap, op0, op1, R, W):
    K.S.op(eng, lambda e: e.scalar_tensor_tensor(out=out_ap, in0=a_ap, scalar=s, in1=b_ap, op0=op0, op1=op1), R=R, W=W)


def rsqrt(K, ob, out_ap, ib, in_ap, scale):
    ts(K, "dve", out_ap, in_ap, scale, EPS, ALU.mult, ALU.add, R=[ib], W=[ob])
    act(K, out_ap, out_ap, AF.Sqrt, R=[ob], W=[ob])
    K.S.op("dve", lambda e: e.reciprocal(out=out_ap, in_=out_ap), R=[ob], W=[ob])


def cp(K, eng, out_ap, in_ap, R, W):
    if eng == "act":
        K.S.op("act", lambda e: e.activation(out=out_ap, in_=in_ap, func=AF.Copy), R=R, W=W)
    else:
        K.S.op(eng, lambda e: e.tensor_copy(out=out_ap, in_=in_ap), R=R, W=W)


def wslot(K, i, k, c):
    return K.ring[i][:, 0:k * c].rearrange("p (k c) -> p k c", k=k)


class Ring:
    def __init__(self, K, plan, slots=None, group=1):
        self.K = K
        self.slots = slots if slots is not None else K.ring
        self.i = 0
        self.plan = list(plan)
        self.issued = []
        self.inuse = []
        self.group = group
        self.nxt = 0
        self._fill()

    def _fill(self):
        while self.nxt < len(self.plan) and len(self.issued) + len(self.inuse) < len(self.slots):
            wbuf, w_ap, k, c = self.plan[self.nxt]
            self.nxt += 1
            slot = self.slots[self.i]
            self.i = (self.i + 1) % len(self.slots)
            v = slot[:, 0:k * c].rearrange("p (k c) -> p k c", k=k)
            self.K.S.dma(v, w_ap, R=[wbuf], W=[slot], q="pool")
            self.issued.append((slot, v, k, c))

    def load(self, wbuf=None, w_ap=None, k=None, c=None):
        if len(self.inuse) == self.group:
            self.inuse.pop(0)
        self._fill()
        slot, v, k_, c_ = self.issued.pop(0)
        assert (k is None or (k, c) == (k_, c_)), ((k, c), (k_, c_))
        self.inuse.append(slot)
        return slot, v


def tiles_of(K):
    out = []
    t = 0
    while t < K.T_S:
        n = min(512, K.T_S - t)
        out.append((t, n, 0))
        t += n
    tot = K.NT
    while t < tot:
        n = min(512, tot - t)
        out.append((t, n, 1))
        t += n
    return out


def supertiles(K, maxtok=1024):
    out, cur, tot = [], [], 0
    for tl in tiles_of(K):
        if tot + tl[1] > maxtok and cur:
            out.append(cur)
            cur, tot = [], 0
        cur.append(tl)
        tot += tl[1]
    if cur:
        out.append(cur)
    return out


def phase_in(K):
    S = K.S
    with ExitStack() as st:
        xin = [sb(K, st, [128, D], F32, "xin") for _ in range(2)]
        xT = [sb(K, st, [128, NKC, 128], F32, "xT") for _ in range(2)]
        for blk in range(K.NT // 128):
            t0 = blk * 128
            if t0 < K.T_S:
                src, r0 = K.x_s, t0
            else:
                src, r0 = K.x_p, t0 - K.T_S
            xi = xin[blk % 2]
            xo = xT[blk % 2]
            S.dma(xi[:], src[r0:r0 + 128, :], R=[src], W=[xi])
            for g in range(4):
                pb = K.PS[g % 4]
                for c in range(4):
                    cc = g * 4 + c
                    tr(K, pb, pb[:, c * 128:(c + 1) * 128], xi, xi[:, cc * 128:(cc + 1) * 128], K.identF, K.identF[:])
                cp(K, "act" if g % 2 else "dve", xo[:, 4 * g:4 * g + 4, :],
                   pb[:, :].rearrange("p (c t) -> p c t", c=4), R=[pb], W=[xo])
            S.dma(K.XS[:, :, t0:t0 + 128].rearrange("c p t -> p c t"), xo[:], R=[xo], W=K.XSB)
        S.barrier()


def phase_mod(K, l):
    S = K.S
    with ExitStack() as st:
        ring = Ring(K, [(K.w_mod, K.w_mod[l, :, cb * 512:(cb + 1) * 512].rearrange("(k p) c -> p k c", p=128), NKC, 512)
                        for cb in range(24)])
        pb = K.PS[6]
        for cb in range(24):
            wb, w = ring.load(K.w_mod, K.w_mod[l, :, cb * 512:(cb + 1) * 512].rearrange("(k p) c -> p k c", p=128), NKC, 512)
            for m in range(4):
                j = cb * 4 + m
                for kc in range(NKC):
                    mm(K, pb, pb[:, j:j + 97:96], wb, w[:, kc, m * 128:(m + 1) * 128], K.sc, K.sc[:, kc, :],
                       start=(kc == 0), stop=(kc == NKC - 1))
        for s in range(2):
            tt(K, "dve", K.mod[:, s, :], pb[:, s * 96:(s + 1) * 96], K.bmodT[:, l, :], ALU.add,
               R=[pb, K.bmodT], W=[K.mod])
        for s in range(2):
            stt(K, "dve", K.A1[:, s, :], K.mod[:, s, 16:32], 1.0, K.norm1T[:, l, :], ALU.add, ALU.mult,
                R=[K.mod, K.norm1T], W=[K.A1])
            stt(K, "dve", K.A2[:, s, :], K.mod[:, s, 64:80], 1.0, K.norm2T[:, l, :], ALU.add, ALU.mult,
                R=[K.mod, K.norm2T], W=[K.A2])
        S.barrier()


def phase_norm(K, l, which):
    S = K.S
    with ExitStack() as st:
        xb = [sb(K, st, [128, NKC, 512], F32, "nx") for _ in range(2)]
        sq = sb(K, st, [128, NKC, 512], BF16, "nsq")
        rr = sb(K, st, [128, 512], F32, "nr")
        tmp = [sb(K, st, [128, 512], F32, "ntmp") for _ in range(2)]
        if which == "f":
            hf = sb(K, st, [128, NKC, 512], F32, "nhf")
            yo = [sb(K, st, [128, D], F32, "nyo") for _ in range(2)]
        else:
            hb = [sb(K, st, [128, NKC, 512], BF16, "nh") for _ in range(2)]
        for ti, (t0, n, s) in enumerate(tiles_of(K)):
            x = xb[ti % 2]
            S.dma(x[:, :, :n], K.XS[:, :, t0:t0 + n].rearrange("c p t -> p c t"), R=K.XSB, W=[x])
            for g in range(4):
                act(K, sq[:, 4 * g:4 * g + 4, :n], x[:, 4 * g:4 * g + 4, :n], AF.Square, R=[x], W=[sq])
            pb = K.PS[4 + ti % 2]
            for c in range(NKC):
                mm(K, pb, pb[:, :n], K.onesB, K.onesB[:], sq, sq[:, c, :n], start=(c == 0), stop=(c == NKC - 1))
            rsqrt(K, rr, rr[:, :n], pb, pb[:, :n], 1.0 / D)
            if which == "f":
                for c in range(NKC):
                    stt(K, "dve", hf[:, c, :n], x[:, c, :n], K.normfT[:, c:c + 1], rr[:, :n],
                        ALU.mult, ALU.mult, R=[x, rr, K.normfT], W=[hf])
                for b in range(n // 128):
                    y = yo[b % 2]
                    for g in range(4):
                        pb2 = K.PS[g % 4]
                        for c in range(4):
                            cc = 4 * g + c
                            tr(K, pb2, pb2[:, c * 128:(c + 1) * 128], hf, hf[:, cc, b * 128:(b + 1) * 128],
                               K.identF, K.identF[:])
                        cp(K, "act" if g % 2 else "dve", y[:, g * 512:(g + 1) * 512], pb2[:, :], R=[pb2], W=[y])
                    tg = t0 + b * 128
                    if tg < K.T_S:
                        S.dma(K.y_s[tg:tg + 128, :], y[:], R=[y], W=[K.y_s])
                    else:
                        S.dma(K.y_p[tg - K.T_S:tg - K.T_S + 128, :], y[:], R=[y], W=[K.y_p])
            else:
                A = K.A1 if which == 1 else K.A2
                bo = 0 if which == 1 else 48
                h = hb[ti % 2]
                for c in range(NKC):
                    tm = tmp[c % 2]
                    tt(K, "dve", tm[:, :n], x[:, c, :n], rr[:, :n], ALU.mult, R=[x, rr], W=[tm])
                    act(K, h[:, c, :n], tm[:, :n], AF.Identity, R=[tm, A, K.mod], W=[h],
                        scale=A[:, s, c:c + 1], bias=K.mod[:, s, bo + c:bo + c + 1])
                S.dma(K.HD[:, :, t0:t0 + n].rearrange("c p t -> p c t"), h[:, :, :n], R=[h], W=[K.HD])
        S.barrier()


def phase_merge(K, l):
    S = K.S
    brs = [(K.w_br_pool, 0, 4), (K.w_br_attn, 4, 8), (K.w_br_delta, 12, 4), (K.w_br_gla, 16, 4)]
    steps = [("m", mgp, bi) for mgp in range(4) for bi in range(4)] + [("o", mgp, 0) for mgp in range(4)]

    def views(i):
        kind, mgp, bi = steps[i]
        slot = K.ring2[i % 2]
        gw = slot[:, 0:8192].rearrange("p (k c) -> p k c", k=NKC)
        if kind == "m":
            nk = brs[bi][2]
            bw = slot[:, 8192:8192 + nk * 512].rearrange("p (k c) -> p k c", k=nk)
        else:
            bw = None
        return slot, gw, bw

    def load(i):
        kind, mgp, bi = steps[i]
        slot, gw, bw = views(i)
        if kind == "m":
            wbr = brs[bi][0]
            c0 = OFF["gate"] + bi * D + mgp * 512
            S.dma(gw, K.w_in[l, :, c0:c0 + 512].rearrange("(k p) c -> p k c", p=128), R=[K.w_in], W=[slot], q="pool")
            S.dma(bw, wbr[l, :, mgp * 512:(mgp + 1) * 512].rearrange("(k p) c -> p k c", p=128), R=[wbr], W=[slot], q="pool")
        else:
            S.dma(gw, K.w_out[l, :, mgp * 512:(mgp + 1) * 512].rearrange("(k p) c -> p k c", p=128), R=[K.w_out], W=[slot], q="pool")

    for stl in supertiles(K):
        T0 = stl[0][0]
        TN = sum(t[1] for t in stl)
        with ExitStack() as st:
            h = sb(K, st, [128, NKC, TN], BF16, "mh")
            y = sb(K, st, [128, NYC, TN], BF16, "my")
            mg = sb(K, st, [128, NKC, TN], BF16, "mmg")
            acc = sb(K, st, [128, 4, TN], F32, "macc")
            sg = [sb(K, st, [128, 512], F32, "msg") for _ in range(2)]
            t2 = [sb(K, st, [128, 512], F32, "mt2") for _ in range(2)]
            xt = [sb(K, st, [128, 512], F32, "mxt") for _ in range(2)]
            load(0)
            S.dma(h[:], K.HD[:, :, T0:T0 + TN].rearrange("c p t -> p c t"), R=[K.HD], W=[h])
            S.dma(y[:], K.YD[:, :, T0:T0 + TN].rearrange("c p t -> p c t"), R=K.YDB, W=[y])
            cnt = 0
            for i, (kind, mgp, bi) in enumerate(steps):
                if i + 1 < len(steps):
                    load(i + 1)
                slot, gw, bw = views(i)
                if kind == "m":
                    wbr, yc0, nk = brs[bi]
                    for m in range(4):
                        for (t0, n, s) in stl:
                            o = t0 - T0
                            pg = K.PS[cnt % 2]
                            pbr = K.PS[2 + cnt % 2]
                            for kc in range(NKC):
                                mm(K, pg, pg[:, :n], slot, gw[:, kc, m * 128:(m + 1) * 128], h, h[:, kc, o:o + n],
                                   start=(kc == 0), stop=(kc == NKC - 1))
                            for kc in range(nk):
                                mm(K, pbr, pbr[:, :n], slot, bw[:, kc, m * 128:(m + 1) * 128], y, y[:, yc0 + kc, o:o + n],
                                   start=(kc == 0), stop=(kc == nk - 1))
                            sgt = sg[cnt % 2]
                            act(K, sgt[:, :n], pg[:, :n], AF.Sigmoid, R=[pg], W=[sgt])
                            if bi == 0:
                                tt(K, "dve", acc[:, m, o:o + n], pbr[:, :n], sgt[:, :n], ALU.mult, R=[pbr, sgt], W=[acc])
                            else:
                                tq = t2[cnt % 2]
                                tt(K, "dve", tq[:, :n], pbr[:, :n], sgt[:, :n], ALU.mult, R=[pbr, sgt], W=[tq])
                                if bi < 3:
                                    tt(K, "dve", acc[:, m, o:o + n], acc[:, m, o:o + n], tq[:, :n], ALU.add,
                                       R=[acc, tq], W=[acc])
                                else:
                                    tt(K, "dve", mg[:, mgp * 4 + m, o:o + n], acc[:, m, o:o + n], tq[:, :n], ALU.add,
                                       R=[acc, tq], W=[mg])
                            cnt += 1
                else:
                    for m in range(4):
                        mo = mgp * 4 + m
                        for (t0, n, s) in stl:
                            o = t0 - T0
                            po = K.PS[4 + cnt % 2]
                            x = xt[cnt % 2]
                            S.dma(x[:, :n], K.XS[mo, :, t0:t0 + n], R=[K.XSB[mo]], W=[x])
                            for kc in range(NKC):
                                mm(K, po, po[:, :n], slot, gw[:, kc, m * 128:(m + 1) * 128], mg, mg[:, kc, o:o + n],
                                   start=(kc == 0), stop=(kc == NKC - 1))
                            stt(K, "dve", x[:, :n], po[:, :n], K.mod[:, s, 32 + mo:33 + mo], x[:, :n], ALU.mult, ALU.add,
                                R=[po, x, K.mod], W=[x])
                            S.dma(K.XS[mo, :, t0:t0 + n], x[:, :n], R=[x], W=[K.XSB[mo]])
                            cnt += 1
            S.barrier()


def phase_ffn(K, l):
    S = K.S
    wv = lambda w, r0, r1, c0, c1: w[l, r0:r1, c0:c1].rearrange("(k p) c -> p k c", p=128)
    planA = []
    for hp in range(NHC // 2):
        planA.append((K.w_gate, wv(K.w_gate, 0, D, hp * 256, (hp + 1) * 256), NKC, 256))
        planA.append((K.w_up, wv(K.w_up, 0, D, hp * 256, (hp + 1) * 256), NKC, 256))
    planB = []
    for mg2 in range(8):
        planB.append((K.w_down, wv(K.w_down, 0, 22 * 128, mg2 * 256, (mg2 + 1) * 256), 22, 256))
        planB.append((K.w_down, wv(K.w_down, 22 * 128, 44 * 128, mg2 * 256, (mg2 + 1) * 256), 22, 256))
    for stl in supertiles(K):
        T0 = stl[0][0]
        TN = sum(t[1] for t in stl)
        with ExitStack() as st:
            h = sb(K, st, [128, NKC, TN], BF16, "fh")
            a = sb(K, st, [128, NHC, TN], BF16, "fa")
            sg = [sb(K, st, [128, 512], F32, "fsg") for _ in range(2)]
            xt = [sb(K, st, [128, 512], F32, "fxt") for _ in range(2)]
            ring = Ring(K, planA, slots=K.ring6, group=2)
            S.dma(h[:], K.HD[:, :, T0:T0 + TN].rearrange("c p t -> p c t"), R=[K.HD], W=[h])
            cnt = 0
            for hp in range(NHC // 2):
                gb, gw = ring.load()
                ub, uw = ring.load()
                for m in range(2):
                    hc = hp * 2 + m
                    for (t0, n, s) in stl:
                        o = t0 - T0
                        pg = K.PS[cnt % 2]
                        pu = K.PS[2 + cnt % 2]
                        for kc in range(NKC):
                            mm(K, pg, pg[:, :n], gb, gw[:, kc, m * 128:(m + 1) * 128], h, h[:, kc, o:o + n],
                               start=(kc == 0), stop=(kc == NKC - 1))
                        for kc in range(NKC):
                            mm(K, pu, pu[:, :n], ub, uw[:, kc, m * 128:(m + 1) * 128], h, h[:, kc, o:o + n],
                               start=(kc == 0), stop=(kc == NKC - 1))
                        sgt = sg[cnt % 2]
                        act(K, sgt[:, :n], pg[:, :n], AF.Silu, R=[pg], W=[sgt])
                        tt(K, "dve", a[:, hc, o:o + n], pu[:, :n], sgt[:, :n], ALU.mult, R=[pu, sgt], W=[a])
                        cnt += 1
            S.barrier()
            ring = Ring(K, planB, slots=K.ring4, group=2)
            for mg2 in range(8):
                d0b, d0w = ring.load()
                d1b, d1w = ring.load()
                for m in range(2):
                    mo = mg2 * 2 + m
                    for (t0, n, s) in stl:
                        o = t0 - T0
                        po = K.PS[4 + cnt % 2]
                        x = xt[cnt % 2]
                        S.dma(x[:, :n], K.XS[mo, :, t0:t0 + n], R=[K.XSB[mo]], W=[x])
                        for kc in range(NHC):
                            db, dw = (d0b, d0w) if kc < 22 else (d1b, d1w)
                            mm(K, po, po[:, :n], db, dw[:, kc % 22, m * 128:(m + 1) * 128], a, a[:, kc, o:o + n],
                               start=(kc == 0), stop=(kc == NHC - 1))
                        stt(K, "dve", x[:, :n], po[:, :n], K.mod[:, s, 80 + mo:81 + mo], x[:, :n], ALU.mult, ALU.add,
                            R=[po, x, K.mod], W=[x])
                        S.dma(K.XS[mo, :, t0:t0 + n], x[:, :n], R=[x], W=[K.XSB[mo]])
                        cnt += 1
            S.barrier()


def phase_zero_y(K):
    S = K.S
    with ExitStack() as st:
        z = sb(K, st, [128, NYC, 512], BF16, "zy")
        S.op("dve", lambda e: e.memset(z[:], 0.0), W=[z])
        for (t0, n, s) in tiles_of(K):
            S.dma(K.YD[:, :, t0:t0 + n].rearrange("c p t -> p c t"), z[:, :, :n], R=[z], W=K.YDB)
        S.barrier()


def build_program(cfg):
    nc = bass.Bass("TRN2", target_bir_lowering=False)
    K = Ctx()
    K.nc = nc
    K.uid = 0
    K.cfg = cfg
    K.T_S, K.NP, K.T_P, K.L = cfg["T_S"], cfg["NP"], cfg["T_P"], cfg["L"]
    K.NT = K.T_S + K.NP * K.T_P
    L = K.L

    def ein(name, shape, dt=F32):
        b = Buf(nc.dram_tensor(name, list(shape), dt, kind="ExternalInput").ap(), name)
        setattr(K, name, b)
        return b

    def eout(name, shape):
        b = Buf(nc.dram_tensor(name, list(shape), F32, kind="ExternalOutput").ap(), name)
        setattr(K, name, b)
        return b

    def scratch(name, shape, dt):
        b = Buf(nc.dram_tensor(name, list(shape), dt).ap(), name)
        setattr(K, name, b)
        return b

    ein("x_s", [K.T_S, D])
    ein("x_p", [K.NP * K.T_P, D])
    ein("condT", [128, NKC, 2])
    ein("bmodT_d", [128, L, 96])
    ein("norm1T_d", [128, L, NKC])
    ein("norm2T_d", [128, L, NKC])
    ein("normfT_d", [128, NKC])
    ein("identF_d", [128, 128])
    ein("w_mod", [L, D, 6 * D])
    ein("w_in", [L, D, INW])
    ein("w_br_pool", [L, 512, D])
    ein("w_br_attn", [L, 1024, D])
    ein("w_br_delta", [L, 512, D])
    ein("w_br_gla", [L, 512, D])
    ein("w_out", [L, D, D])
    ein("w_gate", [L, D, FH])
    ein("w_up", [L, D, FH])
    ein("w_down", [L, FH, D])
    ein("pool_w", [L, 4, 128, 128])
    ein("pscT_d", [128, L, 4])
    ein("rcnt_S", [4, K.T_S])
    ein("rcnt_P", [4, K.T_P])
    ein("cos_d", [128, K.T_S])
    ein("sin_d", [128, K.T_S])
    ein("RT_d", [128, 128])
    ein("maskL_d", [128, 4, 128])
    ein("maskR_d", [128, 4, 128])
    ein("sink_rep", [L, 1, 1024])
    ein("cache_k", [L, 256, 256])
    ein("cache_v", [L, 256, 256])
    ein("state_delta", [L, 2, 4, 128, 128])
    ein("state_gla", [L, 2, 4, 64, 128])
    mix_inputs(K, ein)
    eout("nk", [K.NP, L, K.T_P, 256])
    eout("nv", [K.NP, L, K.T_P, 256])
    eout("nsd", [K.NP, L, 2, 4, 128, 128])
    eout("nsg", [K.NP, L, 2, 4, 64, 128])
    eout("y_s", [K.T_S, D])
    eout("y_p", [K.NP * K.T_P, D])
    scratch("XS", [NKC, 128, K.NT], F32)
    scratch("HD", [NKC, 128, K.NT], BF16)
    scratch("YD", [NYC, 128, K.NT], BF16)
    K.YDB = [Buf(K.YD.t, "yd%d" % i) for i in range(NYC)]
    K.XSB = [Buf(K.XS.t, "xs%d" % i) for i in range(NKC)]

    with ExitStack() as top:
        S = K.S = Sched(nc, top)
        K.PS = [Buf(top.enter_context(nc.psum_tensor("ps%d" % i, [128, 512], F32)), "ps%d" % i, excl=True) for i in range(7)]
        K.PSQ = [[Buf(K.PS[i].t, "ps%dq%d" % (i, q)) for q in range(4)] for i in range(7)]
        K.PB = Buf(top.enter_context(nc.psum_tensor("psb", [128, 1024], BF16)), "psb", excl=True)
        K.identF = sb(K, top, [128, 128], F32, "identF")
        K.onesB = sb(K, top, [128, 128], BF16, "onesB")
        K.sc = sb(K, top, [128, NKC, 2], BF16, "sc")
        K.mod = sb(K, top, [128, 2, 96], F32, "mod")
        K.A1 = sb(K, top, [128, 2, NKC], F32, "A1")
        K.A2 = sb(K, top, [128, 2, NKC], F32, "A2")
        K.bmodT = sb(K, top, [128, L, 96], F32, "bmodT")
        K.norm1T = sb(K, top, [128, L, NKC], F32, "norm1T")
        K.norm2T = sb(K, top, [128, L, NKC], F32, "norm2T")
        K.normfT = sb(K, top, [128, NKC], F32, "normfT")
        ringmem = sb(K, top, [128, 3 * SLOT], BF16, "ringmem")
        K.ring = [Buf(ringmem.t[:, i * SLOT:(i + 1) * SLOT], "ring%d" % i) for i in range(3)]
        K.ring2 = [Buf(ringmem.t[:, i * 12288:(i + 1) * 12288], "ringb%d" % i) for i in range(2)]
        K.ring6 = [Buf(ringmem.t[:, i * 4096:(i + 1) * 4096], "ringc%d" % i) for i in range(6)]
        K.ring4 = [Buf(ringmem.t[:, i * 6144:(i + 1) * 6144], "ringd%d" % i) for i in range(4)]
        K.pscT = sb(K, top, [128, L, 4], F32, "pscT")
        K.RT = sb(K, top, [128, 128], BF16, "RT")
        K.maskL = sb(K, top, [128, 4, 128], BF16, "maskL")
        K.maskR = sb(K, top, [128, 4, 128], BF16, "maskR")
        S.dma(K.pscT[:], K.pscT_d[:, :, :], R=[K.pscT_d], W=[K.pscT])
        S.dma(K.RT[:], K.RT_d[:, :], R=[K.RT_d], W=[K.RT], q="pool")
        S.dma(K.maskL[:], K.maskL_d[:, :, :], R=[K.maskL_d], W=[K.maskL], q="pool")
        S.dma(K.maskR[:], K.maskR_d[:, :, :], R=[K.maskR_d], W=[K.maskR], q="pool")
        mix_consts(K, top)
        cT = sb(K, top, [128, NKC, 2], F32, "cT")
        S.dma(K.identF[:], K.identF_d[:, :], R=[K.identF_d], W=[K.identF])
        S.dma(cT[:], K.condT[:, :, :], R=[K.condT], W=[cT])
        S.dma(K.bmodT[:], K.bmodT_d[:, :, :], R=[K.bmodT_d], W=[K.bmodT])
        S.dma(K.norm1T[:], K.norm1T_d[:, :, :], R=[K.norm1T_d], W=[K.norm1T])
        S.dma(K.norm2T[:], K.norm2T_d[:, :, :], R=[K.norm2T_d], W=[K.norm2T])
        S.dma(K.normfT[:], K.normfT_d[:, :], R=[K.normfT_d], W=[K.normfT])
        S.op("dve", lambda e: e.memset(K.onesB[:], 1.0), W=[K.onesB])
        act(K, K.sc[:], cT[:], AF.Silu, R=[cT], W=[K.sc])

        phase_in(K)
        for l in range(L):
            phase_mod(K, l)
            phase_norm(K, l, 1)
            if cfg.get("mixers", True):
                phase_mixers(K, l)
            else:
                phase_zero_y(K)
            phase_merge(K, l)
            phase_norm(K, l, 2)
            phase_ffn(K, l)
        phase_norm(K, L, "f")
        S.final_wait()
    K.n_ops = S.n_ops
    return nc, K


def host_common(inp, L):
    f = np.float32
    d = {}
    d["bmodT_d"] = np.ascontiguousarray(inp["b_mod"][:L].reshape(L, 96, 128).transpose(2, 0, 1)).astype(f)
    d["norm1T_d"] = np.ascontiguousarray(inp["norm1"][:L].reshape(L, NKC, 128).transpose(2, 0, 1)).astype(f)
    d["norm2T_d"] = np.ascontiguousarray(inp["norm2"][:L].reshape(L, NKC, 128).transpose(2, 0, 1)).astype(f)
    d["normfT_d"] = np.ascontiguousarray(inp["norm_f"].reshape(NKC, 128).T).astype(f)
    d["identF_d"] = np.eye(128, dtype=f)
    for k in ("w_mod", "w_in", "w_br_pool", "w_br_attn", "w_br_delta", "w_br_gla", "w_out", "w_gate", "w_up", "w_down"):
        d[k] = np.ascontiguousarray(inp[k][:L])
    return d


def host_core(inp, common, b_s, p_list, L):
    d = dict(common)
    d["x_s"] = np.ascontiguousarray(inp["x_sample"][b_s])
    d["x_p"] = np.ascontiguousarray(np.concatenate([inp["x_prompt"][p] for p in p_list], axis=0))
    cond = np.stack([inp["c"][b_s], inp["c_ctx"]], axis=0)
    d["condT"] = np.ascontiguousarray(cond.reshape(2, NKC, 128).transpose(2, 1, 0)).astype(np.float32)
    return d


def seqs_of(K):
    out = [("S", 0, K.T_S, 0)]
    for p in range(K.NP):
        out.append(("P", K.T_S + p * K.T_P, K.T_P, p))
    return out


def seq_tiles(T):
    return [(t, min(512, T - t)) for t in range(0, T, 512)]


def load_h(K, st, t0, T):
    h = sb(K, st, [128, NKC, T], BF16, "H")
    K.S.dma(h[:], K.HD[:, :, t0:t0 + T].rearrange("c p t -> p c t"), R=[K.HD], W=[h])
    return h


def win_cols(K, l, c0, n):
    return K.w_in[l, :, c0:c0 + n].rearrange("(k p) c -> p k c", p=128)


def mixer_pool(K, l, seq):
    S = K.S
    kind, t0, T, idx = seq
    ring = Ring(K, [(K.w_in, win_cols(K, l, OFF["pool"], 512), NKC, 512)])
    rc_d = K.rcnt_S if kind == "S" else K.rcnt_P
    with ExitStack() as st:
        h = load_h(K, st, t0, T)
        u = sb(K, st, [128, T + 16], F32, "pu")
        s1 = sb(K, st, [128, T + 16], F32, "ps1")
        s2 = sb(K, st, [128, T + 16], F32, "ps2")
        rc = sb(K, st, [128, T], F32, "prc")
        pbf = sb(K, st, [128, T], BF16, "ppb")
        pw = sb(K, st, [128, 4, 128], BF16, "ppw")
        yo = [sb(K, st, [128, 512], BF16, "pyo") for _ in range(2)]
        S.dma(pw[:], K.pool_w[l].rearrange("g c d -> c g d"), R=[K.pool_w], W=[pw], q="pool")
        wb, w = ring.load(K.w_in, win_cols(K, l, OFF["pool"], 512), NKC, 512)
        cnt = 0
        for g, win in enumerate((2, 4, 8, 16)):
            S.op("pool", lambda e: e.memset(u[:, 0:8], 0.0), W=[u])
            S.op("pool", lambda e: e.memset(u[:, T + 8:T + 16], 0.0), W=[u])
            S.dma(rc[:], rc_d[g:g + 1, :].partition_broadcast(128), R=[rc_d], W=[rc])
            for (o, n) in seq_tiles(T):
                pb = K.PS[cnt % 2]
                cnt += 1
                for kc in range(NKC):
                    mm(K, pb, pb[:, :n], wb, w[:, kc, g * 128:(g + 1) * 128], h, h[:, kc, o:o + n],
                       start=(kc == 0), stop=(kc == NKC - 1))
                cp(K, "act", u[:, 8 + o:8 + o + n], pb[:, :n], R=[pb], W=[u])
            N = T + 16
            tt(K, "dve", s1[:, 1:N], u[:, 0:N - 1], u[:, 1:N], ALU.add, R=[u], W=[s1])
            cur = s1
            if win >= 4:
                tt(K, "dve", s2[:, 2:N - 1], s1[:, 1:N - 2], s1[:, 3:N], ALU.add, R=[s1], W=[s2])
                cur = s2
            if win >= 8:
                tt(K, "dve", s1[:, 4:N - 3], s2[:, 2:N - 5], s2[:, 6:N - 1], ALU.add, R=[s2], W=[s1])
                cur = s1
            if win >= 16:
                tt(K, "dve", s2[:, 8:N - 8], s1[:, 4:N - 12], s1[:, 12:N - 4], ALU.add, R=[s1], W=[s2])
                cur = s2
            oth = s1 if cur is s2 else s2
            tt(K, "dve", oth[:, 8:8 + T], cur[:, 8:8 + T], rc[:], ALU.mult, R=[cur, rc], W=[oth])
            tt(K, "dve", pbf[:], oth[:, 8:8 + T], u[:, 8:8 + T], ALU.subtract, R=[oth, u], W=[pbf])
            for (o, n) in seq_tiles(T):
                pb = K.PS[2 + cnt % 2]
                y = yo[cnt % 2]
                cnt += 1
                mm(K, pb, pb[:, :n], pw, pw[:, g, :], pbf, pbf[:, o:o + n])
                ts(K, "dve", y[:, :n], pb[:, :n], K.pscT[:, l, g:g + 1], None, ALU.mult, None, R=[pb, K.pscT], W=[y])
                S.dma(K.YD[g, :, t0 + o:t0 + o + n], y[:, :n], R=[y], W=[K.YDB[g]])
        S.barrier()


def mixer_attn(K, l, seq):
    S = K.S
    kind, t0, T, idx = seq
    ring = Ring(K, [(K.w_in, win_cols(K, l, OFF["q_at"], 512), NKC, 512),
                    (K.w_in, win_cols(K, l, OFF["q_at"] + 512, 512), NKC, 512),
                    (K.w_in, win_cols(K, l, OFF["k_at"], 256), NKC, 256),
                    (K.w_in, win_cols(K, l, OFF["k_at"], 512), NKC, 512)])
    rope = (kind == "S")
    nb = T // 128
    with ExitStack() as outer:
        qT = sb(K, outer, [128, 8, T], BF16, "aq")
        kT = sb(K, outer, [128, 2, T], BF16, "ak")
        vt = sb(K, outer, [128, nb, 256], BF16, "av")
        with ExitStack() as st:
            h = load_h(K, st, t0, T)
            qb = [sb(K, st, [128, 512], BF16, "aqb") for _ in range(2)]
            t1 = [sb(K, st, [128, 512], F32, "at1") for _ in range(2)]
            t2 = [sb(K, st, [128, 512], F32, "at2") for _ in range(2)]
            kvo = [sb(K, st, [128, 512], F32, "akvo") for _ in range(2)]
            if rope:
                cs = sb(K, st, [128, T], BF16, "acos")
                sn = sb(K, st, [128, T], BF16, "asin")
                S.dma(cs[:], K.cos_d[:, :], R=[K.cos_d], W=[cs], q="pool")
                S.dma(sn[:], K.sin_d[:, :], R=[K.sin_d], W=[sn], q="pool")
            cnt = 0
            for hh in range(10):
                if hh % 4 == 0:
                    ncols = 512 if hh < 8 else 256
                    wb, w = ring.load(K.w_in, win_cols(K, l, OFF["q_at"] + hh * 128, ncols), NKC, ncols)
                m = hh % 4
                dst = qT[:, hh, :] if hh < 8 else kT[:, hh - 8, :]
                dbuf = qT if hh < 8 else kT
                scale = float(128 ** -0.5) if hh < 8 else 1.0
                for (o, n) in seq_tiles(T):
                    pb = K.PS[cnt % 2]
                    for kc in range(NKC):
                        mm(K, pb, pb[:, :n], wb, w[:, kc, m * 128:(m + 1) * 128], h, h[:, kc, o:o + n],
                           start=(kc == 0), stop=(kc == NKC - 1))
                    if not rope:
                        q_ = qb[cnt % 2]
                        K.S.op("act", lambda e, d=q_[:, :n], p=pb[:, :n], s=scale: e.activation(out=d, in_=p, func=AF.Copy, scale=s),
                               R=[pb], W=[q_])
                        cp(K, "dve", dst[:, o:o + n], q_[:, :n], R=[q_], W=[dbuf])
                    else:
                        q_ = qb[cnt % 2]
                        K.S.op("act", lambda e, d=q_[:, :n], p=pb[:, :n], s=scale: e.activation(out=d, in_=p, func=AF.Copy, scale=s),
                               R=[pb], W=[q_])
                        pr = K.PS[2 + cnt % 2]
                        mm(K, pr, pr[:, :n], K.RT, K.RT[:], q_, q_[:, :n])
                        a1, a2 = t1[cnt % 2], t2[cnt % 2]
                        tt(K, "dve", a1[:, :n], q_[:, :n], cs[:, o:o + n], ALU.mult, R=[q_, cs], W=[a1])
                        tt(K, "dve", a2[:, :n], pr[:, :n], sn[:, o:o + n], ALU.mult, R=[pr, sn], W=[a2])
                        tt(K, "dve", dst[:, o:o + n], a1[:, :n], a2[:, :n], ALU.add, R=[a1, a2], W=[dbuf])
                    cnt += 1
            wb, w = ring.load(K.w_in, win_cols(K, l, OFF["k_at"], 512), NKC, 512)
            for b in range(nb):
                pb = K.PS[4 + b % 2]
                for kc in range(NKC):
                    mm(K, pb, pb[:, :], h, h[:, kc, b * 128:(b + 1) * 128], wb, w[:, kc, :],
                       start=(kc == 0), stop=(kc == NKC - 1))
                if kind == "P" and K.cfg.get("kvout", 1):
                    ko = kvo[b % 2]
                    cp(K, "dve", ko[:], pb[:, :], R=[pb], W=[ko])
                    cp(K, "act", vt[:, b, :], ko[:, 256:512], R=[ko], W=[vt])
                else:
                    cp(K, "act", vt[:, b, :], pb[:, 256:512], R=[pb], W=[vt])
                if kind == "P" and K.cfg.get("kvout", 1):
                    S.dma(K.nk[idx, l, b * 128:(b + 1) * 128, :], ko[:, 0:256], R=[ko], W=[K.nk])
                    S.dma(K.nv[idx, l, b * 128:(b + 1) * 128, :], ko[:, 256:512], R=[ko], W=[K.nv])
            S.barrier()
        with ExitStack() as st:
            E = [sb(K, st, [128, 512], BF16, "aE") for _ in range(3)]
            rd = [sb(K, st, [128, 512], F32, "ard") for _ in range(2)]
            ob = [sb(K, st, [128, 512], BF16, "aob") for _ in range(2)]
            esr = sb(K, st, [1, 1024], BF16, "aes")
            esf = sb(K, st, [1, 1024], F32, "aesf")
            S.dma(esf[:], K.sink_rep[l, :, :], R=[K.sink_rep], W=[esf])
            act(K, esr[:], esf[:], AF.Exp, R=[esf], W=[esr])
            if kind == "S" and K.cfg.get("attn_dbg", 3) >= 2:
                kc_f = sb(K, st, [128, 2, 256], F32, "akcf")
                kcT = sb(K, st, [128, 2, 256], BF16, "akcT")
                vc = sb(K, st, [128, 2, 256], BF16, "avc")
                S.dma(kc_f[:], K.cache_k[l].rearrange("(b p) c -> p b c", p=128), R=[K.cache_k], W=[kc_f])
                S.dma(vc[:], K.cache_v[l].rearrange("(b p) c -> p b c", p=128), R=[K.cache_v], W=[vc], q="pool")
                for n in range(2):
                    pb = K.PS[6]
                    for b in range(2):
                        tr(K, pb, pb[:, b * 128:(b + 1) * 128], kc_f, kc_f[:, b, n * 128:(n + 1) * 128], K.identF, K.identF[:])
                    cp(K, "dve", kcT[:, n, :], pb[:, 0:256], R=[pb], W=[kcT])
            cnt = 0
            ecnt = 0
            dbg = K.cfg.get("attn_dbg", 3)
            for n in range(2 if dbg >= 3 else 0):
                for i in range(nb):
                    kbs = []
                    if kind == "S":
                        for b in range(2):
                            kbs.append((kcT, kcT[:, n, b * 128:(b + 1) * 128], vc, vc[:, b, n * 128:(n + 1) * 128], None))
                        for j, mk in ((i - 1, K.maskL), (i, None), (i + 1, K.maskR)):
                            if 0 <= j < nb:
                                kbs.append((kT, kT[:, n, j * 128:(j + 1) * 128], vt, vt[:, j, n * 128:(n + 1) * 128], mk))
                    else:
                        for b in range(nb):
                            kbs.append((kT, kT[:, n, b * 128:(b + 1) * 128], vt, vt[:, b, n * 128:(n + 1) * 128], None))
                    po = K.PS[2 + cnt % 2]
                    pd = K.PS[4 + cnt % 2]
                    qap = qT[:, 4 * n:4 * n + 4, i * 128:(i + 1) * 128]
                    def score(bi):
                        kb_, kap = kbs[bi][0], kbs[bi][1]
                        pS_ = K.PS[(ecnt + bi) % 2]
                        mm(K, pS_, pS_[:, :].rearrange("p (a b) -> p a b", a=4), kb_, kap, qT, qap)
                    score(0)
                    for bi, (kb_, kap, vb_, vap, mk) in enumerate(kbs):
                        if bi + 1 < len(kbs):
                            score(bi + 1)
                        pS = K.PS[(ecnt + bi) % 2]
                        e_ = E[(ecnt + bi) % 3]
                        act(K, e_[:], pS[:, :], AF.Exp, R=[pS], W=[e_])
                        if mk is not None:
                            tt(K, "dve", e_[:].rearrange("p (a b) -> p a b", a=4), e_[:].rearrange("p (a b) -> p a b", a=4),
                               mk[:], ALU.mult, R=[e_, mk], W=[e_])
                        mm(K, po, po[:, :], vb_, vap, e_, e_[:], start=(bi == 0), stop=(bi == len(kbs) - 1))
                        mm(K, pd, pd[:, :], K.onesB, K.onesB[:], e_, e_[:], start=(bi == 0), stop=False)
                    ecnt += len(kbs)
                    mm(K, pd, pd[:, :], K.onesB, K.onesB[0:1, :], esr, esr[0:1, n * 512:(n + 1) * 512], start=False, stop=True)
                    r_ = rd[cnt % 2]
                    o_ = ob[cnt % 2]
                    K.S.op("dve", lambda e, a=r_[:], b=pd[:, :]: e.reciprocal(out=a, in_=b), R=[pd], W=[r_])
                    tt(K, "dve", o_[:], po[:, :], r_[:], ALU.mult, R=[po, r_], W=[o_])
                    S.dma(K.YD[4 + 4 * n:8 + 4 * n, :, t0 + i * 128:t0 + (i + 1) * 128].rearrange("c p t -> p c t"),
                          o_[:].rearrange("p (a b) -> p a b", a=4), R=[o_], W=K.YDB[4 + 4 * n:8 + 4 * n])
                    cnt += 1
            S.barrier()


def zero_y(K, chunks, seq):
    S = K.S
    kind, t0, T, idx = seq
    with ExitStack() as st:
        z = sb(K, st, [128, 512], BF16, "zy")
        S.op("dve", lambda e: e.memset(z[:], 0.0), W=[z])
        for c in chunks:
            for (o, n) in seq_tiles(T):
                S.dma(K.YD[c, :, t0 + o:t0 + o + n], z[:, :n], R=[z], W=[K.YDB[c]])
        S.barrier()


def phase_mixers(K, l):
    mix = K.cfg.get("mix", ("pool", "attn", "dn", "gla"))
    for seq in seqs_of(K):
        if "pool" in mix:
            mixer_pool(K, l, seq)
        else:
            zero_y(K, range(0, 4), seq)
        if "attn" in mix and seq[0] in K.cfg.get("attn_kinds", "SP"):
            mixer_attn(K, l, seq)
        else:
            zero_y(K, range(4, 12), seq)
        if "dn" in mix:
            mixer_dn(K, l, seq)
        else:
            zero_y(K, range(12, 16), seq)
        if "gla" in mix:
            mixer_gla(K, l, seq)
        else:
            zero_y(K, range(16, 20), seq)


def host_mix_common(inp, cfg):
    f = np.float32
    L, T_S, T_P = cfg["L"], cfg["T_S"], cfg["T_P"]
    d = {}
    d["pool_w"] = np.ascontiguousarray(inp["pool_w"][:L])
    d["pscT_d"] = np.ascontiguousarray(inp["pool_scale"][:L].reshape(L, 4, 128).transpose(2, 0, 1)).astype(f)

    def rcnt(T):
        pos = np.arange(T)
        out = np.zeros((4, T), f)
        for g, w in enumerate((2, 4, 8, 16)):
            lo = np.clip(pos - w // 2, 0, T)
            hi = np.clip(pos + w // 2, 0, T)
            out[g] = 1.0 / (hi - lo)
        return out
    d["rcnt_S"] = rcnt(T_S)
    d["rcnt_P"] = rcnt(T_P)
    t = np.arange(T_S)
    row = (t // 64).astype(np.float64)
    col = (t % 64).astype(np.float64)
    inv = 10000.0 ** (-np.arange(32, dtype=np.float64) / 32)
    ang = np.zeros((128, T_S))
    for fi in range(128):
        pos = row if fi < 64 else col
        ang[fi] = pos * inv[fi % 32]
    d["cos_d"] = np.cos(ang).astype(f)
    d["sin_d"] = np.sin(ang).astype(f)
    R = np.zeros((128, 128), f)
    for fi in range(128):
        if fi % 64 < 32:
            R[fi, fi + 32] = -1.0
        else:
            R[fi, fi - 32] = 1.0
    d["RT_d"] = np.ascontiguousarray(R.T)
    j = np.arange(128)[:, None]
    r = np.arange(128)[None, :]
    d["maskL_d"] = np.ascontiguousarray(np.broadcast_to((j >= r).astype(f)[:, None, :], (128, 4, 128)))
    d["maskR_d"] = np.ascontiguousarray(np.broadcast_to((j <= r).astype(f)[:, None, :], (128, 4, 128)))
    d["sink_rep"] = np.ascontiguousarray(np.repeat(inp["attn_sink"][:L], 128, axis=1).reshape(L, 1, 1024)).astype(f)
    same = (j // 64) == (r // 64)
    d["triF_d"] = np.stack([((j <= r) & same), ((j >= r) & same)]).astype(f)
    d["blk1_d"] = same.astype(f)
    d["negoff_d"] = (np.eye(128) - 1.0).astype(f)
    es = np.zeros((3, 3, 128), f)
    for k in range(3):
        es[k, k, :] = 1.0
    d["esel_d"] = es
    w2p = np.zeros((L, 32, 512), f)
    w2p[:, 0:16, 0:256] = inp["gla_w2"][:L, 0]
    w2p[:, 16:32, 256:512] = inp["gla_w2"][:L, 1]
    d["w2pad_d"] = w2p
    d["b2row_d"] = np.ascontiguousarray(inp["gla_b2"][:L].reshape(L, 1, 512)).astype(f)
    d["gnT_d"] = np.ascontiguousarray(inp["gla_norm"][:L].T).astype(f)
    d["dnT_d"] = np.ascontiguousarray(inp["dn_norm"][:L].T).astype(f)
    d["convT_d"] = np.ascontiguousarray(inp["dn_conv"][:L].reshape(L, 4, 12, 128).transpose(3, 0, 2, 1)).astype(f)
    d["dtb_d"] = np.ascontiguousarray(np.broadcast_to(inp["dn_dt_bias"][:L].reshape(1, L, 8), (128, L, 8))).astype(f)
    d["alog_d"] = np.ascontiguousarray(np.broadcast_to(inp["dn_a_log"][:L].reshape(1, L, 8), (128, L, 8))).astype(f)
    return d


def host_mix_core(inp, b_s, p_list, cfg):
    L = cfg["L"]
    d = {}
    d["cache_k"] = np.ascontiguousarray(inp["cache_k"][b_s, :L].reshape(L, 256, 256))
    d["cache_v"] = np.ascontiguousarray(inp["cache_v"][b_s, :L].reshape(L, 256, 256))
    d["state_delta"] = np.ascontiguousarray(inp["state_delta"][b_s, :L])
    d["state_gla"] = np.ascontiguousarray(inp["state_gla"][b_s, :L])
    return d


def mix_inputs(K, ein):
    L = K.L
    ein("triF_d", [2, 128, 128])
    ein("blk1_d", [128, 128])
    ein("negoff_d", [128, 128])
    ein("esel_d", [3, 3, 128])
    ein("w2pad_d", [L, 32, 512])
    ein("b2row_d", [L, 1, 512])
    ein("gnT_d", [128, L])
    ein("dnT_d", [128, L])
    ein("convT_d", [128, L, 12, 4])
    ein("dtb_d", [128, L, 8])
    ein("alog_d", [128, L, 8])
    K.DS = Buf(K.nc.dram_tensor("DS", [16, 128, K.NT], BF16).ap(), "DS")


def mix_consts(K, top):
    S = K.S
    L = K.L
    K.triF = sb(K, top, [128, 2, 128], F32, "triF")
    K.blk1 = sb(K, top, [128, 128], F32, "blk1")
    K.negoff = sb(K, top, [128, 128], F32, "negoff")
    K.esel = sb(K, top, [3, 3, 128], F32, "esel")
    K.identB = sb(K, top, [128, 128], BF16, "identB")
    K.gnT = sb(K, top, [128, L], F32, "gnT")
    K.dnT = sb(K, top, [128, L], F32, "dnT")
    K.convT = sb(K, top, [128, L, 12, 4], F32, "convT")
    K.dtb = sb(K, top, [128, L, 8], F32, "dtb")
    K.nea = sb(K, top, [128, L, 8], F32, "nea")
    K.onesF = sb(K, top, [128, 128], F32, "onesF")
    S.dma(K.triF[:], K.triF_d.t.rearrange("d j i -> j d i"), R=[K.triF_d], W=[K.triF])
    S.dma(K.blk1[:], K.blk1_d[:, :], R=[K.blk1_d], W=[K.blk1])
    S.dma(K.negoff[:], K.negoff_d[:, :], R=[K.negoff_d], W=[K.negoff])
    S.dma(K.esel[:], K.esel_d[:, :, :], R=[K.esel_d], W=[K.esel])
    S.dma(K.gnT[:], K.gnT_d[:, :], R=[K.gnT_d], W=[K.gnT])
    S.dma(K.dnT[:], K.dnT_d[:, :], R=[K.dnT_d], W=[K.dnT])
    S.dma(K.convT[:], K.convT_d[:, :, :, :], R=[K.convT_d], W=[K.convT])
    S.dma(K.dtb[:], K.dtb_d[:, :, :], R=[K.dtb_d], W=[K.dtb])
    S.dma(K.nea[:], K.alog_d[:, :, :], R=[K.alog_d], W=[K.nea])
    S.dma(K.identB[:], K.identF_d[:, :], R=[K.identF_d], W=[K.identB], q="pool")
    S.op("dve", lambda e: e.memset(K.onesF[:], 1.0), W=[K.onesF])
    act(K, K.nea[:], K.nea[:], AF.Exp, R=[K.nea], W=[K.nea])
    ts(K, "dve", K.nea[:], K.nea[:], -1.0, None, ALU.mult, None, R=[K.nea], W=[K.nea])


def final_gate_norm(K, l, seq, oacc, gbuf, gsil, nw, ychunk, st):
    S = K.S
    kind, t0, T, idx = seq
    sqs = [sb(K, st, [128, 512], BF16, "fsq") for _ in range(2)]
    rrs = [sb(K, st, [128, 512], F32, "frr") for _ in range(2)]
    tms = [sb(K, st, [128, 512], F32, "ftm") for _ in range(2)]
    yo = [sb(K, st, [128, 512], BF16, "fyo") for _ in range(2)]
    for ti, (o, n) in enumerate(seq_tiles(T)):
        sq, rr, tm = sqs[ti % 2], rrs[ti % 2], tms[ti % 2]
        act(K, sq[:, :n], oacc[:, o:o + n], AF.Square, R=[oacc], W=[sq])
        pb = K.PS[6]
        mm(K, pb, pb[:, :n], K.onesB, K.onesB[:], sq, sq[:, :n])
        rsqrt(K, rr, rr[:, :n], pb, pb[:, :n], 1.0 / 128)
        tt(K, "dve", tm[:, :n], oacc[:, o:o + n], rr[:, :n], ALU.mult, R=[oacc, rr], W=[tm])
        y = yo[ti % 2]
        stt(K, "dve", y[:, :n], tm[:, :n], nw, gsil[:, o:o + n], ALU.mult, ALU.mult, R=[tm, gbuf], W=[y])
        S.dma(K.YD[ychunk, :, t0 + o:t0 + o + n], y[:, :n], R=[y], W=[K.YDB[ychunk]])


def proj_fm(K, h, wb, w_ap, T, pbanks, consume):
    for ti, (o, n) in enumerate(seq_tiles(T)):
        pb = pbanks[ti % len(pbanks)]
        for kc in range(NKC):
            mm(K, pb, pb[0:w_ap.shape[-1], :n], wb, w_ap[:, kc, :], h, h[:, kc, o:o + n],
               start=(kc == 0), stop=(kc == NKC - 1))
        consume(o, n, pb)


def mixer_gla(K, l, seq):
    S = K.S
    kind, t0, T, idx = seq
    ring = Ring(K, [(K.w_in, win_cols(K, l, OFF["q_gl"], 512), NKC, 512),
                    (K.w_in, win_cols(K, l, OFF["r_gl"], 512), NKC, 512),
                    (K.w_in, win_cols(K, l, OFF["lr"], 32), NKC, 32),
                    (K.w_in, win_cols(K, l, OFF["v_gl"], 512), NKC, 512)])
    nb = T // 128
    with ExitStack() as outer:
        qk = sb(K, outer, [128, 4, T], BF16, "gqk")
        rs = sb(K, outer, [128, 4, T], BF16, "grs")
        vt = sb(K, outer, [128, nb, 512], BF16, "gvt")
        lrT = sb(K, outer, [33, T], BF16, "glr")
        w2p = sb(K, outer, [33, 512], BF16, "gw2")
        S.dma(w2p[0:32, :], K.w2pad_d[l], R=[K.w2pad_d], W=[w2p], q="pool")
        S.dma(w2p[32:33, :], K.b2row_d[l], R=[K.b2row_d], W=[w2p], q="pool")
        S.op("dve", lambda e: e.memset(lrT[32:33, :], 1.0), W=[lrT])
        with ExitStack() as st:
            h = load_h(K, st, t0, T)
            wb, w = ring.load(K.w_in, win_cols(K, l, OFF["q_gl"], 512), NKC, 512)
            for c in range(4):
                proj_fm(K, h, wb, w[:, :, c * 128:(c + 1) * 128], T, [K.PS[0], K.PS[1]],
                        lambda o, n, pb, c=c: cp(K, "act", qk[:, c, o:o + n], pb[:, :n], R=[pb], W=[qk]))
            wb, w = ring.load(K.w_in, win_cols(K, l, OFF["r_gl"], 512), NKC, 512)
            for c in range(4):
                proj_fm(K, h, wb, w[:, :, c * 128:(c + 1) * 128], T, [K.PS[0], K.PS[1]],
                        lambda o, n, pb, c=c: act(K, rs[:, c, o:o + n], pb[:, :n], AF.Silu, R=[pb], W=[rs]))
            wb, w = ring.load(K.w_in, win_cols(K, l, OFF["lr"], 32), NKC, 32)
            proj_fm(K, h, wb, w[:, :, 0:32], T, [K.PS[0], K.PS[1]],
                    lambda o, n, pb: cp(K, "act", lrT[0:32, o:o + n], pb[0:32, :n], R=[pb], W=[lrT]))
            wb, w = ring.load(K.w_in, win_cols(K, l, OFF["v_gl"], 512), NKC, 512)
            for b in range(nb):
                pb = K.PS[2 + b % 2]
                for kc in range(NKC):
                    mm(K, pb, pb[:, :], h, h[:, kc, b * 128:(b + 1) * 128], wb, w[:, kc, :],
                       start=(kc == 0), stop=(kc == NKC - 1))
                cp(K, "dve", vt[:, b, :], pb[:, :], R=[pb], W=[vt])
            S.barrier()
        def head_ctx(hd, st, hi):
            hp = (hd % 2) * 64
            qv = qk[hp:hp + 64, hd // 2, :]
            kv = qk[hp:hp + 64, 2 + hd // 2, :]
            oacc = sb(K, st, [128, T], F32, "goacc")
            S.op("pool", lambda e: e.memset(oacc[:], 0.0), W=[oacc])
            Sf = [sb(K, st, [128, 128], F32, "gS") for _ in range(2)]
            P = slice(hp, hp + 64)
            Sb = [sb(K, st, [128, 128], BF16, "gSb") for _ in range(2)]
            for d in range(2):
                if kind == "S":
                    S.dma(Sf[d][P, :], K.state_gla[l, d, hd], R=[K.state_gla], W=[Sf[d]])
                else:
                    S.op("pool", lambda e, d=d: e.memset(Sf[d][:], 0.0), W=[Sf[d]])
                cp(K, "act", Sb[d][P, :], Sf[d][P, :], R=[Sf[d]], W=[Sb[d]])
            U = [dict(e1=sb(K, st, [128, 64], F32, "ge1"), sp=sb(K, st, [128, 64], F32, "gsp"),
                      gcp=sb(K, st, [128, 128], F32, "ggcp"), egc=sb(K, st, [128, 128], F32, "gegc"),
                      engc=sb(K, st, [128, 128], F32, "gengc"), ekd=sb(K, st, [128, 128], F32, "gekd"),
                      nb_=sb(K, st, [128, 2], F32, "gnb"), qg=sb(K, st, [128, 128], BF16, "gqg"),
                      kg=sb(K, st, [128, 128], BF16, "gkg"), kdT=sb(K, st, [128, 128], BF16, "gkdT"),
                      kd=sb(K, st, [128, 64], BF16, "gkd"), aT=sb(K, st, [128, 128], BF16, "gaT"))
                 for _ in range(2)]
            def gla_unit(d, b):
                u = U[d]
                pA, pC = K.PS[3 * hi], K.PS[3 * hi + 1 + d]
                gq = slice(d * 64, (d + 1) * 64)
                gr = slice(128 + d * 128, 256 + d * 128)
                tk = slice(b * 128, (b + 1) * 128)
                cw = slice(d * 256 + hd * 64, d * 256 + hd * 64 + 64)
                mm(K, pA, pA[:, gq], lrT, lrT[:, tk], w2p, w2p[:, cw])
                yield
                act(K, u["e1"][:], pA[:, gq], AF.Exp, R=[pA], W=[u["e1"]], scale=-1.0)
                yield
                act(K, u["sp"][:], u["e1"][:], AF.Ln, R=[u["e1"]], W=[u["sp"]], bias=1.0)
                yield
                mm(K, pA, pA[P, gr], u["sp"], u["sp"][:], K.triF, K.triF[:, d, :])
                yield
                cp(K, "act", u["gcp"][P, :], pA[P, gr], R=[pA], W=[u["gcp"]])
                yield
                act(K, u["egc"][P, :], u["gcp"][P, :], AF.Exp, R=[u["gcp"]], W=[u["egc"]], scale=-1.0 / 16)
                yield
                act(K, u["engc"][P, :], u["gcp"][P, :], AF.Exp, R=[u["gcp"]], W=[u["engc"]], scale=1.0 / 16)
                yield
                c0 = 63 if d == 0 else 0
                ts(K, "dve", u["nb_"][P, :], u["gcp"][P, c0:c0 + 65:64], -1.0 / 16, None, ALU.mult, None,
                   R=[u["gcp"]], W=[u["nb_"]])
                yield
                for c in range(2):
                    act(K, u["ekd"][P, c * 64:(c + 1) * 64], u["gcp"][P, c * 64:(c + 1) * 64], AF.Exp,
                        R=[u["gcp"], u["nb_"]], W=[u["ekd"]], scale=1.0 / 16, bias=u["nb_"][P, c:c + 1])
                    yield
                stt(K, "dve", u["qg"][P, :], qv[:, tk], 0.125, u["egc"][P, :], ALU.mult, ALU.mult, R=[qk, u["egc"]], W=[u["qg"]])
                yield
                tt(K, "dve", u["kg"][P, :], kv[:, tk], u["engc"][P, :], ALU.mult, R=[qk, u["engc"]], W=[u["kg"]])
                yield
                tt(K, "dve", u["kdT"][P, :], kv[:, tk], u["ekd"][P, :], ALU.mult, R=[qk, u["ekd"]], W=[u["kdT"]])
                yield
                pT = K.PB
                tr(K, pT, pT[:, (2 * hi + d) * 64:(2 * hi + d + 1) * 64], u["kdT"], u["kdT"][P, :], K.identB, K.identB[P, hp:hp + 64])
                yield
                cp(K, "act", u["kd"][:], pT[:, (2 * hi + d) * 64:(2 * hi + d + 1) * 64], R=[pT], W=[u["kd"]])
                yield
                mm(K, pC, pC[:, 256:384], u["kg"], u["kg"][P, :], u["qg"], u["qg"][P, :])
                yield
                tt(K, "dve", u["aT"][:], pC[:, 256:384], K.triF[:, d, :], ALU.mult, R=[pC, K.triF], W=[u["aT"]])
                yield
                for c in ((0, 1) if d == 0 else (1, 0)):
                    cs_ = slice(c * 64, (c + 1) * 64)
                    vb = vt[:, b, hd * 128:(hd + 1) * 128]
                    mm(K, pC, pC[:, 0:64], Sb[d], Sb[d][P, :], u["qg"], u["qg"][P, cs_], start=True, stop=False)
                    yield
                    mm(K, pC, pC[:, 0:64], vt, vb, u["aT"], u["aT"][:, cs_], start=False, stop=True)
                    yield
                    ot = oacc[:, b * 128 + c * 64:b * 128 + (c + 1) * 64]
                    tt(K, "dve", ot, pC[:, 0:64], ot, ALU.add, R=[pC, oacc], W=[oacc])
                    yield
                    mm(K, pC, pC[P, 128:256], u["kd"], u["kd"][cs_, :], vt, vt[cs_, b, hd * 128:(hd + 1) * 128])
                    yield
                    col = c * 64 + (63 if d == 0 else 0)
                    stt(K, "dve", Sf[d][P, :], Sf[d][P, :], u["egc"][P, col:col + 1], pC[P, 128:256], ALU.mult, ALU.add,
                        R=[Sf[d], u["egc"], pC], W=[Sf[d]])
                    yield
                    cp(K, "act", Sb[d][P, :], Sf[d][P, :], R=[Sf[d]], W=[Sb[d]])
                    yield
            def finish():
                if kind == "P":
                    for d in range(2):
                        S.dma(K.nsg[idx, l, d, hd], Sf[d][P, :], R=[Sf[d]], W=[K.nsg])
                final_gate_norm(K, l, seq, oacc, rs, rs[:, hd, :], K.gnT[:, l:l + 1], 16 + hd, st)
            return gla_unit, finish

        for pair in range(2):
            with ExitStack() as st:
                ctxs = [head_ctx(2 * pair + hi, st, hi) for hi in range(2)]
                for s_ in range(nb):
                    gens = []
                    for gu, _ in ctxs:
                        gens += [gu(0, s_), gu(1, nb - 1 - s_)]
                    while gens:
                        for g_ in list(gens):
                            try:
                                next(g_)
                            except StopIteration:
                                gens.remove(g_)
                for _, fn in ctxs:
                    fn()
                S.barrier()
def mixer_dn(K, l, seq):
    S = K.S
    kind, t0, T, idx = seq
    ring = Ring(K, [(K.w_in, win_cols(K, l, OFF["qkv"] + c * 128, 512), NKC, 512) for c in (0, 4, 8, 12)]
                + [(K.w_in, win_cols(K, l, OFF["a"], 16), NKC, 16)])
    nb = T // 128
    with ExitStack() as outer:
        gg = sb(K, outer, [128, nb, 8], F32, "dg")
        be = sb(K, outer, [128, nb, 8], F32, "dbeta")
        with ExitStack() as st:
            h = load_h(K, st, t0, T)
            xp = [sb(K, st, [128, T + 3], F32, "dxp") for _ in range(2)]
            cv = [sb(K, st, [128, T], F32, "dcv") for _ in range(2)]
            sqs = [sb(K, st, [128, 512], BF16, "dsq") for _ in range(2)]
            rns = [sb(K, st, [128, 512], F32, "drn") for _ in range(2)]
            ob = [sb(K, st, [128, T], BF16, "dob") for _ in range(2)]
            e1 = sb(K, st, [128, 8], F32, "de1")
            for x in xp:
                S.op("pool", lambda e, x=x: e.memset(x[:, 0:2], 0.0), W=[x])
                S.op("pool", lambda e, x=x: e.memset(x[:, T + 2:T + 3], 0.0), W=[x])
            for c in range(16):
                if c % 4 == 0:
                    wb, w = ring.load(K.w_in, win_cols(K, l, OFF["qkv"] + c * 128, 512), NKC, 512)
                wa = w[:, :, (c % 4) * 128:(c % 4 + 1) * 128]
                o_ = ob[c % 2]
                if c >= 12:
                    proj_fm(K, h, wb, wa, T, [K.PS[0], K.PS[1]],
                            lambda o, n, pb, o_=o_: act(K, o_[:, o:o + n], pb[:, :n], AF.Silu, R=[pb], W=[o_]))
                else:
                    x = xp[c % 2]
                    y = cv[c % 2]
                    proj_fm(K, h, wb, wa, T, [K.PS[0], K.PS[1]],
                            lambda o, n, pb, x=x: cp(K, "act", x[:, 2 + o:2 + o + n], pb[:, :n], R=[pb], W=[x]))
                    act(K, y[:], x[:, 0:T], AF.Copy, R=[x, K.convT], W=[y], scale=K.convT[:, l, c, 0:1])
                    for j in range(1, 4):
                        stt(K, "dve", y[:], x[:, j:j + T], K.convT[:, l, c, j:j + 1], y[:], ALU.mult, ALU.add,
                            R=[x, y, K.convT], W=[y])
                    if c >= 8:
                        act(K, o_[:], y[:], AF.Silu, R=[y], W=[o_])
                    else:
                        act(K, y[:], y[:], AF.Silu, R=[y], W=[y])
                        for ti_, (o, n) in enumerate(seq_tiles(T)):
                            sq, rn = sqs[ti_ % 2], rns[ti_ % 2]
                            act(K, sq[:, :n], y[:, o:o + n], AF.Square, R=[y], W=[sq])
                            pb = K.PS[2 + ti_ % 2]
                            mm(K, pb, pb[:, :n], K.onesB, K.onesB[:], sq, sq[:, :n])
                            rsqrt(K, rn, rn[:, :n], pb, pb[:, :n], 1.0)
                            if c < 4:
                                stt(K, "dve", o_[:, o:o + n], y[:, o:o + n], float(128 ** -0.5), rn[:, :n], ALU.mult, ALU.mult,
                                    R=[y, rn], W=[o_])
                            else:
                                tt(K, "dve", o_[:, o:o + n], y[:, o:o + n], rn[:, :n], ALU.mult, R=[y, rn], W=[o_])
                S.dma(K.DS[c, :, t0:t0 + T], o_[:], R=[o_], W=[K.DS])
            wb, w = ring.load(K.w_in, win_cols(K, l, OFF["a"], 16), NKC, 16)
            for b in range(nb):
                pb = K.PS[3]
                for kc in range(NKC):
                    mm(K, pb, pb[:, 0:16], h, h[:, kc, b * 128:(b + 1) * 128], wb, w[:, kc, :],
                       start=(kc == 0), stop=(kc == NKC - 1))
                tt(K, "dve", e1[:], pb[:, 0:8], K.dtb[:, l, :], ALU.add, R=[pb, K.dtb], W=[e1])
                act(K, be[:, b, :], pb[:, 8:16], AF.Sigmoid, R=[pb], W=[be])
                act(K, e1[:], e1[:], AF.Exp, R=[e1], W=[e1])
                act(K, e1[:], e1[:], AF.Ln, R=[e1], W=[e1], bias=1.0)
                tt(K, "dve", gg[:, b, :], e1[:], K.nea[:, l, :], ALU.mult, R=[e1, K.nea], W=[gg])
            S.barrier()
        for hd in range(4):
            with ExitStack() as st:
                qh = sb(K, st, [128, T], BF16, "dqh")
                kh = sb(K, st, [128, T], BF16, "dkh")
                vv = sb(K, st, [128, T], BF16, "dvv")
                zs = sb(K, st, [128, T], BF16, "dzs")
                for buf, c in ((qh, hd), (kh, 4 + hd), (vv, 8 + hd), (zs, 12 + hd)):
                    S.dma(buf[:], K.DS[c, :, t0:t0 + T], R=[K.DS], W=[buf])
                oacc = sb(K, st, [128, T], F32, "doacc")
                S.op("pool", lambda e: e.memset(oacc[:], 0.0), W=[oacc])
                Sf = [sb(K, st, [128, 128], F32, "dS") for _ in range(2)]
                for d in range(2):
                    if kind == "S":
                        S.dma(Sf[d][:], K.state_delta[l, d, hd], R=[K.state_delta], W=[Sf[d]])
                    else:
                        S.op("pool", lambda e, d=d: e.memset(Sf[d][:], 0.0), W=[Sf[d]])

                def mk():
                    f = lambda sh, dt, nm: sb(K, st, sh, dt, nm)
                    return dict(g3=f([128, 3], F32, "g3"), g3T=f([3, 128], F32, "g3T"), sc=f([128, 4], F32, "dsc"),
                                dl=f([128, 2], F32, "ddl"), dmt=f([128, 128], F32, "dmt"), dmm=f([128, 128], F32, "dmm"),
                                tn=f([128, 128], F32, "dtn"), m1=f([128, 128], F32, "dm1"), aqk=f([128, 128], F32, "daqk"),
                                X=[f([128, 128], F32, "dX") for _ in range(2)], XT=[f([128, 128], F32, "dXT") for _ in range(2)],
                                R=[f([128, 128], F32, "dR") for _ in range(2)], bv=f([128, 128], F32, "dbv"),
                                kbg=f([128, 128], F32, "dkbg"), kd=f([128, 128], F32, "dkd"), u=f([128, 128], F32, "du"),
                                wT=f([128, 128], F32, "dwT"), egr=f([128, 128], F32, "degr"), qe=f([128, 128], F32, "dqe"),
                                vn=f([128, 128], F32, "dvn"))
                U = [mk(), mk()]
                for d in range(2):
                    S.op("pool", lambda e, d=d: e.memset(U[d]["vn"][:], 0.0), W=[U[d]["vn"]])
                def dn_unit(d, b):
                    u = U[d]
                    pA, pB_, pC = K.PS[3 * d], K.PS[3 * d + 1], K.PS[3 * d + 2]
                    qB, qC = [pB_] * 4, [pC] * 4
                    pT = K.PB
                    tk = slice(b * 128, (b + 1) * 128)
                    col = d * 4 + hd
                    gcol = gg[:, b, col:col + 1]
                    bcol = be[:, b, col:col + 1]
                    mm(K, pA, pA[:, 0:1], K.triF, K.triF[:, d, :], gg, gcol)
                    yield
                    mm(K, pA, pA[:, 1:2], K.blk1, K.blk1[:], gg, gcol)
                    yield
                    cp(K, "act", u["g3"][:, 0:1], pA[:, 0:1], R=[pA], W=[u["g3"]])
                    yield
                    cp(K, "act", u["g3"][:, 2:3], pA[:, 1:2], R=[pA], W=[u["g3"]])
                    yield
                    cp(K, "act", u["g3"][:, 1:2], bcol, R=[be], W=[u["g3"]])
                    yield
                    act(K, u["sc"][:, 0:1], u["g3"][:, 0:1], AF.Exp, R=[u["g3"]], W=[u["sc"]])
                    yield
                    tt(K, "dve", u["sc"][:, 1:2], u["sc"][:, 0:1], u["g3"][:, 1:2], ALU.mult, R=[u["sc"], u["g3"]], W=[u["sc"]])
                    yield
                    act(K, u["sc"][:, 2:3], u["g3"][:, 0:1], AF.Exp, R=[u["g3"]], W=[u["sc"]], scale=-1.0, bias=u["g3"][:, 2:3])
                    yield
                    tr(K, pA, pA[0:3, 2:130], u["g3"], u["g3"][:], K.identF, K.identF[:])
                    yield
                    cp(K, "act", u["g3T"][:], pA[0:3, 2:130], R=[pA], W=[u["g3T"]])
                    yield
                    for r in range(3):
                        mm(K, pA, pA[:, 128 * (r + 1):128 * (r + 2)], K.esel, K.esel[:, r, :], u["g3T"], u["g3T"][:])
                        yield
                    Grow = pA[:, 128:256]
                    Brow = pA[:, 256:384]
                    Trow = pA[:, 384:512]
                    act(K, u["dl"][:], Trow[:, 0:65:64], AF.Exp, R=[pA], W=[u["dl"]])
                    yield
                    act(K, u["egr"][:], Grow, AF.Exp, R=[pA], W=[u["egr"]])
                    yield
                    ts(K, "dve", u["dmt"][:], Grow, u["g3"][:, 0:1], 0.0, ALU.subtract, ALU.min, R=[pA, u["g3"]], W=[u["dmt"]])
                    yield
                    act(K, u["dmt"][:], u["dmt"][:], AF.Exp, R=[u["dmt"]], W=[u["dmt"]])
                    yield
                    tt(K, "dve", u["dmm"][:], u["dmt"][:], K.triF[:, d, :], ALU.mult, R=[u["dmt"], K.triF], W=[u["dmm"]])
                    yield
                    tt(K, "dve", u["tn"][:], u["dmm"][:], K.negoff[:], ALU.mult, R=[u["dmm"], K.negoff], W=[u["tn"]])
                    yield
                    mm(K, qB[0], pB_[:, 0:128], kh, kh[:, tk], kh, kh[:, tk])
                    yield
                    mm(K, qB[1], pB_[:, 128:256], kh, kh[:, tk], qh, qh[:, tk])
                    yield
                    tt(K, "dve", u["aqk"][:], pB_[:, 128:256], u["dmm"][:], ALU.mult, R=[qB[1], u["dmm"]], W=[u["aqk"]])
                    yield
                    tt(K, "dve", u["m1"][:], pB_[:, 0:128], u["tn"][:], ALU.mult, R=[qB[0], u["tn"]], W=[u["m1"]])
                    yield
                    X, XT, R_ = u["X"], u["XT"], u["R"]
                    tt(K, "dve", X[0][:], Brow, u["m1"][:], ALU.mult, R=[pA, u["m1"]], W=[X[0]])
                    yield
                    tt(K, "dve", R_[0][:], X[0][:], K.identF[:], ALU.add, R=[X[0], K.identF], W=[R_[0]])
                    yield
                    tr(K, qC[3], pC[:, 384:512], X[0], X[0][:], K.identF, K.identF[:])
                    yield
                    cp(K, "dve", XT[0][:], pC[:, 384:512], R=[qC[3]], W=[XT[0]])
                    yield
                    ci, ri = 0, 0
                    for k in range(1, 6):
                        ni = 1 - ci
                        if k < 5:
                            mm(K, qB[2], pB_[:, 256:384], XT[ci], XT[ci][:], X[ci], X[ci][:])
                            yield
                            cp(K, "dve", X[ni][:], pB_[:, 256:384], R=[qB[2]], W=[X[ni]])
                            yield
                        mm(K, qB[3], pB_[:, 384:512], X[ci], X[ci][:], XT[ci], XT[ci][:])
                        yield
                        cp(K, "dve", XT[ni][:], pB_[:, 384:512], R=[qB[3]], W=[XT[ni]])
                        yield
                        mm(K, qB[0], pB_[:, 0:128], XT[ni], XT[ni][:], R_[ri], R_[ri][:])
                        yield
                        tt(K, "dve", R_[1 - ri][:], pB_[:, 0:128], R_[ri][:], ALU.add, R=[qB[0], R_[ri]], W=[R_[1 - ri]])
                        yield
                        ci, ri = ni, 1 - ri
                    Rf = R_[ri]
                    tr(K, pT, pT[:, d * 256:d * 256 + 128], kh, kh[:, tk], K.identB, K.identB[:])
                    yield
                    tr(K, pT, pT[:, d * 256 + 128:d * 256 + 256], vv, vv[:, tk], K.identB, K.identB[:])
                    yield
                    act(K, u["kbg"][:], pT[:, d * 256:d * 256 + 128], AF.Copy, R=[pT, u["sc"]], W=[u["kbg"]], scale=u["sc"][:, 1:2])
                    yield
                    act(K, u["kd"][:], pT[:, d * 256:d * 256 + 128], AF.Copy, R=[pT, u["sc"]], W=[u["kd"]], scale=u["sc"][:, 2:3])
                    yield
                    act(K, u["bv"][:], pT[:, d * 256 + 128:d * 256 + 256], AF.Copy, R=[pT, u["g3"]], W=[u["bv"]], scale=u["g3"][:, 1:2])
                    yield
                    mm(K, qB[1], pB_[:, 128:256], Rf, Rf[:], u["bv"], u["bv"][:])
                    yield
                    cp(K, "dve", u["u"][:], pB_[:, 128:256], R=[qB[1]], W=[u["u"]])
                    yield
                    mm(K, qB[2], pB_[:, 256:384], u["kbg"], u["kbg"][:], Rf, Rf[:])
                    yield
                    cp(K, "dve", u["wT"][:], pB_[:, 256:384], R=[qB[2]], W=[u["wT"]])
                    yield
                    tt(K, "dve", u["qe"][:], qh[:, tk], u["egr"][:], ALU.mult, R=[qh, u["egr"]], W=[u["qe"]])
                    yield
                    for c in ((0, 1) if d == 0 else (1, 0)):
                        cs_ = slice(c * 64, (c + 1) * 64)
                        mm(K, qC[0], pC[:, 0:128], u["wT"], u["wT"][:], Sf[d], Sf[d][:])
                        yield
                        tt(K, "dve", u["vn"][cs_, :], u["u"][cs_, :], pC[cs_, 0:128], ALU.subtract, R=[u["u"], qC[0]], W=[u["vn"]])
                        yield
                        mm(K, qC[1], pC[:, 128:192], Sf[d], Sf[d][:], u["qe"], u["qe"][:, cs_], start=True, stop=False)
                        yield
                        mm(K, qC[1], pC[:, 128:192], u["vn"], u["vn"][cs_, :], u["aqk"], u["aqk"][cs_, cs_], start=False, stop=True)
                        yield
                        ot = oacc[:, b * 128 + c * 64:b * 128 + (c + 1) * 64]
                        tt(K, "dve", ot, pC[:, 128:192], ot, ALU.add, R=[qC[1], oacc], W=[oacc])
                        yield
                        mm(K, qC[2], pC[:, 256:384], u["kd"], u["kd"][cs_, :], u["vn"], u["vn"][cs_, :])
                        yield
                        stt(K, "dve", Sf[d][:], Sf[d][:], u["dl"][:, c:c + 1], pC[:, 256:384], ALU.mult, ALU.add,
                            R=[Sf[d], u["dl"], qC[2]], W=[Sf[d]])
                        yield
                for s_ in range(nb):
                    gens = [dn_unit(0, s_), dn_unit(1, nb - 1 - s_)]
                    while gens:
                        for g_ in list(gens):
                            try:
                                next(g_)
                            except StopIteration:
                                gens.remove(g_)
                if kind == "P":
                    for d in range(2):
                        S.dma(K.nsd[idx, l, d, hd], Sf[d][:], R=[Sf[d]], W=[K.nsd])
                final_gate_norm(K, l, seq, oacc, zs, zs[:, :], K.dnT[:, l:l + 1], 12 + hd, st)
                S.barrier()


CFG = dict(T_S=2048, NP=2, T_P=256, L=4)


def kernel(**inputs):
    inp = {k: np.asarray(v) for k, v in inputs.items()}
    cfg = dict(CFG)
    L = cfg["L"]
    nc, K = build_program(cfg)
    common = host_common(inp, L)
    common.update(host_mix_common(inp, cfg))
    in_maps = []
    for c in range(8):
        pl = [2 * c, 2 * c + 1]
        d = host_core(inp, common, c // 2, pl, L)
        d.update(host_mix_core(inp, c // 2, pl, cfg))
        in_maps.append(d)
    res = run_bass_kernel_spmd(nc, in_maps, core_ids=list(range(8)))
    r = res.results
    f = np.float32
    y_prompt = np.stack([r[p // 2]["y_p"].reshape(2, 256, D)[p % 2] for p in range(16)]).astype(f)
    y_sample = np.stack([r[2 * b]["y_s"] for b in range(4)]).astype(f)
    nk = np.concatenate([r[c]["nk"].reshape(2, L, 256, 2, 128) for c in range(8)]).astype(f)
    nv = np.concatenate([r[c]["nv"].reshape(2, L, 256, 2, 128) for c in range(8)]).astype(f)
    nsd = np.concatenate([r[c]["nsd"] for c in range(8)]).astype(f)
    nsg = np.concatenate([r[c]["nsg"] for c in range(8)]).astype(f)
    return (y_prompt, y_sample, nk, nv, nsd, nsg)
```
